# Optimizing a Trainium2 kernel written in Bass

```python
import math
import jax, jax.numpy as jnp
from jax import lax
import numpy as np

D_MODEL = 1024
BATCH = 2
SEQ = 8192
DEPTH = 1

CHUNK = 64
D_MIX = D_MODEL
GM_GROUPS = 8
GM_GROUP_DIM = 64
GM_WIDTH = GM_GROUPS * GM_GROUP_DIM
GM_BLOCK = 128
DN_HEADS = 4
DN_HEAD_DIM = 128
DN_WIDTH = DN_HEADS * DN_HEAD_DIM
DN_CONV = 4
D_IN = 2 * GM_WIDTH + 4 * DN_WIDTH + 2 * DN_HEADS
D_FF = 2816
EPS = 1e-6

kernel_name = "hybrid_gmlp_gated_deltanet_macaron"


def rmsnorm(x, g):
    xf = x.astype(jnp.float32)
    y = xf * lax.rsqrt(jnp.mean(xf * xf, axis=-1, keepdims=True) + EPS)
    return (y * g.astype(jnp.float32)).astype(x.dtype)


def layernorm(x, g, b):
    xf = x.astype(jnp.float32)
    mu = jnp.mean(xf, axis=-1, keepdims=True)
    var = jnp.mean(jnp.square(xf - mu), axis=-1, keepdims=True)
    y = (xf - mu) * lax.rsqrt(var + EPS)
    return (y * g.astype(jnp.float32) + b.astype(jnp.float32)).astype(x.dtype)


def l2norm(x):
    return x * lax.rsqrt(jnp.sum(x * x, axis=-1, keepdims=True) + EPS)


def swiglu(x, w_gate, w_up, w_down):
    return (jax.nn.silu(x @ w_gate) * (x @ w_up)) @ w_down


def chunk_causal_mask(n):
    c = jnp.arange(n) // CHUNK
    return c[None, :] <= c[:, None]


def spatial_gating(u, v, w_s, b_s, ln_g, ln_b):
    B, T, _ = v.shape
    v = layernorm(v, ln_g, ln_b)
    nb = T // GM_BLOCK
    vb = v.reshape(B, nb, GM_BLOCK, GM_GROUPS, GM_GROUP_DIM)
    w = jnp.where(chunk_causal_mask(GM_BLOCK)[None], w_s, 0.0).astype(v.dtype)
    mixed = jnp.einsum('gij,bnjgc->bnigc', w, vb) + b_s.T[:, :, None].astype(v.dtype)
    return u * mixed.reshape(B, T, GM_WIDTH)


def causal_dwconv(x, w):
    K, C = w.shape
    return lax.conv_general_dilated(
        x, w[:, None, :].astype(x.dtype), window_strides=(1,), padding=[(K - 1, 0)],
        dimension_numbers=('NWC', 'WIO', 'NWC'), feature_group_count=C)


def gated_delta_rule(q, k, v, beta, g):
    B, T, H, Dk = q.shape
    Dv = v.shape[-1]
    C = CHUNK
    N = T // C
    scale = Dk ** -0.5

    def to_chunks(t):
        return t.reshape(B, N, C, H, *t.shape[3:]).swapaxes(2, 3)

    q, k, v = to_chunks(q * scale), to_chunks(k), to_chunks(v)
    beta, g = to_chunks(beta), to_chunks(g)
    g = jnp.cumsum(g, axis=-1)
    tri = jnp.tril(jnp.ones((C, C), bool))
    strict = jnp.tril(jnp.ones((C, C), bool), -1)
    decay = jnp.exp(jnp.where(tri, g[..., :, None] - g[..., None, :], -jnp.inf))
    k_beta = k * beta[..., None]
    v_beta = v * beta[..., None]
    L = jnp.where(strict, jnp.einsum('bnhid,bnhjd->bnhij', k_beta, k) * decay, 0.0)
    eye = jnp.eye(C, dtype=L.dtype)
    rhs = jnp.concatenate([v_beta, k_beta * jnp.exp(g)[..., None]], axis=-1)
    sol = lax.linalg.triangular_solve(L + eye, rhs, left_side=True, lower=True,
                                      unit_diagonal=True)
    u_c, w_c = sol[..., :Dv], sol[..., Dv:]
    attn = jnp.where(tri, jnp.einsum('bnhid,bnhjd->bnhij', q, k) * decay, 0.0)
    g_last = g[..., -1]
    k_dec = k * jnp.exp(g_last[..., None] - g)[..., None]
    q_dec = q * jnp.exp(g)[..., None]
    xs = tuple(jnp.moveaxis(t, 1, 0) for t in (q_dec, k_dec, u_c, w_c, attn, g_last))

    def step(S, inp):
        qd, kd, uc, wc, a, gl = inp
        v_new = uc - jnp.einsum('bhcd,bhde->bhce', wc, S)
        o = jnp.einsum('bhcd,bhde->bhce', qd, S) + jnp.einsum('bhij,bhje->bhie', a, v_new)
        S = S * jnp.exp(gl)[..., None, None] + jnp.einsum('bhcd,bhce->bhde', kd, v_new)
        return S, o

    S0 = jnp.zeros((B, H, Dk, Dv), q.dtype)
    _, o = lax.scan(step, S0, xs)
    return jnp.moveaxis(o, 0, 1).swapaxes(2, 3).reshape(B, T, H, Dv)


def setup_inputs(seed: int = 0) -> dict:
    key = jax.random.key(seed)
    ks = jax.random.split(key, 24)
    f32 = jnp.float32
    L = DEPTH

    def nrm(k, shape, scale):
        return jax.random.normal(k, shape, f32) * scale

    def gain(k, shape):
        return 1.0 + 0.05 * jax.random.normal(k, shape, f32)

    dt = jnp.exp(jax.random.uniform(ks[13], (L, DN_HEADS), f32, math.log(1e-3), math.log(1e-1)))
    return {
        "x": jax.random.normal(ks[0], (BATCH, SEQ, D_MODEL), f32),
        "ffn1_norm": gain(ks[1], (L, D_MODEL)),
        "ffn1_w_gate": nrm(ks[2], (L, D_MODEL, D_FF), D_MODEL ** -0.5),
        "ffn1_w_up": nrm(ks[3], (L, D_MODEL, D_FF), D_MODEL ** -0.5),
        "ffn1_w_down": nrm(ks[4], (L, D_FF, D_MODEL), D_FF ** -0.5),
        "mix_norm": gain(ks[5], (L, D_MODEL)),
        "w_in": nrm(ks[6], (L, D_MODEL, D_IN), D_MODEL ** -0.5),
        "gm_ln_g": gain(ks[7], (L, GM_WIDTH)),
        "gm_ln_b": nrm(ks[8], (L, GM_WIDTH), 0.02),
        "gm_w_s": nrm(ks[9], (L, GM_GROUPS, GM_BLOCK, GM_BLOCK), GM_BLOCK ** -0.5),
        "gm_b_s": 1.0 + nrm(ks[10], (L, GM_GROUPS, GM_BLOCK), 0.1),
        "dn_conv_w": nrm(ks[11], (L, DN_CONV, 3 * DN_WIDTH), DN_CONV ** -0.5),
        "dn_a_log": jnp.log(jax.random.uniform(ks[12], (L, DN_HEADS), f32, 1.0, 16.0)),
        "dn_dt_bias": dt + jnp.log(-jnp.expm1(-dt)),
        "dn_norm": gain(ks[14], (L, DN_HEAD_DIM)),
        "w_out": nrm(ks[15], (L, D_MIX, D_MODEL), D_MIX ** -0.5),
        "ffn2_norm": gain(ks[16], (L, D_MODEL)),
        "ffn2_w_gate": nrm(ks[17], (L, D_MODEL, D_FF), D_MODEL ** -0.5),
        "ffn2_w_up": nrm(ks[18], (L, D_MODEL, D_FF), D_MODEL ** -0.5),
        "ffn2_w_down": nrm(ks[19], (L, D_FF, D_MODEL), D_FF ** -0.5),
        "final_norm": gain(ks[20], (D_MODEL,)),
    }


def reference(x, ffn1_norm, ffn1_w_gate, ffn1_w_up, ffn1_w_down, mix_norm, w_in,
              gm_ln_g, gm_ln_b, gm_w_s, gm_b_s, dn_conv_w, dn_a_log, dn_dt_bias, dn_norm,
              w_out, ffn2_norm, ffn2_w_gate, ffn2_w_up, ffn2_w_down, final_norm):
    B, T, _ = x.shape
    split_at = [GM_WIDTH, 2 * GM_WIDTH, 2 * GM_WIDTH + 3 * DN_WIDTH,
                2 * GM_WIDTH + 4 * DN_WIDTH, 2 * GM_WIDTH + 4 * DN_WIDTH + DN_HEADS]
    for l in range(DEPTH):
        x = x + 0.5 * swiglu(rmsnorm(x, ffn1_norm[l]), ffn1_w_gate[l], ffn1_w_up[l], ffn1_w_down[l])

        h = rmsnorm(x, mix_norm[l])
        p = h @ w_in[l]
        u_a, v_a, qkv, z, b_raw, a_raw = jnp.split(p, split_at, axis=-1)

        y_a = spatial_gating(jax.nn.gelu(u_a), jax.nn.gelu(v_a), gm_w_s[l], gm_b_s[l],
                             gm_ln_g[l], gm_ln_b[l])

        qkv = jax.nn.silu(causal_dwconv(qkv, dn_conv_w[l])).astype(jnp.float32)
        q, k, v = jnp.split(qkv, 3, axis=-1)
        q = l2norm(q.reshape(B, T, DN_HEADS, DN_HEAD_DIM))
        k = l2norm(k.reshape(B, T, DN_HEADS, DN_HEAD_DIM))
        v = v.reshape(B, T, DN_HEADS, DN_HEAD_DIM)
        beta = jax.nn.sigmoid(b_raw.astype(jnp.float32))
        g = -jnp.exp(dn_a_log[l].astype(jnp.float32)) * jax.nn.softplus(
            a_raw.astype(jnp.float32) + dn_dt_bias[l].astype(jnp.float32))
        o = gated_delta_rule(q, k, v, beta, g)
        zg = jax.nn.silu(z.astype(jnp.float32)).reshape(B, T, DN_HEADS, DN_HEAD_DIM)
        y_b = (rmsnorm(o, dn_norm[l]) * zg).reshape(B, T, DN_WIDTH).astype(x.dtype)

        x = x + jnp.concatenate([y_a, y_b], axis=-1) @ w_out[l]

        x = x + 0.5 * swiglu(rmsnorm(x, ffn2_norm[l]), ffn2_w_gate[l], ffn2_w_up[l], ffn2_w_down[l])
    return rmsnorm(x, final_norm)
```

```python
import numpy as np
import ml_dtypes
import concourse.bass as bass
import concourse.mybir as mybir
from concourse.bass_utils import run_bass_kernel_spmd

F32 = mybir.dt.float32
BF16 = mybir.dt.bfloat16
AF = mybir.ActivationFunctionType
ALU = mybir.AluOpType
AX = mybir.AxisListType

NT = 2048
TS = 8192
D = 1024
KC = 8
FF = 2816
NFB = 11
EPS = 1e-6
NEG = -30000.0


class Tok:
    __slots__ = ("name", "w", "r")

    def __init__(self, name=""):
        self.name = name
        self.w = None
        self.r = []


class Ins:
    __slots__ = ("eng", "fn", "dma", "deps", "sig", "ord", "ch", "chval", "n", "inc", "bar")

    def __init__(self, eng, fn, dma, ch):
        self.eng = eng
        self.fn = fn
        self.dma = dma
        self.deps = []
        self.sig = False
        self.ord = 0
        self.ch = ch
        self.chval = 0
        self.n = 0
        self.inc = 16
        self.bar = None


class Prog:
    ENGS = ("pe", "act", "dve", "pool", "sp")

    def __init__(self, nc):
        self.nc = nc
        self.ins = []
        self.ch_last = {}
        self.ch_cnt = {}
        self.last = {e: None for e in self.ENGS}

    def add(self, eng, fn, reads=(), writes=(), dma=False, ch=None, inc=16):
        i = Ins(eng, fn, dma, ch)
        i.inc = inc
        i.n = len(self.ins)
        deps = {}
        for t in reads:
            if t.w is not None:
                deps[t.w.n] = (t.w, "raw")
        for t in writes:
            if t.w is not None:
                deps[t.w.n] = (t.w, "waw")
            lastr = {}
            for r in t.r:
                if r.dma:
                    if r.n not in deps:
                        deps[r.n] = (r, "war")
                else:
                    lastr[r.eng] = r
            for r in lastr.values():
                if r.n not in deps:
                    deps[r.n] = (r, "war")
        if dma:
            prev = self.ch_last.get(ch)
            if prev is not None:
                deps[prev.n] = (prev, "raw")
            self.ch_last[ch] = i
            self.ch_cnt[ch] = self.ch_cnt.get(ch, 0) + inc
            i.chval = self.ch_cnt[ch]
        i.deps = list(deps.values())
        for t in reads:
            t.r.append(i)
        for t in writes:
            t.w = i
            t.r = []
        self.ins.append(i)
        if not dma:
            self.last[eng] = i
        return i

    def barrier(self):
        lastc = {e: self.last[e] for e in self.ENGS if self.last[e] is not None}
        chs = {c: v for c, v in self.ch_cnt.items() if not str(c).startswith("cc")}
        for e in self.ENGS:
            i = Ins(e, None, False, None)
            i.n = len(self.ins)
            i.bar = (lastc, chs)
            self.ins.append(i)

    def emit(self):
        nc = self.nc
        needed = {}
        for i in self.ins:
            lst = []
            if i.bar is not None:
                for e, d in i.bar[0].items():
                    d.sig = True
                needed[i.n] = lst
                continue
            for d, kind in i.deps:
                if d.dma:
                    lst.append(d)
                    continue
                if d.eng == i.eng and not i.dma:
                    if i.eng == "pe":
                        continue
                    if kind == "war":
                        continue
                d.sig = True
                lst.append(d)
            needed[i.n] = lst
        cnt = {e: 0 for e in self.ENGS}
        for i in self.ins:
            if i.bar is None and i.sig and not i.dma:
                cnt[i.eng] += 1
                i.ord = cnt[i.eng]
        esem = {e: nc.alloc_semaphore("sem_" + e) for e in self.ENGS}
        chsem = {c: nc.alloc_semaphore("ch_" + str(c)) for c in self.ch_cnt}
        streams = {e: [i for i in self.ins if i.eng == e] for e in self.ENGS}
        stats = {e: [len(streams[e]), cnt[e], 0] for e in self.ENGS}

        def run(eng_name, eng):
            waited = {}

            def do_wait(key, sem, val):
                if waited.get(key, 0) >= val:
                    return
                eng.wait_ge(sem, val)
                waited[key] = val
                stats[eng_name][2] += 1

            for i in streams[eng_name]:
                if i.bar is not None:
                    for e, d in i.bar[0].items():
                        do_wait(("e", e), esem[e], d.ord)
                    for c, v in i.bar[1].items():
                        do_wait(("c", c), chsem[c], v)
                    continue
                w = {}
                for d in needed[i.n]:
                    if d.dma:
                        key, val, sem = ("c", d.ch), d.chval, chsem[d.ch]
                    else:
                        key, val, sem = ("e", d.eng), d.ord, esem[d.eng]
                    if key not in w or w[key][1] < val:
                        w[key] = (sem, val)
                for key, (sem, val) in w.items():
                    do_wait(key, sem, val)
                bi = i.fn(eng)
                if i.dma:
                    bi.then_inc(chsem[i.ch], i.inc)
                elif i.sig:
                    bi.then_inc(esem[i.eng], 1)
            if eng_name == "sp":
                for c, n in self.ch_cnt.items():
                    do_wait(("c", c), chsem[c], n)
                for e in self.ENGS:
                    if cnt[e]:
                        do_wait(("e", e), esem[e], cnt[e])

        with nc.Block() as block:
            @block.tensor
            def _(e):
                run("pe", e)

            @block.scalar
            def _(e):
                run("act", e)

            @block.vector
            def _(e):
                run("dve", e)

            @block.gpsimd
            def _(e):
                run("pool", e)

            @block.sync
            def _(e):
                run("sp", e)
        return stats


class Arena:
    def __init__(self, nc, lo, hi):
        self.nc = nc
        self.lo = lo
        self.hi = hi
        self.cur = lo
        self.n = 0

    def alloc(self, shape, dtype, name="t"):
        sz = 4 if dtype == F32 else 2
        per = sz
        for s in shape[1:]:
            per *= s
        off = (self.cur + 63) // 64 * 64
        assert off + per <= self.hi, ("SBUF overflow", name, off + per - self.hi)
        self.cur = off + per
        self.n += 1
        return self.nc.alloc_sbuf_tensor_at("%s_%d" % (name, self.n), list(shape), dtype, offset=off)

    def mark(self):
        return self.cur

    def reset(self, m):
        self.cur = m


PV_N1, PV_NM, PV_N2, PV_NF = 0, 8, 16, 24
PV_DN = 32
PV_CQ, PV_CK, PV_CV = 33, 37, 41
PV_ALOG, PV_DTB = 45, 46
PV_SEL = 47
NPV = 52


def build(stage=99, dbg_cols=0):
    nc = bass.Bass("TRN2", target_bir_lowering=False)
    P = Prog(nc)

    def din(name, shape, dt=F32):
        return nc.dram_tensor(name, list(shape), dt, kind="ExternalInput").ap()

    xT = din("xT", [D, NT])
    w1g, w1u, w1d = din("w1g", [D, FF]), din("w1u", [D, FF]), din("w1d", [FF, D])
    w2g, w2u, w2d = din("w2g", [D, FF]), din("w2u", [D, FF]), din("w2d", [FF, D])
    w_uv = din("w_uv", [D, 1024])
    w_dn = din("w_dn", [D, 768])
    w_o = din("w_o", [D, D])
    pvec_d = din("pvec", [128, NPV])
    lng_d, lnb_d = din("lng", [128, 512]), din("lnb", [128, 512])
    wsT_d = din("wsT", [128, 8, 128])
    bsT_d = din("bsT", [128, 4, 128])
    yT = nc.dram_tensor("yT", [D, NT], F32, kind="ExternalOutput").ap()
    dbg = None
    if dbg_cols:
        dbg = nc.dram_tensor("dbg", [128, dbg_cols], F32, kind="ExternalOutput").ap()
    hg_in = nc.dram_tensor("hg_in", [8, KC, 128, 256], BF16).ap()
    hg_out = nc.dram_tensor("hg_out", [8, 4, KC, 128, 256], BF16).ap()
    yb_in = nc.dram_tensor("yb_in", [16, 128, 512], BF16).ap()
    yb_out = nc.dram_tensor("yb_out", [16, 4, 128, 512], BF16).ap()
    groups = [[0, 1, 2, 3], [4, 5, 6, 7]]

    ar = Arena(nc, 16512, 229344)
    X = ar.alloc([128, KC, NT], F32, "X")
    ya = ar.alloc([128, 4, NT], BF16, "ya")
    pvec = ar.alloc([128, NPV], F32, "pvec")
    ones_bf = ar.alloc([128, 128], BF16, "ones")
    identf = ar.alloc([128, 128], F32, "identf")
    identb = ar.alloc([128, 128], BF16, "identb")
    negA = ar.alloc([128, 1], F32, "negA")
    sq = [ar.alloc([128, 512], BF16, "sq") for _ in range(2)]
    rs = ar.alloc([128, 512], F32, "rs")
    mark_noH = ar.mark()
    hT = ar.alloc([128, KC, NT], BF16, "hT")
    PS = [nc.alloc_psum_tensor("ps%d" % i, [128, 512], F32) for i in range(8)]
    kPS = [Tok("ps%d" % i) for i in range(8)]
    kX = [[Tok() for _ in range(4)] for _ in range(KC)]
    kH = [[Tok() for _ in range(4)] for _ in range(KC)]
    kYa = [Tok() for _ in range(16)]
    kpv, kconst, knegA = Tok(), Tok(), Tok()
    ksq = [Tok(), Tok()]
    krs = Tok()
    phase_mark = ar.mark()

    def T(i):
        return slice(i * 512, (i + 1) * 512)

    P.add("sp", lambda e: e.dma_start(out=pvec[:, :], in_=pvec_d), [], [kpv], dma=True, ch="pv")
    for h in range(4):
        P.add("sp", lambda e, h=h: e.dma_start(out=X[:, 2 * h:2 * h + 2, :],
                                               in_=xT.rearrange("(kc p) t -> p kc t", p=128)[:, 2 * h:2 * h + 2, :]),
              [], [kX[2 * h][t] for t in range(4)] + [kX[2 * h + 1][t] for t in range(4)], dma=True, ch="x%d" % h)
    P.add("pool", lambda e: e.memset(ones_bf[:, :], 1.0), [], [kconst])
    P.add("pool", lambda e: e.memset(identf[:, :], 1.0), [kconst], [kconst])
    P.add("pool", lambda e: e.affine_select(out=identf[:, :], in_=identf[:, :], pattern=[[-1, 128]],
                                            compare_op=ALU.is_equal, fill=0.0, base=0, channel_multiplier=1),
          [kconst], [kconst])
    P.add("pool", lambda e: e.tensor_copy(out=identb[:, :], in_=identf[:, :]), [kconst], [kconst])
    P.add("act", lambda e: e.activation(out=negA[:, :], in_=pvec[:, PV_ALOG:PV_ALOG + 1], func=AF.Exp), [kpv], [knegA])
    P.add("dve", lambda e: e.tensor_scalar(out=negA[:, :], in0=negA[:, :], scalar1=-1.0, scalar2=None, op0=ALU.mult),
          [knegA], [knegA])

    def stats_to_rs(t, src_fn, src_toks, scale, nparts=128):
        n = len(src_fn)
        for j, (ap, tk) in enumerate(src_fn):
            b = j % 2
            P.add("act", lambda e, ap=ap, b=b: e.activation(out=sq[b][:, :], in_=ap, func=AF.Square), [tk], [ksq[b]])
            P.add("pe", lambda e, b=b, j=j: e.matmul(PS[7][:, :], lhsT=ones_bf[:, :], rhs=sq[b][:, :],
                                                     start=(j == 0), stop=(j == n - 1)),
                  [ksq[b], kconst], [kPS[7]])
        P.add("act", lambda e: e.activation(out=rs[:, :], in_=PS[7][:, :], func=AF.Ln, scale=scale, bias=EPS),
              [kPS[7]], [krs])
        P.add("act", lambda e: e.activation(out=rs[:, :], in_=rs[:, :], func=AF.Exp, scale=-0.5), [krs], [krs])

    def rmsnorm_to_h(gcol, tiles=range(4)):
        for t in tiles:
            stats_to_rs(t, [(X[:, kc, T(t)], kX[kc][t]) for kc in range(KC)], None, 1.0 / D)
            for kc in range(KC):
                P.add("dve", lambda e, kc=kc, t=t: e.scalar_tensor_tensor(
                    out=hT[:, kc, T(t)], in0=X[:, kc, T(t)], scalar=pvec[:, gcol + kc:gcol + kc + 1], in1=rs[:, :],
                    op0=ALU.mult, op1=ALU.mult), [kX[kc][t], krs, kpv], [kH[kc][t]])

    def dbg_dump(ap, toks, col0, ncols, parts=128):
        if dbg is None:
            return
        P.add("sp", lambda e: e.dma_start(out=dbg[0:parts, col0:col0 + ncols], in_=ap), toks, [Tok()], dma=True, ch="dbg")

    def ffn(wg_d, wu_d, wd_d, tag, gcol, tail=None):
        m = ar.mark()
        stg = [[ar.alloc([128, KC, 256], F32, "stgG") for _ in range(2)],
               [ar.alloc([128, KC, 256], F32, "stgU") for _ in range(2)],
               [ar.alloc([128, 2, 1024], F32, "stgD") for _ in range(2)]]
        wbf = [[ar.alloc([128, KC, 256], BF16, "wg") for _ in range(2)],
               [ar.alloc([128, KC, 256], BF16, "wu") for _ in range(2)],
               [ar.alloc([128, 2, 1024], BF16, "wd") for _ in range(2)]]
        sgt = [ar.alloc([128, 512], F32, "sg") for _ in range(2)]
        at = [ar.alloc([128, 2, 512], BF16, "a") for _ in range(2)]
        kstg = [[Tok(), Tok()] for _ in range(3)]
        kw = [[Tok(), Tok()] for _ in range(3)]
        ksg = [Tok(), Tok()]
        ka = [[Tok(), Tok()] for _ in range(2)]
        wgv = wg_d.rearrange("(kc p) f -> p kc f", p=128)
        wuv = wu_d.rearrange("(kc p) f -> p kc f", p=128)
        wdv = wd_d.rearrange("(fc p) d -> p fc d", p=128)

        def load(F):
            b = F % 2
            fs = slice(F * 256, (F + 1) * 256)
            P.add("sp", lambda e: e.dma_start(out=stg[0][b][:, :, :], in_=wgv[:, :, fs]), [], [kstg[0][b]], dma=True, ch="wG%d" % b)
            P.add("sp", lambda e: e.dma_start(out=stg[1][b][:, :, :], in_=wuv[:, :, fs]), [], [kstg[1][b]], dma=True, ch="wU%d" % b)
            P.add("sp", lambda e: e.dma_start(out=stg[2][b][:, :, :], in_=wdv[:, 2 * F:2 * F + 2, :]), [], [kstg[2][b]], dma=True, ch="wD%d" % b)

        def cast(F):
            b = F % 2
            for k in range(3):
                P.add("act", lambda e, k=k: e.activation(out=wbf[k][b][:, :, :], in_=stg[k][b][:, :, :], func=AF.Copy),
                      [kstg[k][b]], [kw[k][b]])

        def gu(F, t, idx):
            b = F % 2
            for fc in range(2):
                pg, pu = 2 * fc, 2 * fc + 1
                for kc in range(KC):
                    P.add("pe", lambda e, kc=kc, fc=fc, pg=pg: e.matmul(
                        PS[pg][:, :], lhsT=wbf[0][b][:, kc, fc * 128:(fc + 1) * 128], rhs=hT[:, kc, T(t)],
                        start=(kc == 0), stop=(kc == KC - 1)), [kw[0][b], kH[kc][t]], [kPS[pg]])
                for kc in range(KC):
                    P.add("pe", lambda e, kc=kc, fc=fc, pu=pu: e.matmul(
                        PS[pu][:, :], lhsT=wbf[1][b][:, kc, fc * 128:(fc + 1) * 128], rhs=hT[:, kc, T(t)],
                        start=(kc == 0), stop=(kc == KC - 1)), [kw[1][b], kH[kc][t]], [kPS[pu]])
                P.add("act", lambda e, fc=fc, pg=pg: e.activation(out=sgt[fc][:, :], in_=PS[pg][:, :], func=AF.Silu),
                      [kPS[pg]], [ksg[fc]])
                P.add("dve", lambda e, fc=fc, pu=pu: e.tensor_tensor(out=at[idx % 2][:, fc, :], in0=sgt[fc][:, :],
                                                                     in1=PS[pu][:, :], op=ALU.mult),
                      [ksg[fc], kPS[pu]], [ka[idx % 2][fc]])

        def down(F, t, idx):
            b = F % 2
            for dc in range(KC):
                pb = 4 + dc % 4
                for fc in range(2):
                    P.add("pe", lambda e, dc=dc, fc=fc, pb=pb: e.matmul(
                        PS[pb][:, :], lhsT=wbf[2][b][:, fc, dc * 128:(dc + 1) * 128], rhs=at[idx % 2][:, fc, :],
                        start=(fc == 0), stop=(fc == 1)), [kw[2][b], ka[idx % 2][fc]], [kPS[pb]])
                P.add("dve", lambda e, dc=dc, pb=pb: e.scalar_tensor_tensor(
                    out=X[:, dc, T(t)], in0=PS[pb][:, :], scalar=0.5, in1=X[:, dc, T(t)], op0=ALU.mult, op1=ALU.add),
                    [kPS[pb], kX[dc][t]], [kX[dc][t]])

        load(0)
        load(1)
        items = [(F, t) for F in range(NFB) for t in range(4)]
        rmsnorm_to_h(gcol, [0])
        for idx, (F, t) in enumerate(items):
            if t == 0:
                cast(F)
            if F == 0 and t + 1 < 4:
                rmsnorm_to_h(gcol, [t + 1])
            gu(F, t, idx)
            if idx > 0:
                pF, pt = items[idx - 1]
                down(pF, pt, idx - 1)
                if pt == 3 and pF + 2 < NFB:
                    load(pF + 2)
                if tail is not None and pF == NFB - 1:
                    tail(pt)
        pF, pt = items[-1]
        down(pF, pt, len(items) - 1)
        if tail is not None:
            tail(pt)
        ar.reset(m)

    ffn(w1g, w1u, w1d, "f1", PV_N1)
    if stage == 1:
        P.barrier()
        for t in range(4):
            P.add("sp", lambda e, t=t: e.dma_start(out=yT.rearrange("(kc p) t -> p kc t", p=128)[:, :, T(t)], in_=X[:, :, T(t)]),
                  [kX[kc][t] for kc in range(KC)], [Tok()], dma=True, ch="out")
        return nc, P.emit()
    P.barrier()

    mB = ar.mark()
    stgB = [ar.alloc([128, KC, 256], F32, "stgB") for _ in range(2)]
    wuv_bf = ar.alloc([128, KC, 1024], BF16, "wuv")
    gu_t = ar.alloc([128, 4, NT], BF16, "gu")
    lng = ar.alloc([128, 512], F32, "lng")
    lnb = ar.alloc([128, 512], F32, "lnb")
    wsTf = ar.alloc([128, 8, 128], F32, "wsTf")
    wsTb = ar.alloc([128, 8, 128], BF16, "wsTb")
    bsT = ar.alloc([128, 4, 128], F32, "bsT")
    NTH = 4
    vg = [ar.alloc([128, 512], F32, "vg") for _ in range(NTH)]
    vsq_ = [ar.alloc([128, 512], F32, "vsq") for _ in range(NTH)]
    vn = [ar.alloc([128, 512], BF16, "vn") for _ in range(NTH)]
    st = [ar.alloc([128, 8], F32, "st") for _ in range(NTH)]
    kstgB = [Tok(), Tok()]
    kwuv = [Tok() for _ in range(4)]
    kgu = [[Tok() for _ in range(4)] for _ in range(4)]
    kln, kws, kbs = Tok(), Tok(), Tok()
    kvg = [Tok() for _ in range(NTH)]
    kvsq_ = [Tok() for _ in range(NTH)]
    kvn = [Tok() for _ in range(NTH)]
    kst = [Tok() for _ in range(NTH)]
    P.add("sp", lambda e: e.dma_start(out=lng[:, :], in_=lng_d), [], [kln], dma=True, ch="c1")
    P.add("sp", lambda e: e.dma_start(out=lnb[:, :], in_=lnb_d), [], [kln], dma=True, ch="c1")
    P.add("sp", lambda e: e.dma_start(out=wsTf[:, :, :], in_=wsT_d), [], [kws], dma=True, ch="c2")
    P.add("sp", lambda e: e.dma_start(out=bsT[:, :, :], in_=bsT_d), [], [kbs], dma=True, ch="c3")
    P.add("pool", lambda e: e.memset(wsTf[64:128, :, 0:64], 0.0), [kws], [kws])
    P.add("pool", lambda e: e.tensor_copy(out=wsTb[:, :, :], in_=wsTf[:, :, :]), [kws], [kws])
    wuvv = w_uv.rearrange("(kc p) f -> p kc f", p=128)
    for q in range(4):
        b = q % 2
        P.add("sp", lambda e, q=q, b=b: e.dma_start(out=stgB[b][:, :, :], in_=wuvv[:, :, q * 256:(q + 1) * 256]),
              [], [kstgB[b]], dma=True, ch="wB%d" % b)
        P.add("act", lambda e, q=q, b=b: e.activation(out=wuv_bf[:, :, q * 256:(q + 1) * 256], in_=stgB[b][:, :, :], func=AF.Copy),
              [kstgB[b]], [kwuv[q]])
    rmsnorm_to_h(PV_NM)
    khg_in = [Tok() for _ in range(KC)]
    khg_out = [Tok() for _ in range(KC)]
    for pc in range(8):
        P.add("sp", lambda e, pc=pc: e.dma_start(out=hg_in[pc].rearrange("kc p t -> p kc t"), in_=hT[:, :, pc * 256:(pc + 1) * 256]),
              [kH[kc][pc // 2] for kc in range(KC)], [khg_in[pc]], dma=True, ch="hgx%d" % (pc % 2))
        P.add("pool", lambda e, pc=pc: e.collective_compute("AllGather", ALU.bypass, replica_groups=groups,
                                                            ins=[hg_in[pc].opt()], outs=[hg_out[pc].opt()]),
              [khg_in[pc]], [khg_out[pc]], dma=True, ch="cch%d" % pc, inc=1)

    for fc in range(4):
        for t in range(4):
            pb = (fc * 4 + t) % 4
            for kc in range(KC):
                P.add("pe", lambda e, kc=kc, fc=fc, t=t, pb=pb: e.matmul(
                    PS[pb][:, :], lhsT=wuv_bf[:, kc, fc * 128:(fc + 1) * 128], rhs=hT[:, kc, T(t)],
                    start=(kc == 0), stop=(kc == KC - 1)), [kwuv[fc // 2], kH[kc][t]], [kPS[pb]])
            P.add("act", lambda e, fc=fc, t=t, pb=pb: e.activation(out=gu_t[:, fc, T(t)], in_=PS[pb][:, :],
                                                                   func=AF.Gelu_apprx_tanh), [kPS[pb]], [kgu[fc][t]])
    class RecB:
        def __init__(self):
            self.calls = []

        def add(self, *a, **k):
            self.calls.append((a, k))

    def gm_block(P, blk):
        b = blk % NTH
        t = blk // 4
        bs = slice(blk * 128, (blk + 1) * 128)
        pv_, pm_ = b, 4 + b
        vsq, vt = vsq_[b], vg[b]
        mt = vsq_[b][:, :].rearrange("p (f i) -> p f i", f=4)
        kvsq, kvt, kmt = kvsq_[b], kvg[b], kvsq_[b]
        for kc in range(KC):
            P.add("pe", lambda e, kc=kc, bs=bs, pv_=pv_: e.matmul(
                PS[pv_][:, :], lhsT=hT[:, kc, bs], rhs=wuv_bf[:, kc, 512:1024],
                start=(kc == 0), stop=(kc == KC - 1)), [kwuv[2], kwuv[3], kH[kc][t]], [kPS[pv_]])
        P.add("act", lambda e, b=b, pv_=pv_: e.activation(out=vg[b][:, :], in_=PS[pv_][:, :], func=AF.Gelu_apprx_tanh),
              [kPS[pv_]], [kvg[b]])
        s = st[b]
        P.add("dve", lambda e, b=b, s=s: e.reduce_sum(out=s[:, 0:1], in_=vg[b][:, :], axis=AX.X), [kvg[b]], [kst[b]])
        P.add("act", lambda e, b=b, vsq=vsq: e.activation(out=vsq[:, :], in_=vg[b][:, :], func=AF.Square), [kvg[b]], [kvsq])
        P.add("dve", lambda e, s=s, vsq=vsq: e.reduce_sum(out=s[:, 1:2], in_=vsq[:, :], axis=AX.X), [kvsq], [kst[b]])
        P.add("dve", lambda e, s=s: e.tensor_scalar(out=s[:, 2:3], in0=s[:, 0:1], scalar1=1.0 / 512, scalar2=None, op0=ALU.mult),
              [kst[b]], [kst[b]])
        P.add("dve", lambda e, s=s: e.tensor_tensor(out=s[:, 3:4], in0=s[:, 2:3], in1=s[:, 2:3], op=ALU.mult), [kst[b]], [kst[b]])
        P.add("dve", lambda e, s=s: e.scalar_tensor_tensor(out=s[:, 4:5], in0=s[:, 1:2], scalar=1.0 / 512, in1=s[:, 3:4],
                                                           op0=ALU.mult, op1=ALU.subtract), [kst[b]], [kst[b]])
        P.add("act", lambda e, s=s: e.activation(out=s[:, 5:6], in_=s[:, 4:5], func=AF.Sqrt, bias=EPS), [kst[b]], [kst[b]])
        P.add("dve", lambda e, s=s: e.reciprocal(out=s[:, 5:6], in_=s[:, 5:6]), [kst[b]], [kst[b]])
        P.add("dve", lambda e, s=s: e.scalar_tensor_tensor(out=s[:, 6:7], in0=s[:, 2:3], scalar=-1.0, in1=s[:, 5:6],
                                                           op0=ALU.mult, op1=ALU.mult), [kst[b]], [kst[b]])
        P.add("act", lambda e, b=b, s=s: e.activation(out=vg[b][:, :], in_=vg[b][:, :], func=AF.Identity,
                                                      scale=s[:, 5:6], bias=s[:, 6:7]), [kvg[b], kst[b], kvsq], [kvg[b]])
        P.add("pool", lambda e, b=b: e.tensor_tensor(out=vg[b][:, :], in0=vg[b][:, :], in1=lng[:, :], op=ALU.mult), [kvg[b], kln], [kvg[b]])
        P.add("pool", lambda e, b=b: e.tensor_tensor(out=vn[b][:, :], in0=vg[b][:, :], in1=lnb[:, :], op=ALU.add), [kvg[b], kln], [kvn[b]])
        for fc in range(4):
            for hh in range(2):
                g = 2 * fc + hh
                P.add("pe", lambda e, fc=fc, hh=hh, g=g, b=b, pm_=pm_: e.matmul(
                    PS[pm_][hh * 64:(hh + 1) * 64, fc * 128:(fc + 1) * 128], lhsT=vn[b][:, g * 64:(g + 1) * 64],
                    rhs=wsTb[:, g, :], start=True, stop=True), [kvn[b], kws], [kPS[pm_]])
        P.add("dve", lambda e, pm_=pm_, mt=mt: e.tensor_tensor(out=mt, in0=PS[pm_][:, :].rearrange("p (f i) -> p f i", f=4),
                                                               in1=bsT[:, :, :], op=ALU.add), [kPS[pm_], kbs], [kmt])
        P.add("dve", lambda e, bs=bs, mt=mt: e.tensor_tensor(out=ya[:, :, bs], in0=mt, in1=gu_t[:, :, bs], op=ALU.mult),
              [kmt] + [kgu[fc][t] for fc in range(4)], [kYa[blk]])

    rr_ = [RecB() for _ in range(NTH)]
    for k in range(NTH):
        for m in range(16 // NTH):
            gm_block(rr_[k], NTH * m + k)
    nblk = len(rr_[0].calls) // (16 // NTH)
    posb = [0] * NTH
    nb = [len(r.calls) for r in rr_]
    while any(posb[k] < nb[k] for k in range(NTH)):
        best, bf = None, None
        for k in range(NTH):
            if posb[k] < nb[k]:
                f = posb[k] + k * nblk // NTH
                if bf is None or f < bf:
                    best, bf = k, f
        c0 = rr_[best].calls[posb[best]]
        P.add(*c0[0], **c0[1])
        posb[best] += 1
    if stage == 2:
        P.barrier()
        tmpf = ar.alloc([128, 4, 512], F32, "tmpf")
        ktmp = Tok()
        for t in range(4):
            P.add("pool", lambda e, t=t: e.tensor_copy(out=tmpf[:, :, :], in_=ya[:, :, T(t)]), kYa[4 * t:4 * t + 4], [ktmp])
            P.add("sp", lambda e, t=t: e.dma_start(out=yT.rearrange("(kc p) t -> p kc t", p=128)[:, 0:4, T(t)], in_=tmpf[:, :, :]),
                  [ktmp], [Tok()], dma=True, ch="out")
        return nc, P.emit()
    ar.reset(mB)
    P.barrier()

    mC = ar.mark()
    ar.reset(mark_noH)
    wdn_bf = ar.alloc([128, KC, 768], BF16, "wdn")
    mC1 = ar.mark()
    stgC = [ar.alloc([128, KC, 256], F32, "stgC") for _ in range(2)]
    kwdn = [Tok() for _ in range(3)]
    kstgC = [Tok(), Tok()]
    wdnv = w_dn.rearrange("(kc p) f -> p kc f", p=128)
    for q in range(3):
        b = q % 2
        P.add("sp", lambda e, q=q, b=b: e.dma_start(out=stgC[b][:, :, :], in_=wdnv[:, :, q * 256:(q + 1) * 256]),
              [], [kstgC[b]], dma=True, ch="wC%d" % b)
        P.add("act", lambda e, q=q, b=b: e.activation(out=wdn_bf[:, :, q * 256:(q + 1) * 256], in_=stgC[b][:, :, :], func=AF.Copy),
              [kstgC[b]], [kwdn[q]])
    P.barrier()
    ar.reset(mC1)
    h2 = [ar.alloc([128, KC, 512], BF16, "h2") for _ in range(2)]
    pre = [ar.alloc([128, 515], F32, "pre") for _ in range(3)]
    cacc = [ar.alloc([128, 512], F32, "cacc") for _ in range(3)]
    q_raw = ar.alloc([128, 512], F32, "q_raw")
    k_raw = ar.alloc([128, 512], F32, "k_raw")
    B_bc = ar.alloc([128, 512], F32, "B_bc")
    g_bc = ar.alloc([128, 512], F32, "g_bc")
    G_bc = ar.alloc([128, 512], F32, "G_bc")
    pk = g_bc
    mscan = ar.alloc([128, 512], F32, "mscan")
    negm = ar.alloc([64, 64], F32, "negm")
    q_bf = ar.alloc([128, 512], BF16, "q_bf")
    DT = ar.alloc([64, 8, 64], F32, "DT")
    expGt = ar.alloc([64, 8], F32, "expGt")
    k_bf = [ar.alloc([128, 512], BF16, "k_bf") for _ in range(2)]
    v_bf = [ar.alloc([128, 512], BF16, "v_bf") for _ in range(2)]
    QZ0 = [ar.alloc([64, 8, 128], BF16, "QZ0") for _ in range(2)]
    P0 = [ar.alloc([64, 8, 64], BF16, "P0") for _ in range(2)]
    tokS = [ar.alloc([64, 8, 2], F32, "tokS") for _ in range(2)]
    deckt = [ar.alloc([64, 8], F32, "deckt") for _ in range(2)]
    bexpt = [ar.alloc([64, 8], F32, "bexpt") for _ in range(2)]
    QZ = [ar.alloc([64, 8, 128], BF16, "QZ") for _ in range(2)]
    Pm = [ar.alloc([64, 8, 64], BF16, "Pm") for _ in range(2)]
    kbg = ar.alloc([64, 8, 128], BF16, "kbg")
    vb = ar.alloc([64, 8, 128], BF16, "vb")
    qd = [ar.alloc([128, 512], BF16, "qd") for _ in range(3)]
    expG = ar.alloc([128, 512], F32, "expG")
    kexpG = Tok()
    zg = [ar.alloc([128, 512], F32, "zg") for _ in range(3)]
    attnT = [ar.alloc([64, 8, 64], BF16, "attnT") for _ in range(3)]
    eGl = [ar.alloc([128, 8], F32, "eGl") for _ in range(3)]
    kd = [ar.alloc([64, 8, 128], BF16, "kd") for _ in range(2)]
    uc = [ar.alloc([64, 8, 128], BF16, "uc") for _ in range(2)]
    wc_tok = ar.alloc([64, 8, 128], BF16, "wc_tok")
    negMT = [ar.alloc([128, 8, 128], BF16, "negMT") for _ in range(2)]
    kwc_tok = Tok()
    knegMT = [Tok(), Tok()]
    wcT = [ar.alloc([128, 512], BF16, "wcT") for _ in range(2)]
    vnew = ar.alloc([64, 8, 128], BF16, "vnew")
    S = [ar.alloc([128, 128], F32, "S") for _ in range(2)]
    S_bf = [[ar.alloc([128, 128], BF16, "S_bf") for _ in range(9)] for _ in range(2)]
    kS_bf = [[Tok() for _ in range(9)] for _ in range(2)]
    osb = ar.alloc([128, 512], F32, "osb")
    ybf0 = ar.alloc([128, 512], BF16, "ybf")
    ybf = [ybf0, ybf0]

    kh2 = [Tok(), Tok()]
    kpre = [Tok() for _ in range(3)]
    kcacc = [Tok() for _ in range(3)]
    kq_raw, kk_raw, kB, kg, kG = (Tok() for _ in range(5))
    kpk = kg
    kmask, kq_bf, kDT, kexpGt = (Tok() for _ in range(4))
    kk_bf, kv_bf, kP0, ktokS, kdeckt, kbexpt = ([Tok(), Tok()] for _ in range(6))
    kQZ0 = [[Tok(), Tok()], [Tok(), Tok()]]
    kP0h = [[Tok(), Tok()], [Tok(), Tok()]]
    kQZ = [[Tok(), Tok()], [Tok(), Tok()]]
    kPmh = [[Tok(), Tok()], [Tok(), Tok()]]
    kkbg, kvb = Tok(), Tok()
    kqd, kzg, kattn, keGl = ([Tok(), Tok(), Tok()] for _ in range(4))
    kkd, kuc, kwcT = ([Tok(), Tok()] for _ in range(3))
    kvnew = [Tok() for _ in range(8)]
    kS = [Tok(), Tok()]
    kosb = Tok()
    kybf0 = Tok()
    kybf = [kybf0, kybf0]
    kyb_in = [Tok() for _ in range(16)]
    kyb_out = [Tok() for _ in range(16)]
    kPS5h = [kPS[5], kPS[5]]

    P.add("pool", lambda e: e.memset(mscan[:, :], 1.0), [], [kmask])
    P.add("pool", lambda e: e.memset(mscan[:, :].rearrange("p (c k) -> p c k", k=64)[:, :, 0:1], 0.0), [kmask], [kmask])
    P.add("pool", lambda e: e.memset(negm[:, :], 0.0), [kmask], [kmask])
    P.add("pool", lambda e: e.affine_select(out=negm[:, :], in_=negm[:, :], pattern=[[1, 64]], compare_op=ALU.is_ge,
                                            fill=NEG, base=0, channel_multiplier=-1), [kmask], [kmask])
    P.add("pool", lambda e: e.memset(S[0][:, :], 0.0), [], [kS[0]])
    P.add("pool", lambda e: e.memset(S_bf[1][8][:, :], 0.0), [], [kS_bf[1][8]])
    for i in range(3):
        P.add("pool", lambda e, i=i: e.memset(pre[i][:, 0:3], 0.0), [], [kpre[i]])
    for p2 in range(2):
        P.add("pool", lambda e, p2=p2: e.tensor_copy(out=QZ0[p2][:, :, 64:128],
                                                     in_=identb[0:64, 0:64].unsqueeze(1).to_broadcast([64, 8, 64])),
              [kconst], [kQZ0[p2][0], kQZ0[p2][1]])

    class Rec:
        def __init__(self):
            self.calls = []

        def add(self, *a, **k):
            self.calls.append((a, k))

    def replay(recs):
        recs = [r for r in recs if r.calls]
        units = []
        for r in recs:
            u, curu = [], []
            for call in r.calls:
                is_pe = (call[0][0] == "pe")
                if curu and (is_pe != curu_pe or len(curu) >= 8):
                    u.append(curu)
                    curu = []
                if not is_pe and curu:
                    u.append(curu)
                    curu = []
                curu.append(call)
                curu_pe = is_pe
            if curu:
                u.append(curu)
            units.append(u)
        n = [sum(len(x) for x in u) for u in units]
        done = [0] * len(recs)
        pos = [0] * len(recs)
        while True:
            best, bf = None, None
            for i in range(len(recs)):
                if pos[i] < len(units[i]):
                    f = done[i] / n[i]
                    if bf is None or f < bf:
                        best, bf = i, f
            if best is None:
                break
            for a, k in units[best][pos[best]]:
                P.add(*a, **k)
            done[best] += len(units[best][pos[best]])
            pos[best] += 1

    NSEG = TS // 512
    if stage == 3:
        NSEG = 4
    conv_cols = [PV_CQ, PV_CK, PV_CV]

    def load_h2(R, s):
        b = s % 2
        r = s // 4
        for half in range(2):
            pc = 2 * (s % 4) + half
            src = hg_out[pc, r].rearrange("kc p t -> p kc t")
            R.add("sp", lambda e, half=half, src=src: e.dma_start(out=h2[b][:, :, half * 256:(half + 1) * 256], in_=src),
                  [khg_out[pc]], [kh2[b]], dma=True, ch="h2%d" % b)

    def rsqrt_act(R, dst, kdst, pb, scale):
        R.add("act", lambda e: e.activation(out=dst[:, :], in_=PS[pb][:, :], func=AF.Ln, scale=scale, bias=EPS), [kPS[pb]], [kdst])
        R.add("act", lambda e: e.activation(out=dst[:, :], in_=dst[:, :], func=AF.Exp, scale=-0.5), [kdst], [kdst])

    def thread_A1(R, s):
        hb = s % 2
        p2 = s % 2
        p3 = s % 3
        if s + 1 < NSEG:
            load_h2(R, s + 1)

        def proj(c6, pb):
            for kc in range(KC):
                R.add("pe", lambda e, kc=kc: e.matmul(
                    PS[pb][:, :], lhsT=wdn_bf[:, kc, c6 * 128:(c6 + 1) * 128], rhs=h2[hb][:, kc, :],
                    start=(kc == 0), stop=(kc == KC - 1)), [kwdn[c6 // 2], kh2[hb]], [kPS[pb]])
        proj(0, 0)
        proj(1, 1)
        proj(2, 2)
        for i in range(3):
            R.add("act", lambda e, i=i: e.activation(out=pre[i][:, 3:515], in_=PS[i][:, :], func=AF.Copy), [kPS[i]], [kpre[i]])
        proj(3, 0)
        proj(4, 1)
        proj(5, 2)
        R.add("act", lambda e: e.activation(out=zg[p3][:, :], in_=PS[0][:, :], func=AF.Copy), [kPS[0]], [kzg[p3]])
        R.add("act", lambda e: e.activation(out=B_bc[:, :], in_=PS[1][:, :], func=AF.Sigmoid), [kPS[1]], [kB])
        R.add("act", lambda e: e.activation(out=g_bc[:, :], in_=PS[2][:, :], func=AF.Exp, bias=pvec[:, PV_DTB:PV_DTB + 1]),
              [kPS[2], kpv], [kg])
        R.add("act", lambda e: e.activation(out=g_bc[:, :], in_=g_bc[:, :], func=AF.Ln, bias=1.0), [kg], [kg])
        R.add("dve", lambda e: e.tensor_scalar(out=g_bc[:, :], in0=g_bc[:, :], scalar1=negA[:, 0:1], scalar2=None, op0=ALU.mult),
              [kg, knegA], [kg])
        R.add("dve", lambda e: e.tensor_tensor_scan(out=G_bc[:, :], data0=mscan[:, :], data1=g_bc[:, :], initial=0.0,
                                                    op0=ALU.mult, op1=ALU.add), [kg, kmask], [kG])
        R.add("act", lambda e: e.activation(out=expG[:, :], in_=G_bc[:, :], func=AF.Exp), [kG], [kexpG])
        R.add("act", lambda e: e.activation(out=eGl[p3][:, :], in_=G_bc[:, :].rearrange("p (c k) -> p c k", k=64)[:, :, 63],
                                            func=AF.Exp), [kG], [keGl[p3]])
        for i in range(3):
            cc = conv_cols[i]
            R.add("dve", lambda e, i=i, cc=cc: e.tensor_scalar(out=cacc[i][:, :], in0=pre[i][:, 0:512], scalar1=pvec[:, cc:cc + 1],
                                                               scalar2=None, op0=ALU.mult), [kpre[i], kpv], [kcacc[i]])
            for tap in range(1, 4):
                R.add("dve", lambda e, i=i, cc=cc, tap=tap: e.scalar_tensor_tensor(
                    out=cacc[i][:, :], in0=pre[i][:, tap:tap + 512], scalar=pvec[:, cc + tap:cc + tap + 1], in1=cacc[i][:, :],
                    op0=ALU.mult, op1=ALU.add), [kpre[i], kcacc[i], kpv], [kcacc[i]])
            R.add("dve", lambda e, i=i: e.tensor_copy(out=pre[i][:, 0:3], in_=pre[i][:, 512:515]), [kpre[i]], [kpre[i]])
        R.add("act", lambda e: e.activation(out=q_raw[:, :], in_=cacc[0][:, :], func=AF.Silu), [kcacc[0]], [kq_raw])
        R.add("act", lambda e: e.activation(out=k_raw[:, :], in_=cacc[1][:, :], func=AF.Silu), [kcacc[1]], [kk_raw])
        R.add("act", lambda e: e.activation(out=v_bf[p2][:, :], in_=cacc[2][:, :], func=AF.Silu), [kcacc[2]], [kv_bf[p2]])
        R.add("act", lambda e: e.activation(out=zg[p3][:, :], in_=zg[p3][:, :], func=AF.Silu), [kzg[p3]], [kzg[p3]])
        R.add("act", lambda e: e.activation(out=pk[0:64, :], in_=G_bc[0:64, :], func=AF.Copy), [kG, kg], [kpk])
        R.add("act", lambda e: e.activation(out=pk[64:128, :], in_=B_bc[64:128, :], func=AF.Copy), [kB, kg], [kpk])
        for c in range(8):
            pb = c // 4
            R.add("pe", lambda e, c=c, pb=pb: e.transpose(out=PS[pb][0:64, (c % 4) * 128:(c % 4 + 1) * 128],
                                                          in_=pk[:, c * 64:(c + 1) * 64], identity=identf[:, :]),
                  [kpk, kconst], [kPS[pb]])
        tS = tokS[p2]
        for hh in range(2):
            R.add("act", lambda e, hh=hh: e.activation(
                out=tS[:, 4 * hh:4 * hh + 4, :],
                in_=PS[hh][0:64, :].rearrange("p (c two k) -> p c two k", c=4, two=2)[:, :, :, 0],
                func=AF.Copy), [kPS[hh]], [ktokS[p2]])
        Gt = tS[:, :, 0]
        Bt = tS[:, :, 1]
        Glast64 = G_bc[0:64, :].rearrange("p (c k) -> p c k", k=64)[:, :, 63]
        R.add("act", lambda e: e.activation(out=expGt[:, :], in_=Gt, func=AF.Exp), [ktokS[p2]], [kexpGt])
        R.add("dve", lambda e: e.tensor_tensor(out=deckt[p2][:, :], in0=Glast64, in1=Gt, op=ALU.subtract), [kG, ktokS[p2]], [kdeckt[p2]])
        R.add("act", lambda e: e.activation(out=deckt[p2][:, :], in_=deckt[p2][:, :], func=AF.Exp), [kdeckt[p2]], [kdeckt[p2]])
        R.add("dve", lambda e: e.tensor_tensor(out=bexpt[p2][:, :], in0=Bt, in1=expGt[:, :], op=ALU.mult), [ktokS[p2], kexpGt], [kbexpt[p2]])
        G3 = G_bc[0:64, :].rearrange("p (c k) -> p c k", k=64)
        R.add("dve", lambda e: e.tensor_tensor(out=DT[:, :, :], in0=G3, in1=tS[:, :, 0:1].to_broadcast([64, 8, 64]),
                                               op=ALU.subtract), [kG, ktokS[p2]], [kDT])
        R.add("dve", lambda e: e.tensor_tensor(out=DT[:, :, :], in0=DT[:, :, :],
                                               in1=negm[:, :].unsqueeze(1).to_broadcast([64, 8, 64]), op=ALU.add),
              [kDT, kmask], [kDT])
        R.add("act", lambda e: e.activation(out=DT[:, :, :], in_=DT[:, :, :], func=AF.Exp), [kDT], [kDT])
        R.add("act", lambda e: e.activation(out=sq[0][:, :], in_=q_raw[:, :], func=AF.Square), [kq_raw], [ksq[0]])
        R.add("pe", lambda e: e.matmul(PS[0][:, :], lhsT=ones_bf[:, :], rhs=sq[0][:, :], start=True, stop=True),
              [ksq[0], kconst], [kPS[0]])
        R.add("act", lambda e: e.activation(out=sq[0][:, :], in_=k_raw[:, :], func=AF.Square), [kk_raw], [ksq[0]])
        R.add("pe", lambda e: e.matmul(PS[1][:, :], lhsT=ones_bf[:, :], rhs=sq[0][:, :], start=True, stop=True),
              [ksq[0], kconst], [kPS[1]])
        rsqrt_act(R, cacc[0], kcacc[0], 0, 1.0)
        rsqrt_act(R, cacc[1], kcacc[1], 1, 1.0)
        R.add("dve", lambda e: e.scalar_tensor_tensor(out=q_raw[:, :], in0=q_raw[:, :], scalar=float(128 ** -0.5), in1=cacc[0][:, :],
                                                      op0=ALU.mult, op1=ALU.mult), [kq_raw, kcacc[0]], [kq_raw])
        R.add("dve", lambda e: e.tensor_tensor(out=k_bf[p2][:, :], in0=k_raw[:, :], in1=cacc[1][:, :], op=ALU.mult),
              [kk_raw, kcacc[1]], [kk_bf[p2]])
        R.add("act", lambda e: e.activation(out=q_bf[:, :], in_=q_raw[:, :], func=AF.Copy), [kq_raw], [kq_bf])
        R.add("dve", lambda e: e.tensor_tensor(out=qd[p3][:, :], in0=expG[:, :], in1=q_raw[:, :], op=ALU.mult),
              [kq_raw, kexpG], [kqd[p3]])
        for c in range(8):
            cs = slice(c * 64, (c + 1) * 64)
            R.add("pe", lambda e, cs=cs: e.matmul(PS[2][0:64, cs], lhsT=k_bf[p2][:, cs], rhs=k_bf[p2][:, cs], start=True, stop=True),
                  [kk_bf[p2]], [kPS[2]])
        for c in range(8):
            cs = slice(c * 64, (c + 1) * 64)
            R.add("pe", lambda e, cs=cs: e.matmul(PS[0][0:64, cs], lhsT=k_bf[p2][:, cs], rhs=q_bf[:, cs], start=True, stop=True),
                  [kk_bf[p2], kq_bf], [kPS[0]])
        R.add("dve", lambda e: e.tensor_tensor(out=attnT[p3][:, :, :], in0=PS[0][0:64, :].rearrange("p (c k) -> p c k", k=64),
                                               in1=DT[:, :, :], op=ALU.mult), [kPS[0], kDT], [kattn[p3]])
        R.add("dve", lambda e: e.tensor_tensor(out=DT[:, :, :], in0=DT[:, :, :],
                                               in1=identf[0:64, 0:64].unsqueeze(1).to_broadcast([64, 8, 64]), op=ALU.subtract),
              [kDT, kconst], [kDT])
        R.add("dve", lambda e: e.tensor_tensor(out=DT[:, :, :], in0=DT[:, :, :],
                                               in1=B_bc[0:64, :].rearrange("p (c k) -> p c k", k=64), op=ALU.mult),
              [kDT, kB], [kDT])
        R.add("dve", lambda e: e.scalar_tensor_tensor(out=QZ0[p2][:, :, 0:64], in0=PS[2][0:64, :].rearrange("p (c k) -> p c k", k=64),
                                                      scalar=-1.0, in1=DT[:, :, :], op0=ALU.mult, op1=ALU.mult),
              [kPS[2], kDT], [kQZ0[p2][0], kQZ0[p2][1]])
        psT = PS[1][0:64, 0:256].bitcast(BF16)
        for c in range(8):
            R.add("pe", lambda e, c=c: e.transpose(out=psT[:, c * 64:(c + 1) * 64], in_=QZ0[p2][:, c, 0:64],
                                                   identity=identb[0:64, 0:64]), [kQZ0[p2][c // 4], kconst], [kPS[1]])
        R.add("act", lambda e: e.activation(out=P0[p2][:, :, :], in_=psT.rearrange("p (c k) -> p c k", k=64), func=AF.Copy),
              [kPS[1]], [kP0h[p2][0], kP0h[p2][1]])

    def thread_A2(R, s):
        p2 = s % 2
        psK = PS[3][0:64, :].bitcast(BF16)
        psV = PS[4][0:64, :].bitcast(BF16)
        for c in range(8):
            cs = slice(c * 64, (c + 1) * 64)
            R.add("pe", lambda e, c=c, cs=cs: e.transpose(out=psK[:, c * 128:(c + 1) * 128], in_=k_bf[p2][:, cs], identity=identb[:, :]),
                  [kk_bf[p2], kconst], [kPS[3]])
        for c in range(8):
            cs = slice(c * 64, (c + 1) * 64)
            R.add("pe", lambda e, c=c, cs=cs: e.transpose(out=psV[:, c * 128:(c + 1) * 128], in_=v_bf[p2][:, cs], identity=identb[:, :]),
                  [kv_bf[p2], kconst], [kPS[4]])
        psK3 = psK.rearrange("p (c k) -> p c k", k=128)
        psV3 = psV.rearrange("p (c k) -> p c k", k=128)
        R.add("dve", lambda e: e.tensor_tensor(out=kbg[:, :, :], in0=psK3, in1=bexpt[p2][:, :].unsqueeze(2).to_broadcast([64, 8, 128]),
                                               op=ALU.mult), [kPS[3], kbexpt[p2]], [kkbg])
        R.add("dve", lambda e: e.tensor_tensor(out=kd[p2][:, :, :], in0=psK3, in1=deckt[p2][:, :].unsqueeze(2).to_broadcast([64, 8, 128]),
                                               op=ALU.mult), [kPS[3], kdeckt[p2]], [kkd[p2]])
        R.add("dve", lambda e: e.tensor_tensor(out=vb[:, :, :], in0=psV3, in1=tokS[p2][:, :, 1:2].to_broadcast([64, 8, 128]),
                                               op=ALU.mult), [kPS[4], ktokS[p2]], [kvb])
        for lv in range(6):
            last = (lv == 5)
            if lv == 0:
                srcQZ, ksQZ, srcP, ksP = QZ0[p2], kQZ0[p2], P0[p2], kP0h[p2]
            else:
                srcQZ, ksQZ, srcP, ksP = QZ[(lv - 1) % 2], kQZ[(lv - 1) % 2], Pm[(lv - 1) % 2], kPmh[(lv - 1) % 2]
            dstQZ, kdQZ, dstP, kdP = QZ[lv % 2], kQZ[lv % 2], Pm[lv % 2], kPmh[lv % 2]
            for hh in range(2):
                for c in range(4 * hh, 4 * hh + 4):
                    R.add("pe", lambda e, c=c, hh=hh, srcP=srcP, srcQZ=srcQZ: e.matmul(
                        PS[3 + hh][0:64, (c % 4) * 128:(c % 4 + 1) * 128], lhsT=srcP[:, c, :], rhs=srcQZ[:, c, :],
                        start=True, stop=True), [ksP[hh], ksQZ[hh]], [kPS[3 + hh]])
                if not last:
                    for c in range(4 * hh, 4 * hh + 4):
                        R.add("pe", lambda e, c=c, srcP=srcP, srcQZ=srcQZ: e.matmul(
                            PS[5][0:64, c * 64:(c + 1) * 64], lhsT=srcQZ[:, c, 0:64], rhs=srcP[:, c, :],
                            start=True, stop=True), [ksP[hh], ksQZ[hh]], [kPS5h[hh]])
            for hh in range(2):
                psv = PS[3 + hh][0:64, :].rearrange("p (c k) -> p c k", k=128)
                hs = slice(4 * hh, 4 * hh + 4)
                if not last:
                    R.add("act", lambda e, psv=psv, hs=hs, dstQZ=dstQZ: e.activation(
                        out=dstQZ[:, hs, 0:64], in_=psv[:, :, 0:64], func=AF.Copy), [kPS[3 + hh]], [kdQZ[hh]])
                R.add("dve", lambda e, psv=psv, hs=hs, dstQZ=dstQZ, srcQZ=srcQZ: e.tensor_tensor(
                    out=dstQZ[:, hs, 64:128], in0=psv[:, :, 64:128], in1=srcQZ[:, hs, 64:128],
                    op=ALU.add), [kPS[3 + hh], ksQZ[hh]], [kdQZ[hh]])
                if not last:
                    R.add("act", lambda e, hh=hh, hs=hs, dstP=dstP: e.activation(
                        out=dstP[:, hs, :], in_=PS[5][0:64, hh * 256:(hh + 1) * 256].rearrange("p (c k) -> p c k", k=64),
                        func=AF.Copy), [kPS5h[hh]], [kdP[hh]])
        Zf = QZ[1]
        kZf = kQZ[1]
        for c in range(8):
            pb = 3 + c // 4
            R.add("pe", lambda e, c=c, pb=pb: e.matmul(PS[pb][0:64, (c % 4) * 128:(c % 4 + 1) * 128], lhsT=Zf[:, c, 64:128],
                                                       rhs=vb[:, c, :], start=True, stop=True), [kZf[c // 4], kvb], [kPS[pb]])
        for c in range(8):
            R.add("pe", lambda e, c=c: e.matmul(PS[5][:, c * 64:(c + 1) * 64], lhsT=kbg[:, c, :], rhs=Zf[:, c, 64:128],
                                                start=True, stop=True), [kZf[c // 4], kkbg], [kPS5h[0], kPS5h[1]])
        for hh in range(2):
            R.add("act", lambda e, hh=hh: e.activation(out=uc[p2][:, 4 * hh:4 * hh + 4, :],
                                                       in_=PS[3 + hh][0:64, :].rearrange("p (c k) -> p c k", k=128), func=AF.Copy),
                  [kPS[3 + hh]], [kuc[p2]])
        R.add("act", lambda e: e.activation(out=wcT[p2][:, :], in_=PS[5][:, :], func=AF.Copy), [kPS5h[0], kPS5h[1]], [kwcT[p2]])
        for c in range(8):
            pb = 3 + c // 4
            R.add("pe", lambda e, c=c, pb=pb: e.matmul(PS[pb][0:64, (c % 4) * 128:(c % 4 + 1) * 128], lhsT=Zf[:, c, 64:128],
                                                       rhs=kbg[:, c, :], start=True, stop=True), [kZf[c // 4], kkbg], [kPS[pb]])
        for hh in range(2):
            R.add("act", lambda e, hh=hh: e.activation(out=wc_tok[:, 4 * hh:4 * hh + 4, :],
                                                       in_=PS[3 + hh][0:64, :].rearrange("p (c k) -> p c k", k=128), func=AF.Copy),
                  [kPS[3 + hh]], [kwc_tok])
        for c in range(8):
            pb = 3 + c // 4
            R.add("pe", lambda e, c=c, pb=pb: e.matmul(PS[pb][:, (c % 4) * 128:(c % 4 + 1) * 128], lhsT=wc_tok[:, c, :],
                                                       rhs=kd[p2][:, c, :], start=True, stop=True), [kwc_tok, kkd[p2]], [kPS[pb]])
        for hh in range(2):
            R.add("act", lambda e, hh=hh: e.activation(out=negMT[p2][:, 4 * hh:4 * hh + 4, :],
                                                       in_=PS[3 + hh][:, :].rearrange("p (c k) -> p c k", k=128), func=AF.Copy,
                                                       scale=-1.0), [kPS[3 + hh]], [knegMT[p2]])

    def thread_B(R, s, cur):
        p2 = s % 2
        p3 = s % 3
        sp = s % 2

        def sbf(k):
            return (S_bf[1 - sp][8], kS_bf[1 - sp][8]) if k == 0 else (S_bf[sp][k], kS_bf[sp][k])
        for c in range(8):
            nxt = 1 - cur
            s_in, ks_in = sbf(c)
            s_out, ks_out = sbf(c + 1)
            R.add("pe", lambda e, c=c: e.matmul(PS[6][:, 0:128], lhsT=kd[p2][:, c, :], rhs=uc[p2][:, c, :], start=True, stop=False),
                  [kkd[p2], kuc[p2]], [kPS[6]])
            R.add("pe", lambda e, c=c, s_in=s_in: e.matmul(PS[6][:, 0:128], lhsT=negMT[p2][:, c, :], rhs=s_in[:, :], start=False, stop=True),
                  [knegMT[p2], ks_in], [kPS[6]])
            R.add("dve", lambda e, c=c, cur=cur, s_out=s_out: e.scalar_tensor_tensor(
                out=s_out[:, :], in0=S[cur][:, :], scalar=eGl[p3][:, c:c + 1], in1=PS[6][:, 0:128], op0=ALU.mult, op1=ALU.add),
                [kS[cur], keGl[p3], kPS[6]], [ks_out])
            R.add("dve", lambda e, c=c, cur=cur, nxt=nxt: e.scalar_tensor_tensor(
                out=S[nxt][:, :], in0=S[cur][:, :], scalar=eGl[p3][:, c:c + 1], in1=PS[6][:, 0:128], op0=ALU.mult, op1=ALU.add),
                [kS[cur], keGl[p3], kPS[6]], [kS[nxt]])
            cur = nxt
        for hh in range(2):
            for c in range(4 * hh, 4 * hh + 4):
                cs = slice(c * 64, (c + 1) * 64)
                s_in, ks_in = sbf(c)
                R.add("pe", lambda e, c=c, cs=cs, s_in=s_in: e.matmul(PS[6][0:64, (c % 4) * 128:(c % 4 + 1) * 128], lhsT=wcT[p2][:, cs],
                                                                    rhs=s_in[:, :], start=True, stop=True), [kwcT[p2], ks_in], [kPS[6]])
            R.add("dve", lambda e, hh=hh: e.tensor_tensor(out=vnew[:, 4 * hh:4 * hh + 4, :], in0=uc[p2][:, 4 * hh:4 * hh + 4, :],
                                                          in1=PS[6][0:64, :].rearrange("p (c k) -> p c k", k=128), op=ALU.subtract),
                  [kuc[p2], kPS[6]], [kvnew[hh]])
        for c in range(8):
            cs = slice(c * 64, (c + 1) * 64)
            s_in, ks_in = sbf(c)
            R.add("pe", lambda e, cs=cs, s_in=s_in: e.matmul(PS[7][:, cs], lhsT=s_in[:, :], rhs=qd[p3][:, cs], start=True, stop=False),
                  [ks_in, kqd[p3]], [kPS[7]])
            R.add("pe", lambda e, c=c, cs=cs: e.matmul(PS[7][:, cs], lhsT=vnew[:, c, :], rhs=attnT[p3][:, c, :], start=False, stop=True),
                  [kvnew[c // 4], kattn[p3]], [kPS[7]])
        R.add("act", lambda e: e.activation(out=osb[:, :], in_=PS[7][:, :], func=AF.Copy), [kPS[7]], [kosb])
        R.add("act", lambda e: e.activation(out=sq[1][:, :], in_=osb[:, :], func=AF.Square), [kosb], [ksq[1]])
        R.add("pe", lambda e: e.matmul(PS[6][:, :], lhsT=ones_bf[:, :], rhs=sq[1][:, :], start=True, stop=True),
              [ksq[1], kconst], [kPS[6]])
        rsqrt_act(R, rs, krs, 6, 1.0 / 128)
        R.add("dve", lambda e: e.scalar_tensor_tensor(out=osb[:, :], in0=osb[:, :], scalar=pvec[:, PV_DN:PV_DN + 1], in1=rs[:, :],
                                                      op0=ALU.mult, op1=ALU.mult), [kosb, krs, kpv], [kosb])
        yb_ = s % 2
        R.add("dve", lambda e: e.tensor_tensor(out=ybf[yb_][:, :], in0=osb[:, :], in1=zg[p3][:, :], op=ALU.mult),
              [kosb, kzg[p3]], [kybf[yb_]])
        R.add("sp", lambda e: e.dma_start(out=yb_in[s], in_=ybf[yb_][:, :]),
              [kybf[yb_]], [kyb_in[s]], dma=True, ch="yb%d" % yb_)
        R.add("pool", lambda e: e.collective_compute("AllGather", ALU.bypass, replica_groups=groups,
                                                     ins=[yb_in[s].opt()], outs=[yb_out[s].opt()]),
              [kyb_in[s]], [kyb_out[s]], dma=True, ch="ccy%d" % (s % 4), inc=1)
        return cur

    R0 = Rec()
    load_h2(R0, 0)
    thread_A1(R0, 0)
    replay([R0])
    Ra, Rb = Rec(), Rec()
    thread_A2(Ra, 0)
    if NSEG > 1:
        thread_A1(Rb, 1)
    replay([Ra, Rb])
    cur = 0
    for s in range(NSEG):
        RB, RA2, RA1 = Rec(), Rec(), Rec()
        cur = thread_B(RB, s, cur)
        if s + 1 < NSEG:
            thread_A2(RA2, s + 1)
        if s + 2 < NSEG:
            thread_A1(RA1, s + 2)
        replay([RB, RA2, RA1])
    if stage == 3:
        P.barrier()
        for t in range(4):
            P.add("sp", lambda e, t=t: e.dma_start(out=yT.rearrange("(kc p) t -> p kc t", p=128)[:, :, T(t)], in_=X[:, :, T(t)]),
                  [kX[kc][t] for kc in range(KC)], [Tok()], dma=True, ch="out")
        return nc, P.emit()
    ar.reset(mC)
    P.barrier()

    mD = ar.mark()
    cand = [ar.alloc([128, 4, NT], BF16, "cand") for _ in range(2)]
    ysel = ar.alloc([128, 4, NT], BF16, "ysel")
    wo_bf = ar.alloc([128, KC, D], BF16, "wo")
    stgD_ = [ar.alloc([128, KC, 256], F32, "stgDD") for _ in range(2)]
    kcand = [Tok(), Tok()]
    kysel = Tok()
    kwo = [Tok() for _ in range(4)]
    kstgD_ = [Tok(), Tok()]
    wov = w_o.rearrange("(kc p) f -> p kc f", p=128)
    for q in range(4):
        b = q % 2
        P.add("sp", lambda e, q=q, b=b: e.dma_start(out=stgD_[b][:, :, :], in_=wov[:, :, q * 256:(q + 1) * 256]),
              [], [kstgD_[b]], dma=True, ch="wO%d" % b)
        P.add("act", lambda e, q=q, b=b: e.activation(out=wo_bf[:, :, q * 256:(q + 1) * 256], in_=stgD_[b][:, :, :], func=AF.Copy),
              [kstgD_[b]], [kwo[q]])
    kcandq = [[Tok() for _ in range(4)] for _ in range(2)]
    kyselt = [Tok() for _ in range(4)]
    for j in range(4):
        b = j % 2
        for q in range(4):
            P.add("sp", lambda e, j=j, b=b, q=q: e.dma_start(out=cand[b][:, :, q * 512:(q + 1) * 512],
                                                             in_=yb_out[4 * j + q].rearrange("h e t -> e h t")),
                  [kyb_out[4 * j + q]], [kcandq[b][q]], dma=True, ch="cand%d%d" % (b, q % 2))
        for q in range(4):
            if j == 0:
                P.add("dve", lambda e, b=b, q=q: e.tensor_scalar(out=ysel[:, :, T(q)], in0=cand[b][:, :, T(q)],
                                                                 scalar1=pvec[:, PV_SEL:PV_SEL + 1], scalar2=None, op0=ALU.mult),
                      [kcandq[b][q], kpv], [kyselt[q]])
            else:
                P.add("dve", lambda e, j=j, b=b, q=q: e.scalar_tensor_tensor(
                    out=ysel[:, :, T(q)], in0=cand[b][:, :, T(q)], scalar=pvec[:, PV_SEL + j:PV_SEL + j + 1], in1=ysel[:, :, T(q)],
                    op0=ALU.mult, op1=ALU.add), [kcandq[b][q], kyselt[q], kpv], [kyselt[q]])
    for t in range(4):
        for dc in range(KC):
            pb = (t * KC + dc) % 8
            for kc in range(KC):
                if kc < 4:
                    rhs, rt = ya[:, kc, T(t)], kYa[4 * t:4 * t + 4]
                else:
                    rhs, rt = ysel[:, kc - 4, T(t)], [kyselt[t]]
                P.add("pe", lambda e, dc=dc, kc=kc, pb=pb, rhs=rhs: e.matmul(
                    PS[pb][:, :], lhsT=wo_bf[:, kc, dc * 128:(dc + 1) * 128], rhs=rhs, start=(kc == 0), stop=(kc == KC - 1)),
                    [kwo[dc // 2]] + list(rt), [kPS[pb]])
            P.add("dve", lambda e, dc=dc, t=t, pb=pb: e.tensor_tensor(out=X[:, dc, T(t)], in0=PS[pb][:, :], in1=X[:, dc, T(t)],
                                                                      op=ALU.add), [kPS[pb], kX[dc][t]], [kX[dc][t]])
    ar.reset(mD)
    P.barrier()
    ot = [ar.alloc([128, 512], F32, "ot") for _ in range(3)]
    kot = [Tok(), Tok(), Tok()]
    otc = [0]

    def final_tile(t):
        stats_to_rs(t, [(X[:, kc, T(t)], kX[kc][t]) for kc in range(KC)], None, 1.0 / D)
        for kc in range(KC):
            b = otc[0] % 3
            otc[0] += 1
            P.add("dve", lambda e, kc=kc, t=t, b=b: e.scalar_tensor_tensor(
                out=ot[b][:, :], in0=X[:, kc, T(t)], scalar=pvec[:, PV_NF + kc:PV_NF + kc + 1], in1=rs[:, :],
                op0=ALU.mult, op1=ALU.mult), [kX[kc][t], krs, kpv], [kot[b]])
            P.add("sp", lambda e, kc=kc, t=t, b=b: e.dma_start(out=yT[kc * 128:(kc + 1) * 128, T(t)], in_=ot[b][:, :]),
                  [kot[b]], [Tok()], dma=True, ch="out%d" % b)

    ffn(w2g, w2u, w2d, "f2", PV_N2, tail=final_tile)
    return nc, P.emit()


def make_in_maps(x, ffn1_norm, ffn1_w_gate, ffn1_w_up, ffn1_w_down, mix_norm, w_in,
                 gm_ln_g, gm_ln_b, gm_w_s, gm_b_s, dn_conv_w, dn_a_log, dn_dt_bias, dn_norm,
                 w_out, ffn2_norm, ffn2_w_gate, ffn2_w_up, ffn2_w_down, final_norm):
    f = lambda a: np.ascontiguousarray(np.asarray(a, dtype=np.float32))
    x = f(x)
    w_in0 = f(w_in)[0]
    common = {
        "w1g": f(ffn1_w_gate)[0], "w1u": f(ffn1_w_up)[0], "w1d": f(ffn1_w_down)[0],
        "w2g": f(ffn2_w_gate)[0], "w2u": f(ffn2_w_up)[0], "w2d": f(ffn2_w_down)[0],
        "w_uv": np.ascontiguousarray(w_in0[:, 0:1024]),
        "w_o": f(w_out)[0],
        "lng": np.ascontiguousarray(np.broadcast_to(f(gm_ln_g)[0][None, :], (128, 512))),
        "lnb": np.ascontiguousarray(np.broadcast_to(f(gm_ln_b)[0][None, :], (128, 512))),
        "wsT": np.ascontiguousarray(np.transpose(f(gm_w_s)[0], (2, 0, 1))),
    }
    bs = f(gm_b_s)[0]
    bsT = np.empty((128, 4, 128), np.float32)
    for fc in range(4):
        bsT[0:64, fc, :] = bs[2 * fc][None, :]
        bsT[64:128, fc, :] = bs[2 * fc + 1][None, :]
    common["bsT"] = bsT
    cw = f(dn_conv_w)[0]
    in_maps = []
    for c in range(8):
        b, j = c // 4, c % 4
        hd = j
        m = dict(common)
        m["xT"] = np.ascontiguousarray(x[b, j * NT:(j + 1) * NT, :].T)
        cols = []
        for base in (1024, 1536, 2048, 2560):
            cols.append(w_in0[:, base + hd * 128: base + (hd + 1) * 128])
        cols.append(np.repeat(w_in0[:, 3072 + hd:3072 + hd + 1], 128, axis=1))
        cols.append(np.repeat(w_in0[:, 3076 + hd:3076 + hd + 1], 128, axis=1))
        m["w_dn"] = np.ascontiguousarray(np.concatenate(cols, axis=1))
        pv = np.zeros((128, NPV), np.float32)
        for k0, g in ((PV_N1, ffn1_norm), (PV_NM, mix_norm), (PV_N2, ffn2_norm)):
            pv[:, k0:k0 + 8] = f(g)[0].reshape(8, 128).T
        pv[:, PV_NF:PV_NF + 8] = f(final_norm).reshape(8, 128).T
        pv[:, PV_DN] = f(dn_norm)[0]
        for k0, base in ((PV_CQ, 0), (PV_CK, 512), (PV_CV, 1024)):
            pv[:, k0:k0 + 4] = cw[:, base + hd * 128: base + (hd + 1) * 128].T
        pv[:, PV_ALOG] = f(dn_a_log)[0][hd]
        pv[:, PV_DTB] = f(dn_dt_bias)[0][hd]
        pv[:, PV_SEL + j] = 1.0
        m["pvec"] = pv
        in_maps.append(m)
    return in_maps


_CACHE = {}


def kernel(**inputs):
    if "nc" not in _CACHE:
        _CACHE["nc"] = build()[0]
    nc = _CACHE["nc"]
    in_maps = make_in_maps(**inputs)
    res = run_bass_kernel_spmd(nc, in_maps, core_ids=list(range(8)))
    out = np.empty((2, TS, D), np.float32)
    for c in range(8):
        b, j = c // 4, c % 4
        out[b, j * NT:(j + 1) * NT, :] = np.asarray(res.results[c]["yT"]).T
    return out
```

```python
import numpy as np
import ml_dtypes
import concourse.bass as bass
import concourse.mybir as mybir
from concourse.bass_utils import run_bass_kernel_spmd

F32 = mybir.dt.float32
BF16 = mybir.dt.bfloat16
AF = mybir.ActivationFunctionType
ALU = mybir.AluOpType
AX = mybir.AxisListType

NT = 2048
TS = 8192
D = 1024
KC = 8
FF = 2816
NFB = 11
EPS = 1e-6
NEG = -30000.0


class Tok:
    __slots__ = ("name", "w", "r")

    def __init__(self, name=""):
        self.name = name
        self.w = None
        self.r = []


class Ins:
    __slots__ = ("eng", "fn", "dma", "deps", "sig", "ord", "ch", "chval", "n", "inc", "bar")

    def __init__(self, eng, fn, dma, ch):
        self.eng = eng
        self.fn = fn
        self.dma = dma
        self.deps = []
        self.sig = False
        self.ord = 0
        self.ch = ch
        self.chval = 0
        self.n = 0
        self.inc = 16
        self.bar = None


class Prog:
    ENGS = ("pe", "act", "dve", "pool", "sp")

    def __init__(self, nc):
        self.nc = nc
        self.ins = []
        self.ch_last = {}
        self.ch_cnt = {}
        self.last = {e: None for e in self.ENGS}

    def add(self, eng, fn, reads=(), writes=(), dma=False, ch=None, inc=16):
        i = Ins(eng, fn, dma, ch)
        i.inc = inc
        i.n = len(self.ins)
        deps = {}
        for t in reads:
            if t.w is not None:
                deps[t.w.n] = (t.w, "raw")
        for t in writes:
            if t.w is not None:
                deps[t.w.n] = (t.w, "waw")
            lastr = {}
            for r in t.r:
                if r.dma:
                    if r.n not in deps:
                        deps[r.n] = (r, "war")
                else:
                    lastr[r.eng] = r
            for r in lastr.values():
                if r.n not in deps:
                    deps[r.n] = (r, "war")
        if dma:
            prev = self.ch_last.get(ch)
            if prev is not None:
                deps[prev.n] = (prev, "raw")
            self.ch_last[ch] = i
            self.ch_cnt[ch] = self.ch_cnt.get(ch, 0) + inc
            i.chval = self.ch_cnt[ch]
        i.deps = list(deps.values())
        for t in reads:
            t.r.append(i)
        for t in writes:
            t.w = i
            t.r = []
        self.ins.append(i)
        if not dma:
            self.last[eng] = i
        return i

    def barrier(self):
        lastc = {e: self.last[e] for e in self.ENGS if self.last[e] is not None}
        chs = {c: v for c, v in self.ch_cnt.items() if not str(c).startswith("cc")}
        for e in self.ENGS:
            i = Ins(e, None, False, None)
            i.n = len(self.ins)
            i.bar = (lastc, chs)
            self.ins.append(i)

    def emit(self):
        nc = self.nc
        needed = {}
        for i in self.ins:
            lst = []
            if i.bar is not None:
                for e, d in i.bar[0].items():
                    d.sig = True
                needed[i.n] = lst
                continue
            for d, kind in i.deps:
                if d.dma:
                    lst.append(d)
                    continue
                if d.eng == i.eng and not i.dma:
                    if i.eng == "pe":
                        continue
                    if kind == "war":
                        continue
                d.sig = True
                lst.append(d)
            needed[i.n] = lst
        cnt = {e: 0 for e in self.ENGS}
        for i in self.ins:
            if i.bar is None and i.sig and not i.dma:
                cnt[i.eng] += 1
                i.ord = cnt[i.eng]
        esem = {e: nc.alloc_semaphore("sem_" + e) for e in self.ENGS}
        chsem = {c: nc.alloc_semaphore("ch_" + str(c)) for c in self.ch_cnt}
        streams = {e: [i for i in self.ins if i.eng == e] for e in self.ENGS}
        stats = {e: [len(streams[e]), cnt[e], 0] for e in self.ENGS}

        def run(eng_name, eng):
            waited = {}

            def do_wait(key, sem, val):
                if waited.get(key, 0) >= val:
                    return
                eng.wait_ge(sem, val)
                waited[key] = val
                stats[eng_name][2] += 1

            for i in streams[eng_name]:
                if i.bar is not None:
                    for e, d in i.bar[0].items():
                        do_wait(("e", e), esem[e], d.ord)
                    for c, v in i.bar[1].items():
                        do_wait(("c", c), chsem[c], v)
                    continue
                w = {}
                for d in needed[i.n]:
                    if d.dma:
                        key, val, sem = ("c", d.ch), d.chval, chsem[d.ch]
                    else:
                        key, val, sem = ("e", d.eng), d.ord, esem[d.eng]
                    if key not in w or w[key][1] < val:
                        w[key] = (sem, val)
                for key, (sem, val) in w.items():
                    do_wait(key, sem, val)
                bi = i.fn(eng)
                if i.dma:
                    bi.then_inc(chsem[i.ch], i.inc)
                elif i.sig:
                    bi.then_inc(esem[i.eng], 1)
            if eng_name == "sp":
                for c, n in self.ch_cnt.items():
                    do_wait(("c", c), chsem[c], n)
                for e in self.ENGS:
                    if cnt[e]:
                        do_wait(("e", e), esem[e], cnt[e])

        with nc.Block() as block:
            @block.tensor
            def _(e):
                run("pe", e)

            @block.scalar
            def _(e):
                run("act", e)

            @block.vector
            def _(e):
                run("dve", e)

            @block.gpsimd
            def _(e):
                run("pool", e)

            @block.sync
            def _(e):
                run("sp", e)
        return stats


class Arena:
    def __init__(self, nc, lo, hi):
        self.nc = nc
        self.lo = lo
        self.hi = hi
        self.cur = lo
        self.n = 0

    def alloc(self, shape, dtype, name="t"):
        sz = 4 if dtype == F32 else 2
        per = sz
        for s in shape[1:]:
            per *= s
        off = (self.cur + 31) // 32 * 32
        assert off + per <= self.hi, ("SBUF overflow", name, off + per - self.hi)
        self.cur = off + per
        self.n += 1
        return self.nc.alloc_sbuf_tensor_at("%s_%d" % (name, self.n), list(shape), dtype, offset=off)

    def mark(self):
        return self.cur

    def reset(self, m):
        self.cur = m


PV_N1, PV_NM, PV_N2, PV_NF = 0, 8, 16, 24
PV_DN = 32
PV_CQ, PV_CK, PV_CV = 33, 37, 41
PV_ALOG, PV_DTB = 45, 46
PV_SEL = 47
NPV = 52


def build(stage=99, dbg_cols=0):
    nc = bass.Bass("TRN2", target_bir_lowering=False)
    P = Prog(nc)

    def din(name, shape, dt=F32):
        return nc.dram_tensor(name, list(shape), dt, kind="ExternalInput").ap()

    xT = din("xT", [D, NT])
    w1g, w1u, w1d = din("w1g", [D, FF]), din("w1u", [D, FF]), din("w1d", [FF, D])
    w2g, w2u, w2d = din("w2g", [D, FF]), din("w2u", [D, FF]), din("w2d", [FF, D])
    w_uv = din("w_uv", [D, 1024])
    w_dn = din("w_dn", [D, 768])
    w_o = din("w_o", [D, D])
    pvec_d = din("pvec", [128, NPV])
    lng_d, lnb_d = din("lng", [128, 512]), din("lnb", [128, 512])
    wsT_d = din("wsT", [128, 8, 128])
    bsT_d = din("bsT", [128, 4, 128])
    yT = nc.dram_tensor("yT", [D, NT], F32, kind="ExternalOutput").ap()
    dbg = None
    if dbg_cols:
        dbg = nc.dram_tensor("dbg", [128, dbg_cols], F32, kind="ExternalOutput").ap()
    hg_in = nc.dram_tensor("hg_in", [8, KC, 128, 256], BF16).ap()
    hg_out = nc.dram_tensor("hg_out", [8, 4, KC, 128, 256], BF16).ap()
    yb_in = nc.dram_tensor("yb_in", [16, 128, 512], BF16).ap()
    yb_out = nc.dram_tensor("yb_out", [16, 4, 128, 512], BF16).ap()
    groups = [[0, 1, 2, 3], [4, 5, 6, 7]]

    ar = Arena(nc, 16512, 229344)
    X = ar.alloc([128, KC, NT], F32, "X")
    ya = ar.alloc([128, 4, NT], BF16, "ya")
    pvec = ar.alloc([128, NPV], F32, "pvec")
    ones_bf = ar.alloc([128, 128], BF16, "ones")
    identf = ar.alloc([128, 128], F32, "identf")
    identb = ar.alloc([128, 128], BF16, "identb")
    negA = ar.alloc([128, 1], F32, "negA")
    sq = [ar.alloc([128, 512], BF16, "sq") for _ in range(2)]
    rs = ar.alloc([128, 512], F32, "rs")
    mark_noH = ar.mark()
    hT = ar.alloc([128, KC, NT], BF16, "hT")
    PS = [nc.alloc_psum_tensor("ps%d" % i, [128, 512], F32) for i in range(8)]
    kPS = [Tok("ps%d" % i) for i in range(8)]
    kX = [[Tok() for _ in range(4)] for _ in range(KC)]
    kH = [[Tok() for _ in range(4)] for _ in range(KC)]
    kYa = [Tok() for _ in range(16)]
    kpv, kconst, knegA = Tok(), Tok(), Tok()
    ksq = [Tok(), Tok()]
    krs = Tok()
    phase_mark = ar.mark()

    def T(i):
        return slice(i * 512, (i + 1) * 512)

    P.add("sp", lambda e: e.dma_start(out=pvec[:, :], in_=pvec_d), [], [kpv], dma=True, ch="pv")
    for h in range(4):
        P.add("sp", lambda e, h=h: e.dma_start(out=X[:, 2 * h:2 * h + 2, :],
                                               in_=xT.rearrange("(kc p) t -> p kc t", p=128)[:, 2 * h:2 * h + 2, :]),
              [], [kX[2 * h][t] for t in range(4)] + [kX[2 * h + 1][t] for t in range(4)], dma=True, ch="x%d" % h)
    P.add("pool", lambda e: e.memset(ones_bf[:, :], 1.0), [], [kconst])
    P.add("pool", lambda e: e.memset(identf[:, :], 1.0), [kconst], [kconst])
    P.add("pool", lambda e: e.affine_select(out=identf[:, :], in_=identf[:, :], pattern=[[-1, 128]],
                                            compare_op=ALU.is_equal, fill=0.0, base=0, channel_multiplier=1),
          [kconst], [kconst])
    P.add("pool", lambda e: e.tensor_copy(out=identb[:, :], in_=identf[:, :]), [kconst], [kconst])
    P.add("act", lambda e: e.activation(out=negA[:, :], in_=pvec[:, PV_ALOG:PV_ALOG + 1], func=AF.Exp), [kpv], [knegA])
    P.add("dve", lambda e: e.tensor_scalar(out=negA[:, :], in0=negA[:, :], scalar1=-1.0, scalar2=None, op0=ALU.mult),
          [knegA], [knegA])

    def stats_to_rs(t, src_fn, src_toks, scale, nparts=128):
        n = len(src_fn)
        for j, (ap, tk) in enumerate(src_fn):
            b = j % 2
            P.add("act", lambda e, ap=ap, b=b: e.activation(out=sq[b][:, :], in_=ap, func=AF.Square), [tk], [ksq[b]])
            P.add("pe", lambda e, b=b, j=j: e.matmul(PS[7][:, :], lhsT=ones_bf[:, :], rhs=sq[b][:, :],
                                                     start=(j == 0), stop=(j == n - 1)),
                  [ksq[b], kconst], [kPS[7]])
        P.add("act", lambda e: e.activation(out=rs[:, :], in_=PS[7][:, :], func=AF.Ln, scale=scale, bias=EPS),
              [kPS[7]], [krs])
        P.add("act", lambda e: e.activation(out=rs[:, :], in_=rs[:, :], func=AF.Exp, scale=-0.5), [krs], [krs])

    def rmsnorm_to_h(gcol, tiles=range(4)):
        for t in tiles:
            stats_to_rs(t, [(X[:, kc, T(t)], kX[kc][t]) for kc in range(KC)], None, 1.0 / D)
            for kc in range(KC):
                P.add("dve", lambda e, kc=kc, t=t: e.scalar_tensor_tensor(
                    out=hT[:, kc, T(t)], in0=X[:, kc, T(t)], scalar=pvec[:, gcol + kc:gcol + kc + 1], in1=rs[:, :],
                    op0=ALU.mult, op1=ALU.mult), [kX[kc][t], krs, kpv], [kH[kc][t]])

    def dbg_dump(ap, toks, col0, ncols, parts=128):
        if dbg is None:
            return
        P.add("sp", lambda e: e.dma_start(out=dbg[0:parts, col0:col0 + ncols], in_=ap), toks, [Tok()], dma=True, ch="dbg")

    def ffn(wg_d, wu_d, wd_d, tag, gcol, tail=None):
        m = ar.mark()
        stg = [[ar.alloc([128, KC, 256], F32, "stgG") for _ in range(2)],
               [ar.alloc([128, KC, 256], F32, "stgU") for _ in range(2)],
               [ar.alloc([128, 2, 1024], F32, "stgD") for _ in range(2)]]
        wbf = [[ar.alloc([128, KC, 256], BF16, "wg") for _ in range(2)],
               [ar.alloc([128, KC, 256], BF16, "wu") for _ in range(2)],
               [ar.alloc([128, 2, 1024], BF16, "wd") for _ in range(2)]]
        sgt = [ar.alloc([128, 512], F32, "sg") for _ in range(2)]
        at = [ar.alloc([128, 2, 512], BF16, "a") for _ in range(2)]
        kstg = [[Tok(), Tok()] for _ in range(3)]
        kw = [[Tok(), Tok()] for _ in range(3)]
        ksg = [Tok(), Tok()]
        ka = [[Tok(), Tok()] for _ in range(2)]
        wgv = wg_d.rearrange("(kc p) f -> p kc f", p=128)
        wuv = wu_d.rearrange("(kc p) f -> p kc f", p=128)
        wdv = wd_d.rearrange("(fc p) d -> p fc d", p=128)

        def load(F):
            b = F % 2
            fs = slice(F * 256, (F + 1) * 256)
            P.add("sp", lambda e: e.dma_start(out=stg[0][b][:, :, :], in_=wgv[:, :, fs]), [], [kstg[0][b]], dma=True, ch="wG%d" % b)
            P.add("sp", lambda e: e.dma_start(out=stg[1][b][:, :, :], in_=wuv[:, :, fs]), [], [kstg[1][b]], dma=True, ch="wU%d" % b)
            P.add("sp", lambda e: e.dma_start(out=stg[2][b][:, :, :], in_=wdv[:, 2 * F:2 * F + 2, :]), [], [kstg[2][b]], dma=True, ch="wD%d" % b)

        def cast(F):
            b = F % 2
            for k in range(3):
                P.add("act", lambda e, k=k: e.activation(out=wbf[k][b][:, :, :], in_=stg[k][b][:, :, :], func=AF.Copy),
                      [kstg[k][b]], [kw[k][b]])

        def gu(F, t, idx):
            b = F % 2
            for fc in range(2):
                pg, pu = 2 * fc, 2 * fc + 1
                for kc in range(KC):
                    P.add("pe", lambda e, kc=kc, fc=fc, pg=pg: e.matmul(
                        PS[pg][:, :], lhsT=wbf[0][b][:, kc, fc * 128:(fc + 1) * 128], rhs=hT[:, kc, T(t)],
                        start=(kc == 0), stop=(kc == KC - 1)), [kw[0][b], kH[kc][t]], [kPS[pg]])
                for kc in range(KC):
                    P.add("pe", lambda e, kc=kc, fc=fc, pu=pu: e.matmul(
                        PS[pu][:, :], lhsT=wbf[1][b][:, kc, fc * 128:(fc + 1) * 128], rhs=hT[:, kc, T(t)],
                        start=(kc == 0), stop=(kc == KC - 1)), [kw[1][b], kH[kc][t]], [kPS[pu]])
                P.add("act", lambda e, fc=fc, pg=pg: e.activation(out=sgt[fc][:, :], in_=PS[pg][:, :], func=AF.Silu),
                      [kPS[pg]], [ksg[fc]])
                P.add("dve", lambda e, fc=fc, pu=pu: e.tensor_tensor(out=at[idx % 2][:, fc, :], in0=sgt[fc][:, :],
                                                                     in1=PS[pu][:, :], op=ALU.mult),
                      [ksg[fc], kPS[pu]], [ka[idx % 2][fc]])

        def down(F, t, idx):
            b = F % 2
            for dc in range(KC):
                pb = 4 + dc % 4
                for fc in range(2):
                    P.add("pe", lambda e, dc=dc, fc=fc, pb=pb: e.matmul(
                        PS[pb][:, :], lhsT=wbf[2][b][:, fc, dc * 128:(dc + 1) * 128], rhs=at[idx % 2][:, fc, :],
                        start=(fc == 0), stop=(fc == 1)), [kw[2][b], ka[idx % 2][fc]], [kPS[pb]])
                P.add("dve", lambda e, dc=dc, pb=pb: e.scalar_tensor_tensor(
                    out=X[:, dc, T(t)], in0=PS[pb][:, :], scalar=0.5, in1=X[:, dc, T(t)], op0=ALU.mult, op1=ALU.add),
                    [kPS[pb], kX[dc][t]], [kX[dc][t]])

        load(0)
        load(1)
        items = [(F, t) for F in range(NFB) for t in range(4)]
        rmsnorm_to_h(gcol, [0])
        for idx, (F, t) in enumerate(items):
            if t == 0:
                cast(F)
            if F == 0 and t + 1 < 4:
                rmsnorm_to_h(gcol, [t + 1])
            gu(F, t, idx)
            if idx > 0:
                pF, pt = items[idx - 1]
                down(pF, pt, idx - 1)
                if pt == 3 and pF + 2 < NFB:
                    load(pF + 2)
                if tail is not None and pF == NFB - 1:
                    tail(pt)
        pF, pt = items[-1]
        down(pF, pt, len(items) - 1)
        if tail is not None:
            tail(pt)
        ar.reset(m)

    ffn(w1g, w1u, w1d, "f1", PV_N1)
    if stage == 1:
        P.barrier()
        for t in range(4):
            P.add("sp", lambda e, t=t: e.dma_start(out=yT.rearrange("(kc p) t -> p kc t", p=128)[:, :, T(t)], in_=X[:, :, T(t)]),
                  [kX[kc][t] for kc in range(KC)], [Tok()], dma=True, ch="out")
        return nc, P.emit()
    P.barrier()

    mB = ar.mark()
    stgB = [ar.alloc([128, KC, 256], F32, "stgB") for _ in range(2)]
    wuv_bf = ar.alloc([128, KC, 1024], BF16, "wuv")
    gu_t = ar.alloc([128, 4, NT], BF16, "gu")
    lng = ar.alloc([128, 512], F32, "lng")
    lnb = ar.alloc([128, 512], F32, "lnb")
    wsTf = ar.alloc([128, 8, 128], F32, "wsTf")
    wsTb = ar.alloc([128, 8, 128], BF16, "wsTb")
    bsT = ar.alloc([128, 4, 128], F32, "bsT")
    NTH = 4
    vg = [ar.alloc([128, 512], F32, "vg") for _ in range(NTH)]
    vsq_ = [ar.alloc([128, 512], F32, "vsq") for _ in range(NTH)]
    vn = [ar.alloc([128, 512], BF16, "vn") for _ in range(NTH)]
    st = [ar.alloc([128, 8], F32, "st") for _ in range(NTH)]
    kstgB = [Tok(), Tok()]
    kwuv = [Tok() for _ in range(4)]
    kgu = [[Tok() for _ in range(4)] for _ in range(4)]
    kln, kws, kbs = Tok(), Tok(), Tok()
    kvg = [Tok() for _ in range(NTH)]
    kvsq_ = [Tok() for _ in range(NTH)]
    kvn = [Tok() for _ in range(NTH)]
    kst = [Tok() for _ in range(NTH)]
    P.add("sp", lambda e: e.dma_start(out=lng[:, :], in_=lng_d), [], [kln], dma=True, ch="c1")
    P.add("sp", lambda e: e.dma_start(out=lnb[:, :], in_=lnb_d), [], [kln], dma=True, ch="c1")
    P.add("sp", lambda e: e.dma_start(out=wsTf[:, :, :], in_=wsT_d), [], [kws], dma=True, ch="c2")
    P.add("sp", lambda e: e.dma_start(out=bsT[:, :, :], in_=bsT_d), [], [kbs], dma=True, ch="c3")
    P.add("pool", lambda e: e.memset(wsTf[64:128, :, 0:64], 0.0), [kws], [kws])
    P.add("pool", lambda e: e.tensor_copy(out=wsTb[:, :, :], in_=wsTf[:, :, :]), [kws], [kws])
    wuvv = w_uv.rearrange("(kc p) f -> p kc f", p=128)
    for q in range(4):
        b = q % 2
        P.add("sp", lambda e, q=q, b=b: e.dma_start(out=stgB[b][:, :, :], in_=wuvv[:, :, q * 256:(q + 1) * 256]),
              [], [kstgB[b]], dma=True, ch="wB%d" % b)
        P.add("act", lambda e, q=q, b=b: e.activation(out=wuv_bf[:, :, q * 256:(q + 1) * 256], in_=stgB[b][:, :, :], func=AF.Copy),
              [kstgB[b]], [kwuv[q]])
    rmsnorm_to_h(PV_NM)
    khg_in = [Tok() for _ in range(KC)]
    khg_out = [Tok() for _ in range(KC)]
    for pc in range(8):
        P.add("sp", lambda e, pc=pc: e.dma_start(out=hg_in[pc].rearrange("kc p t -> p kc t"), in_=hT[:, :, pc * 256:(pc + 1) * 256]),
              [kH[kc][pc // 2] for kc in range(KC)], [khg_in[pc]], dma=True, ch="hgx%d" % (pc % 2))
        P.add("pool", lambda e, pc=pc: e.collective_compute("AllGather", ALU.bypass, replica_groups=groups,
                                                            ins=[hg_in[pc].opt()], outs=[hg_out[pc].opt()]),
              [khg_in[pc]], [khg_out[pc]], dma=True, ch="cch%d" % pc, inc=1)

    for fc in range(4):
        for t in range(4):
            pb = (fc * 4 + t) % 4
            for kc in range(KC):
                P.add("pe", lambda e, kc=kc, fc=fc, t=t, pb=pb: e.matmul(
                    PS[pb][:, :], lhsT=wuv_bf[:, kc, fc * 128:(fc + 1) * 128], rhs=hT[:, kc, T(t)],
                    start=(kc == 0), stop=(kc == KC - 1)), [kwuv[fc // 2], kH[kc][t]], [kPS[pb]])
            P.add("act", lambda e, fc=fc, t=t, pb=pb: e.activation(out=gu_t[:, fc, T(t)], in_=PS[pb][:, :],
                                                                   func=AF.Gelu_apprx_tanh), [kPS[pb]], [kgu[fc][t]])
    class RecB:
        def __init__(self):
            self.calls = []

        def add(self, *a, **k):
            self.calls.append((a, k))

    def gm_block(P, blk):
        b = blk % NTH
        t = blk // 4
        bs = slice(blk * 128, (blk + 1) * 128)
        pv_, pm_ = b, 4 + b
        vsq, vt = vsq_[b], vg[b]
        mt = vsq_[b][:, :].rearrange("p (f i) -> p f i", f=4)
        kvsq, kvt, kmt = kvsq_[b], kvg[b], kvsq_[b]
        for kc in range(KC):
            P.add("pe", lambda e, kc=kc, bs=bs, pv_=pv_: e.matmul(
                PS[pv_][:, :], lhsT=hT[:, kc, bs], rhs=wuv_bf[:, kc, 512:1024],
                start=(kc == 0), stop=(kc == KC - 1)), [kwuv[2], kwuv[3], kH[kc][t]], [kPS[pv_]])
        P.add("act", lambda e, b=b, pv_=pv_: e.activation(out=vg[b][:, :], in_=PS[pv_][:, :], func=AF.Gelu_apprx_tanh),
              [kPS[pv_]], [kvg[b]])
        s = st[b]
        P.add("dve", lambda e, b=b, s=s: e.reduce_sum(out=s[:, 0:1], in_=vg[b][:, :], axis=AX.X), [kvg[b]], [kst[b]])
        P.add("act", lambda e, b=b, vsq=vsq: e.activation(out=vsq[:, :], in_=vg[b][:, :], func=AF.Square), [kvg[b]], [kvsq])
        P.add("dve", lambda e, s=s, vsq=vsq: e.reduce_sum(out=s[:, 1:2], in_=vsq[:, :], axis=AX.X), [kvsq], [kst[b]])
        P.add("dve", lambda e, s=s: e.tensor_scalar(out=s[:, 2:3], in0=s[:, 0:1], scalar1=1.0 / 512, scalar2=None, op0=ALU.mult),
              [kst[b]], [kst[b]])
        P.add("dve", lambda e, s=s: e.tensor_tensor(out=s[:, 3:4], in0=s[:, 2:3], in1=s[:, 2:3], op=ALU.mult), [kst[b]], [kst[b]])
        P.add("dve", lambda e, s=s: e.scalar_tensor_tensor(out=s[:, 4:5], in0=s[:, 1:2], scalar=1.0 / 512, in1=s[:, 3:4],
                                                           op0=ALU.mult, op1=ALU.subtract), [kst[b]], [kst[b]])
        P.add("act", lambda e, s=s: e.activation(out=s[:, 5:6], in_=s[:, 4:5], func=AF.Sqrt, bias=EPS), [kst[b]], [kst[b]])
        P.add("dve", lambda e, s=s: e.reciprocal(out=s[:, 5:6], in_=s[:, 5:6]), [kst[b]], [kst[b]])
        P.add("dve", lambda e, s=s: e.scalar_tensor_tensor(out=s[:, 6:7], in0=s[:, 2:3], scalar=-1.0, in1=s[:, 5:6],
                                                           op0=ALU.mult, op1=ALU.mult), [kst[b]], [kst[b]])
        P.add("act", lambda e, b=b, s=s: e.activation(out=vg[b][:, :], in_=vg[b][:, :], func=AF.Identity,
                                                      scale=s[:, 5:6], bias=s[:, 6:7]), [kvg[b], kst[b], kvsq], [kvg[b]])
        P.add("pool", lambda e, b=b: e.tensor_tensor(out=vg[b][:, :], in0=vg[b][:, :], in1=lng[:, :], op=ALU.mult), [kvg[b], kln], [kvg[b]])
        P.add("pool", lambda e, b=b: e.tensor_tensor(out=vn[b][:, :], in0=vg[b][:, :], in1=lnb[:, :], op=ALU.add), [kvg[b], kln], [kvn[b]])
        for fc in range(4):
            for hh in range(2):
                g = 2 * fc + hh
                P.add("pe", lambda e, fc=fc, hh=hh, g=g, b=b, pm_=pm_: e.matmul(
                    PS[pm_][hh * 64:(hh + 1) * 64, fc * 128:(fc + 1) * 128], lhsT=vn[b][:, g * 64:(g + 1) * 64],
                    rhs=wsTb[:, g, :], start=True, stop=True), [kvn[b], kws], [kPS[pm_]])
        P.add("dve", lambda e, pm_=pm_, mt=mt: e.tensor_tensor(out=mt, in0=PS[pm_][:, :].rearrange("p (f i) -> p f i", f=4),
                                                               in1=bsT[:, :, :], op=ALU.add), [kPS[pm_], kbs], [kmt])
        P.add("dve", lambda e, bs=bs, mt=mt: e.tensor_tensor(out=ya[:, :, bs], in0=mt, in1=gu_t[:, :, bs], op=ALU.mult),
              [kmt] + [kgu[fc][t] for fc in range(4)], [kYa[blk]])

    rr_ = [RecB() for _ in range(NTH)]
    for k in range(NTH):
        for m in range(16 // NTH):
            gm_block(rr_[k], NTH * m + k)
    nblk = len(rr_[0].calls) // (16 // NTH)
    posb = [0] * NTH
    nb = [len(r.calls) for r in rr_]
    while any(posb[k] < nb[k] for k in range(NTH)):
        best, bf = None, None
        for k in range(NTH):
            if posb[k] < nb[k]:
                f = posb[k] + k * nblk // NTH
                if bf is None or f < bf:
                    best, bf = k, f
        c0 = rr_[best].calls[posb[best]]
        P.add(*c0[0], **c0[1])
        posb[best] += 1
    if stage == 2:
        P.barrier()
        tmpf = ar.alloc([128, 4, 512], F32, "tmpf")
        ktmp = Tok()
        for t in range(4):
            P.add("pool", lambda e, t=t: e.tensor_copy(out=tmpf[:, :, :], in_=ya[:, :, T(t)]), kYa[4 * t:4 * t + 4], [ktmp])
            P.add("sp", lambda e, t=t: e.dma_start(out=yT.rearrange("(kc p) t -> p kc t", p=128)[:, 0:4, T(t)], in_=tmpf[:, :, :]),
                  [ktmp], [Tok()], dma=True, ch="out")
        return nc, P.emit()
    ar.reset(mB)
    P.barrier()

    mC = ar.mark()
    ar.reset(mark_noH)
    wdn_bf = ar.alloc([128, KC, 768], BF16, "wdn")
    mC1 = ar.mark()
    stgC = [ar.alloc([128, KC, 256], F32, "stgC") for _ in range(2)]
    kwdn = [Tok() for _ in range(3)]
    kstgC = [Tok(), Tok()]
    wdnv = w_dn.rearrange("(kc p) f -> p kc f", p=128)
    for q in range(3):
        b = q % 2
        P.add("sp", lambda e, q=q, b=b: e.dma_start(out=stgC[b][:, :, :], in_=wdnv[:, :, q * 256:(q + 1) * 256]),
              [], [kstgC[b]], dma=True, ch="wC%d" % b)
        P.add("act", lambda e, q=q, b=b: e.activation(out=wdn_bf[:, :, q * 256:(q + 1) * 256], in_=stgC[b][:, :, :], func=AF.Copy),
              [kstgC[b]], [kwdn[q]])
    P.barrier()
    ar.reset(mC1)
    h2 = [ar.alloc([128, KC, 512], BF16, "h2") for _ in range(2)]
    pre = [ar.alloc([128, 515], F32, "pre") for _ in range(3)]
    cacc = [ar.alloc([128, 512], F32, "cacc") for _ in range(3)]
    q_raw = ar.alloc([128, 512], F32, "q_raw")
    k_raw = ar.alloc([128, 512], F32, "k_raw")
    B_bc = ar.alloc([128, 512], F32, "B_bc")
    g_bc = ar.alloc([128, 512], F32, "g_bc")
    G_bc = ar.alloc([128, 512], F32, "G_bc")
    pk = g_bc
    mscan = ar.alloc([128, 512], F32, "mscan")
    negm = ar.alloc([64, 64], F32, "negm")
    q_bf = ar.alloc([128, 512], BF16, "q_bf")
    DT = ar.alloc([64, 8, 64], F32, "DT")
    expGt = ar.alloc([64, 8], F32, "expGt")
    k_bf = [ar.alloc([128, 512], BF16, "k_bf") for _ in range(2)]
    v_bf = [ar.alloc([128, 512], BF16, "v_bf") for _ in range(2)]
    QZ0 = [ar.alloc([64, 8, 128], BF16, "QZ0") for _ in range(2)]
    P0 = [ar.alloc([64, 8, 64], BF16, "P0") for _ in range(2)]
    tokS = [ar.alloc([64, 8, 2], F32, "tokS") for _ in range(2)]
    deckt = [ar.alloc([64, 8], F32, "deckt") for _ in range(2)]
    bexpt = [ar.alloc([64, 8], F32, "bexpt") for _ in range(2)]
    QZ = [ar.alloc([64, 8, 128], BF16, "QZ") for _ in range(2)]
    Pm = [ar.alloc([64, 8, 64], BF16, "Pm") for _ in range(2)]
    kbg = [ar.alloc([64, 8, 128], BF16, "kbg") for _ in range(2)]
    vb = [ar.alloc([64, 8, 128], BF16, "vb") for _ in range(2)]
    Zfin = [ar.alloc([64, 8, 64], BF16, "Zfin") for _ in range(2)]
    kZfin = [[Tok(), Tok()], [Tok(), Tok()]]
    qd = [ar.alloc([128, 512], BF16, "qd") for _ in range(3)]
    expG = ar.alloc([128, 512], F32, "expG")
    kexpG = Tok()
    zg = [ar.alloc([128, 512], F32, "zg") for _ in range(3)]
    attnT = [ar.alloc([64, 8, 64], BF16, "attnT") for _ in range(3)]
    eGl = [ar.alloc([128, 8], F32, "eGl") for _ in range(3)]
    kd = [ar.alloc([64, 8, 128], BF16, "kd") for _ in range(2)]
    uc = [ar.alloc([64, 8, 128], BF16, "uc") for _ in range(2)]
    wc_tok = ar.alloc([64, 8, 128], BF16, "wc_tok")
    negMT = [ar.alloc([128, 8, 128], BF16, "negMT") for _ in range(2)]
    kwc_tok = Tok()
    knegMT = [Tok(), Tok()]
    wcT = [ar.alloc([128, 512], BF16, "wcT") for _ in range(2)]
    vnew = ar.alloc([64, 8, 128], BF16, "vnew")
    S = [ar.alloc([128, 128], F32, "S") for _ in range(2)]
    S_bf = [[ar.alloc([128, 128], BF16, "S_bf") for _ in range(9)] for _ in range(2)]
    kS_bf = [[Tok() for _ in range(9)] for _ in range(2)]
    osb = ar.alloc([128, 512], F32, "osb")
    ybf0 = ar.alloc([128, 512], BF16, "ybf")
    ybf = [ybf0, ybf0]

    kh2 = [Tok(), Tok()]
    kpre = [Tok() for _ in range(3)]
    kcacc = [Tok() for _ in range(3)]
    kq_raw, kk_raw, kB, kg, kG = (Tok() for _ in range(5))
    kpk = kg
    kmask, kq_bf, kDT, kexpGt = (Tok() for _ in range(4))
    kk_bf, kv_bf, kP0, ktokS, kdeckt, kbexpt = ([Tok(), Tok()] for _ in range(6))
    kQZ0 = [[Tok(), Tok()], [Tok(), Tok()]]
    kP0h = [[Tok(), Tok()], [Tok(), Tok()]]
    kQZ = [[Tok(), Tok()], [Tok(), Tok()]]
    kPmh = [[Tok(), Tok()], [Tok(), Tok()]]
    kkbg, kvb = [Tok(), Tok()], [Tok(), Tok()]
    kqd, kzg, kattn, keGl = ([Tok(), Tok(), Tok()] for _ in range(4))
    kkd, kuc, kwcT = ([Tok(), Tok()] for _ in range(3))
    kvnew = [Tok() for _ in range(8)]
    kS = [Tok(), Tok()]
    kosb = Tok()
    kybf0 = Tok()
    kybf = [kybf0, kybf0]
    kyb_in = [Tok() for _ in range(16)]
    kyb_out = [Tok() for _ in range(16)]
    kPS5h = [kPS[5], kPS[5]]

    P.add("pool", lambda e: e.memset(mscan[:, :], 1.0), [], [kmask])
    P.add("pool", lambda e: e.memset(mscan[:, :].rearrange("p (c k) -> p c k", k=64)[:, :, 0:1], 0.0), [kmask], [kmask])
    P.add("pool", lambda e: e.memset(negm[:, :], 0.0), [kmask], [kmask])
    P.add("pool", lambda e: e.affine_select(out=negm[:, :], in_=negm[:, :], pattern=[[1, 64]], compare_op=ALU.is_ge,
                                            fill=NEG, base=0, channel_multiplier=-1), [kmask], [kmask])
    P.add("pool", lambda e: e.memset(S[0][:, :], 0.0), [], [kS[0]])
    P.add("pool", lambda e: e.memset(S_bf[1][8][:, :], 0.0), [], [kS_bf[1][8]])
    for i in range(3):
        P.add("pool", lambda e, i=i: e.memset(pre[i][:, 0:3], 0.0), [], [kpre[i]])
    for p2 in range(2):
        P.add("pool", lambda e, p2=p2: e.tensor_copy(out=QZ0[p2][:, :, 64:128],
                                                     in_=identb[0:64, 0:64].unsqueeze(1).to_broadcast([64, 8, 64])),
              [kconst], [kQZ0[p2][0], kQZ0[p2][1]])

    class Rec:
        def __init__(self):
            self.calls = []

        def add(self, *a, **k):
            self.calls.append((a, k))

    def replay(recs):
        recs = [r for r in recs if r.calls]
        units = []
        for r in recs:
            u, curu = [], []
            for call in r.calls:
                is_pe = (call[0][0] == "pe")
                if curu and (is_pe != curu_pe or len(curu) >= 8):
                    u.append(curu)
                    curu = []
                if not is_pe and curu:
                    u.append(curu)
                    curu = []
                curu.append(call)
                curu_pe = is_pe
            if curu:
                u.append(curu)
            units.append(u)
        n = [sum(len(x) for x in u) for u in units]
        done = [0] * len(recs)
        pos = [0] * len(recs)
        while True:
            best, bf = None, None
            for i in range(len(recs)):
                if pos[i] < len(units[i]):
                    f = done[i] / n[i]
                    if bf is None or f < bf:
                        best, bf = i, f
            if best is None:
                break
            for a, k in units[best][pos[best]]:
                P.add(*a, **k)
            done[best] += len(units[best][pos[best]])
            pos[best] += 1

    NSEG = TS // 512
    if stage == 3:
        NSEG = 4
    conv_cols = [PV_CQ, PV_CK, PV_CV]

    def load_h2(R, s):
        b = s % 2
        r = s // 4
        for half in range(2):
            pc = 2 * (s % 4) + half
            src = hg_out[pc, r].rearrange("kc p t -> p kc t")
            R.add("sp", lambda e, half=half, src=src: e.dma_start(out=h2[b][:, :, half * 256:(half + 1) * 256], in_=src),
                  [khg_out[pc]], [kh2[b]], dma=True, ch="h2%d" % b)

    def rsqrt_act(R, dst, kdst, pb, scale):
        R.add("act", lambda e: e.activation(out=dst[:, :], in_=PS[pb][:, :], func=AF.Ln, scale=scale, bias=EPS), [kPS[pb]], [kdst])
        R.add("act", lambda e: e.activation(out=dst[:, :], in_=dst[:, :], func=AF.Exp, scale=-0.5), [kdst], [kdst])

    def thread_A1(R, s):
        hb = s % 2
        p2 = s % 2
        p3 = s % 3
        if s + 1 < NSEG:
            load_h2(R, s + 1)

        def proj(c6, pb):
            for kc in range(KC):
                R.add("pe", lambda e, kc=kc: e.matmul(
                    PS[pb][:, :], lhsT=wdn_bf[:, kc, c6 * 128:(c6 + 1) * 128], rhs=h2[hb][:, kc, :],
                    start=(kc == 0), stop=(kc == KC - 1)), [kwdn[c6 // 2], kh2[hb]], [kPS[pb]])
        proj(0, 0)
        proj(1, 1)
        proj(2, 2)
        for i in range(3):
            R.add("act", lambda e, i=i: e.activation(out=pre[i][:, 3:515], in_=PS[i][:, :], func=AF.Copy), [kPS[i]], [kpre[i]])
        proj(3, 0)
        proj(4, 1)
        proj(5, 2)
        R.add("act", lambda e: e.activation(out=zg[p3][:, :], in_=PS[0][:, :], func=AF.Copy), [kPS[0]], [kzg[p3]])
        R.add("act", lambda e: e.activation(out=B_bc[:, :], in_=PS[1][:, :], func=AF.Sigmoid), [kPS[1]], [kB])
        R.add("act", lambda e: e.activation(out=g_bc[:, :], in_=PS[2][:, :], func=AF.Exp, bias=pvec[:, PV_DTB:PV_DTB + 1]),
              [kPS[2], kpv], [kg])
        R.add("act", lambda e: e.activation(out=g_bc[:, :], in_=g_bc[:, :], func=AF.Ln, bias=1.0), [kg], [kg])
        R.add("dve", lambda e: e.tensor_scalar(out=g_bc[:, :], in0=g_bc[:, :], scalar1=negA[:, 0:1], scalar2=None, op0=ALU.mult),
              [kg, knegA], [kg])
        R.add("dve", lambda e: e.tensor_tensor_scan(out=G_bc[:, :], data0=mscan[:, :], data1=g_bc[:, :], initial=0.0,
                                                    op0=ALU.mult, op1=ALU.add), [kg, kmask], [kG])
        R.add("act", lambda e: e.activation(out=expG[:, :], in_=G_bc[:, :], func=AF.Exp), [kG], [kexpG])
        R.add("act", lambda e: e.activation(out=eGl[p3][:, :], in_=G_bc[:, :].rearrange("p (c k) -> p c k", k=64)[:, :, 63],
                                            func=AF.Exp), [kG], [keGl[p3]])
        for i in range(3):
            cc = conv_cols[i]
            R.add("dve", lambda e, i=i, cc=cc: e.tensor_scalar(out=cacc[i][:, :], in0=pre[i][:, 0:512], scalar1=pvec[:, cc:cc + 1],
                                                               scalar2=None, op0=ALU.mult), [kpre[i], kpv], [kcacc[i]])
            for tap in range(1, 4):
                R.add("dve", lambda e, i=i, cc=cc, tap=tap: e.scalar_tensor_tensor(
                    out=cacc[i][:, :], in0=pre[i][:, tap:tap + 512], scalar=pvec[:, cc + tap:cc + tap + 1], in1=cacc[i][:, :],
                    op0=ALU.mult, op1=ALU.add), [kpre[i], kcacc[i], kpv], [kcacc[i]])
            R.add("dve", lambda e, i=i: e.tensor_copy(out=pre[i][:, 0:3], in_=pre[i][:, 512:515]), [kpre[i]], [kpre[i]])
        R.add("act", lambda e: e.activation(out=q_raw[:, :], in_=cacc[0][:, :], func=AF.Silu), [kcacc[0]], [kq_raw])
        R.add("act", lambda e: e.activation(out=k_raw[:, :], in_=cacc[1][:, :], func=AF.Silu), [kcacc[1]], [kk_raw])
        R.add("act", lambda e: e.activation(out=v_bf[p2][:, :], in_=cacc[2][:, :], func=AF.Silu), [kcacc[2]], [kv_bf[p2]])
        R.add("act", lambda e: e.activation(out=zg[p3][:, :], in_=zg[p3][:, :], func=AF.Silu), [kzg[p3]], [kzg[p3]])
        R.add("act", lambda e: e.activation(out=pk[0:64, :], in_=G_bc[0:64, :], func=AF.Copy), [kG, kg], [kpk])
        R.add("act", lambda e: e.activation(out=pk[64:128, :], in_=B_bc[64:128, :], func=AF.Copy), [kB, kg], [kpk])
        for c in range(8):
            pb = c // 4
            R.add("pe", lambda e, c=c, pb=pb: e.transpose(out=PS[pb][0:64, (c % 4) * 128:(c % 4 + 1) * 128],
                                                          in_=pk[:, c * 64:(c + 1) * 64], identity=identf[:, :]),
                  [kpk, kconst], [kPS[pb]])
        tS = tokS[p2]
        for hh in range(2):
            R.add("act", lambda e, hh=hh: e.activation(
                out=tS[:, 4 * hh:4 * hh + 4, :],
                in_=PS[hh][0:64, :].rearrange("p (c two k) -> p c two k", c=4, two=2)[:, :, :, 0],
                func=AF.Copy), [kPS[hh]], [ktokS[p2]])
        Gt = tS[:, :, 0]
        Bt = tS[:, :, 1]
        Glast64 = G_bc[0:64, :].rearrange("p (c k) -> p c k", k=64)[:, :, 63]
        R.add("act", lambda e: e.activation(out=expGt[:, :], in_=Gt, func=AF.Exp), [ktokS[p2]], [kexpGt])
        R.add("dve", lambda e: e.tensor_tensor(out=deckt[p2][:, :], in0=Glast64, in1=Gt, op=ALU.subtract), [kG, ktokS[p2]], [kdeckt[p2]])
        R.add("act", lambda e: e.activation(out=deckt[p2][:, :], in_=deckt[p2][:, :], func=AF.Exp), [kdeckt[p2]], [kdeckt[p2]])
        R.add("dve", lambda e: e.tensor_tensor(out=bexpt[p2][:, :], in0=Bt, in1=expGt[:, :], op=ALU.mult), [ktokS[p2], kexpGt], [kbexpt[p2]])
        G3 = G_bc[0:64, :].rearrange("p (c k) -> p c k", k=64)
        R.add("dve", lambda e: e.tensor_tensor(out=DT[:, :, :], in0=G3, in1=tS[:, :, 0:1].to_broadcast([64, 8, 64]),
                                               op=ALU.subtract), [kG, ktokS[p2]], [kDT])
        R.add("dve", lambda e: e.tensor_tensor(out=DT[:, :, :], in0=DT[:, :, :],
                                               in1=negm[:, :].unsqueeze(1).to_broadcast([64, 8, 64]), op=ALU.add),
              [kDT, kmask], [kDT])
        R.add("act", lambda e: e.activation(out=DT[:, :, :], in_=DT[:, :, :], func=AF.Exp), [kDT], [kDT])
        R.add("act", lambda e: e.activation(out=sq[0][:, :], in_=q_raw[:, :], func=AF.Square), [kq_raw], [ksq[0]])
        R.add("pe", lambda e: e.matmul(PS[0][:, :], lhsT=ones_bf[:, :], rhs=sq[0][:, :], start=True, stop=True),
              [ksq[0], kconst], [kPS[0]])
        R.add("act", lambda e: e.activation(out=sq[0][:, :], in_=k_raw[:, :], func=AF.Square), [kk_raw], [ksq[0]])
        R.add("pe", lambda e: e.matmul(PS[1][:, :], lhsT=ones_bf[:, :], rhs=sq[0][:, :], start=True, stop=True),
              [ksq[0], kconst], [kPS[1]])
        rsqrt_act(R, cacc[0], kcacc[0], 0, 1.0)
        rsqrt_act(R, cacc[1], kcacc[1], 1, 1.0)
        R.add("dve", lambda e: e.scalar_tensor_tensor(out=q_raw[:, :], in0=q_raw[:, :], scalar=float(128 ** -0.5), in1=cacc[0][:, :],
                                                      op0=ALU.mult, op1=ALU.mult), [kq_raw, kcacc[0]], [kq_raw])
        R.add("dve", lambda e: e.tensor_tensor(out=k_bf[p2][:, :], in0=k_raw[:, :], in1=cacc[1][:, :], op=ALU.mult),
              [kk_raw, kcacc[1]], [kk_bf[p2]])
        R.add("act", lambda e: e.activation(out=q_bf[:, :], in_=q_raw[:, :], func=AF.Copy), [kq_raw], [kq_bf])
        R.add("dve", lambda e: e.tensor_tensor(out=qd[p3][:, :], in0=expG[:, :], in1=q_raw[:, :], op=ALU.mult),
              [kq_raw, kexpG], [kqd[p3]])
        for c in range(8):
            cs = slice(c * 64, (c + 1) * 64)
            R.add("pe", lambda e, cs=cs: e.matmul(PS[2][0:64, cs], lhsT=k_bf[p2][:, cs], rhs=k_bf[p2][:, cs], start=True, stop=True),
                  [kk_bf[p2]], [kPS[2]])
        for c in range(8):
            cs = slice(c * 64, (c + 1) * 64)
            R.add("pe", lambda e, cs=cs: e.matmul(PS[0][0:64, cs], lhsT=k_bf[p2][:, cs], rhs=q_bf[:, cs], start=True, stop=True),
                  [kk_bf[p2], kq_bf], [kPS[0]])
        R.add("dve", lambda e: e.tensor_tensor(out=attnT[p3][:, :, :], in0=PS[0][0:64, :].rearrange("p (c k) -> p c k", k=64),
                                               in1=DT[:, :, :], op=ALU.mult), [kPS[0], kDT], [kattn[p3]])
        R.add("dve", lambda e: e.tensor_tensor(out=DT[:, :, :], in0=DT[:, :, :],
                                               in1=identf[0:64, 0:64].unsqueeze(1).to_broadcast([64, 8, 64]), op=ALU.subtract),
              [kDT, kconst], [kDT])
        R.add("dve", lambda e: e.tensor_tensor(out=DT[:, :, :], in0=DT[:, :, :],
                                               in1=B_bc[0:64, :].rearrange("p (c k) -> p c k", k=64), op=ALU.mult),
              [kDT, kB], [kDT])
        R.add("dve", lambda e: e.scalar_tensor_tensor(out=QZ0[p2][:, :, 0:64], in0=PS[2][0:64, :].rearrange("p (c k) -> p c k", k=64),
                                                      scalar=-1.0, in1=DT[:, :, :], op0=ALU.mult, op1=ALU.mult),
              [kPS[2], kDT], [kQZ0[p2][0], kQZ0[p2][1]])
        psT = PS[1][0:64, 0:256].bitcast(BF16)
        for c in range(8):
            R.add("pe", lambda e, c=c: e.transpose(out=psT[:, c * 64:(c + 1) * 64], in_=QZ0[p2][:, c, 0:64],
                                                   identity=identb[0:64, 0:64]), [kQZ0[p2][c // 4], kconst], [kPS[1]])
        R.add("act", lambda e: e.activation(out=P0[p2][:, :, :], in_=psT.rearrange("p (c k) -> p c k", k=64), func=AF.Copy),
              [kPS[1]], [kP0h[p2][0], kP0h[p2][1]])

    def thread_A2(R, s):
        p2 = s % 2
        psK = PS[3][0:64, :].bitcast(BF16)
        psV = PS[4][0:64, :].bitcast(BF16)
        for c in range(8):
            cs = slice(c * 64, (c + 1) * 64)
            R.add("pe", lambda e, c=c, cs=cs: e.transpose(out=psK[:, c * 128:(c + 1) * 128], in_=k_bf[p2][:, cs], identity=identb[:, :]),
                  [kk_bf[p2], kconst], [kPS[3]])
        for c in range(8):
            cs = slice(c * 64, (c + 1) * 64)
            R.add("pe", lambda e, c=c, cs=cs: e.transpose(out=psV[:, c * 128:(c + 1) * 128], in_=v_bf[p2][:, cs], identity=identb[:, :]),
                  [kv_bf[p2], kconst], [kPS[4]])
        psK3 = psK.rearrange("p (c k) -> p c k", k=128)
        psV3 = psV.rearrange("p (c k) -> p c k", k=128)
        R.add("dve", lambda e: e.tensor_tensor(out=kbg[p2][:, :, :], in0=psK3, in1=bexpt[p2][:, :].unsqueeze(2).to_broadcast([64, 8, 128]),
                                               op=ALU.mult), [kPS[3], kbexpt[p2]], [kkbg[p2]])
        R.add("dve", lambda e: e.tensor_tensor(out=kd[p2][:, :, :], in0=psK3, in1=deckt[p2][:, :].unsqueeze(2).to_broadcast([64, 8, 128]),
                                               op=ALU.mult), [kPS[3], kdeckt[p2]], [kkd[p2]])
        R.add("dve", lambda e: e.tensor_tensor(out=vb[p2][:, :, :], in0=psV3, in1=tokS[p2][:, :, 1:2].to_broadcast([64, 8, 128]),
                                               op=ALU.mult), [kPS[4], ktokS[p2]], [kvb[p2]])
        for lv in range(6):
            last = (lv == 5)
            if lv == 0:
                srcQZ, ksQZ, srcP, ksP = QZ0[p2], kQZ0[p2], P0[p2], kP0h[p2]
            else:
                srcQZ, ksQZ, srcP, ksP = QZ[(lv - 1) % 2], kQZ[(lv - 1) % 2], Pm[(lv - 1) % 2], kPmh[(lv - 1) % 2]
            dstQZ, kdQZ, dstP, kdP = QZ[lv % 2], kQZ[lv % 2], Pm[lv % 2], kPmh[lv % 2]
            for hh in range(2):
                for c in range(4 * hh, 4 * hh + 4):
                    R.add("pe", lambda e, c=c, hh=hh, srcP=srcP, srcQZ=srcQZ: e.matmul(
                        PS[3 + hh][0:64, (c % 4) * 128:(c % 4 + 1) * 128], lhsT=srcP[:, c, :], rhs=srcQZ[:, c, :],
                        start=True, stop=True), [ksP[hh], ksQZ[hh]], [kPS[3 + hh]])
                if not last:
                    for c in range(4 * hh, 4 * hh + 4):
                        R.add("pe", lambda e, c=c, srcP=srcP, srcQZ=srcQZ: e.matmul(
                            PS[5][0:64, c * 64:(c + 1) * 64], lhsT=srcQZ[:, c, 0:64], rhs=srcP[:, c, :],
                            start=True, stop=True), [ksP[hh], ksQZ[hh]], [kPS5h[hh]])
            for hh in range(2):
                psv = PS[3 + hh][0:64, :].rearrange("p (c k) -> p c k", k=128)
                hs = slice(4 * hh, 4 * hh + 4)
                if not last:
                    R.add("act", lambda e, psv=psv, hs=hs, dstQZ=dstQZ: e.activation(
                        out=dstQZ[:, hs, 0:64], in_=psv[:, :, 0:64], func=AF.Copy), [kPS[3 + hh]], [kdQZ[hh]])
                if last:
                    R.add("dve", lambda e, psv=psv, hs=hs, srcQZ=srcQZ: e.tensor_tensor(
                        out=Zfin[p2][:, hs, :], in0=psv[:, :, 64:128], in1=srcQZ[:, hs, 64:128],
                        op=ALU.add), [kPS[3 + hh], ksQZ[hh]], [kZfin[p2][hh]])
                else:
                    R.add("dve", lambda e, psv=psv, hs=hs, dstQZ=dstQZ, srcQZ=srcQZ: e.tensor_tensor(
                        out=dstQZ[:, hs, 64:128], in0=psv[:, :, 64:128], in1=srcQZ[:, hs, 64:128],
                        op=ALU.add), [kPS[3 + hh], ksQZ[hh]], [kdQZ[hh]])
                if not last:
                    R.add("act", lambda e, hh=hh, hs=hs, dstP=dstP: e.activation(
                        out=dstP[:, hs, :], in_=PS[5][0:64, hh * 256:(hh + 1) * 256].rearrange("p (c k) -> p c k", k=64),
                        func=AF.Copy), [kPS5h[hh]], [kdP[hh]])

    def thread_B(R, s, cur):
        p2 = s % 2
        p3 = s % 3
        sp = s % 2

        def sbf(k):
            return (S_bf[1 - sp][8], kS_bf[1 - sp][8]) if k == 0 else (S_bf[sp][k], kS_bf[sp][k])
        Zf, kZf = Zfin[p2], kZfin[p2]
        for c in range(8):
            pb = 6 + c // 4
            R.add("pe", lambda e, c=c, pb=pb: e.matmul(PS[pb][0:64, (c % 4) * 128:(c % 4 + 1) * 128], lhsT=Zf[:, c, :],
                                                       rhs=vb[p2][:, c, :], start=True, stop=True), [kZf[c // 4], kvb[p2]], [kPS[pb]])
        for hh in range(2):
            R.add("act", lambda e, hh=hh: e.activation(out=uc[p2][:, 4 * hh:4 * hh + 4, :],
                                                       in_=PS[6 + hh][0:64, :].rearrange("p (c k) -> p c k", k=128), func=AF.Copy),
                  [kPS[6 + hh]], [kuc[p2]])
        for c in range(8):
            pb = 6 + c // 4
            R.add("pe", lambda e, c=c, pb=pb: e.matmul(PS[pb][0:64, (c % 4) * 128:(c % 4 + 1) * 128], lhsT=Zf[:, c, :],
                                                       rhs=kbg[p2][:, c, :], start=True, stop=True), [kZf[c // 4], kkbg[p2]], [kPS[pb]])
        for hh in range(2):
            R.add("act", lambda e, hh=hh: e.activation(out=wc_tok[:, 4 * hh:4 * hh + 4, :],
                                                       in_=PS[6 + hh][0:64, :].rearrange("p (c k) -> p c k", k=128), func=AF.Copy),
                  [kPS[6 + hh]], [kwc_tok])
        for c in range(8):
            R.add("pe", lambda e, c=c: e.matmul(PS[6][:, c * 64:(c + 1) * 64], lhsT=kbg[p2][:, c, :], rhs=Zf[:, c, :],
                                                start=True, stop=True), [kZf[c // 4], kkbg[p2]], [kPS[6]])
        R.add("act", lambda e: e.activation(out=wcT[p2][:, :], in_=PS[6][:, :], func=AF.Copy), [kPS[6]], [kwcT[p2]])
        for c in range(8):
            pb = 6 + (c // 4 + 1) % 2
            R.add("pe", lambda e, c=c, pb=pb: e.matmul(PS[pb][:, (c % 4) * 128:(c % 4 + 1) * 128], lhsT=wc_tok[:, c, :],
                                                       rhs=kd[p2][:, c, :], start=True, stop=True), [kwc_tok, kkd[p2]], [kPS[pb]])
        for hh in range(2):
            pb = 6 + (hh + 1) % 2
            R.add("act", lambda e, hh=hh, pb=pb: e.activation(out=negMT[p2][:, 4 * hh:4 * hh + 4, :],
                                                              in_=PS[pb][:, :].rearrange("p (c k) -> p c k", k=128), func=AF.Copy,
                                                              scale=-1.0), [kPS[pb]], [knegMT[p2]])
        for c in range(8):
            nxt = 1 - cur
            s_in, ks_in = sbf(c)
            s_out, ks_out = sbf(c + 1)
            R.add("pe", lambda e, c=c: e.matmul(PS[6][:, 0:128], lhsT=kd[p2][:, c, :], rhs=uc[p2][:, c, :], start=True, stop=False),
                  [kkd[p2], kuc[p2]], [kPS[6]])
            R.add("pe", lambda e, c=c, s_in=s_in: e.matmul(PS[6][:, 0:128], lhsT=negMT[p2][:, c, :], rhs=s_in[:, :], start=False, stop=True),
                  [knegMT[p2], ks_in], [kPS[6]])
            R.add("dve", lambda e, c=c, cur=cur, s_out=s_out: e.scalar_tensor_tensor(
                out=s_out[:, :], in0=S[cur][:, :], scalar=eGl[p3][:, c:c + 1], in1=PS[6][:, 0:128], op0=ALU.mult, op1=ALU.add),
                [kS[cur], keGl[p3], kPS[6]], [ks_out])
            R.add("dve", lambda e, c=c, cur=cur, nxt=nxt: e.scalar_tensor_tensor(
                out=S[nxt][:, :], in0=S[cur][:, :], scalar=eGl[p3][:, c:c + 1], in1=PS[6][:, 0:128], op0=ALU.mult, op1=ALU.add),
                [kS[cur], keGl[p3], kPS[6]], [kS[nxt]])
            cur = nxt
        for hh in range(2):
            for c in range(4 * hh, 4 * hh + 4):
                cs = slice(c * 64, (c + 1) * 64)
                s_in, ks_in = sbf(c)
                R.add("pe", lambda e, c=c, cs=cs, s_in=s_in: e.matmul(PS[6][0:64, (c % 4) * 128:(c % 4 + 1) * 128], lhsT=wcT[p2][:, cs],
                                                                    rhs=s_in[:, :], start=True, stop=True), [kwcT[p2], ks_in], [kPS[6]])
            R.add("dve", lambda e, hh=hh: e.tensor_tensor(out=vnew[:, 4 * hh:4 * hh + 4, :], in0=uc[p2][:, 4 * hh:4 * hh + 4, :],
                                                          in1=PS[6][0:64, :].rearrange("p (c k) -> p c k", k=128), op=ALU.subtract),
                  [kuc[p2], kPS[6]], [kvnew[hh]])
        for c in range(8):
            cs = slice(c * 64, (c + 1) * 64)
            s_in, ks_in = sbf(c)
            R.add("pe", lambda e, cs=cs, s_in=s_in: e.matmul(PS[7][:, cs], lhsT=s_in[:, :], rhs=qd[p3][:, cs], start=True, stop=False),
                  [ks_in, kqd[p3]], [kPS[7]])
            R.add("pe", lambda e, c=c, cs=cs: e.matmul(PS[7][:, cs], lhsT=vnew[:, c, :], rhs=attnT[p3][:, c, :], start=False, stop=True),
                  [kvnew[c // 4], kattn[p3]], [kPS[7]])
        R.add("act", lambda e: e.activation(out=osb[:, :], in_=PS[7][:, :], func=AF.Copy), [kPS[7]], [kosb])
        R.add("act", lambda e: e.activation(out=sq[1][:, :], in_=osb[:, :], func=AF.Square), [kosb], [ksq[1]])
        R.add("pe", lambda e: e.matmul(PS[6][:, :], lhsT=ones_bf[:, :], rhs=sq[1][:, :], start=True, stop=True),
              [ksq[1], kconst], [kPS[6]])
        rsqrt_act(R, rs, krs, 6, 1.0 / 128)
        R.add("dve", lambda e: e.scalar_tensor_tensor(out=osb[:, :], in0=osb[:, :], scalar=pvec[:, PV_DN:PV_DN + 1], in1=rs[:, :],
                                                      op0=ALU.mult, op1=ALU.mult), [kosb, krs, kpv], [kosb])
        yb_ = s % 2
        R.add("dve", lambda e: e.tensor_tensor(out=ybf[yb_][:, :], in0=osb[:, :], in1=zg[p3][:, :], op=ALU.mult),
              [kosb, kzg[p3]], [kybf[yb_]])
        R.add("sp", lambda e: e.dma_start(out=yb_in[s], in_=ybf[yb_][:, :]),
              [kybf[yb_]], [kyb_in[s]], dma=True, ch="yb%d" % yb_)
        R.add("pool", lambda e: e.collective_compute("AllGather", ALU.bypass, replica_groups=groups,
                                                     ins=[yb_in[s].opt()], outs=[yb_out[s].opt()]),
              [kyb_in[s]], [kyb_out[s]], dma=True, ch="ccy%d" % (s % 4), inc=1)
        return cur

    R0 = Rec()
    load_h2(R0, 0)
    thread_A1(R0, 0)
    replay([R0])
    Ra, Rb = Rec(), Rec()
    thread_A2(Ra, 0)
    if NSEG > 1:
        thread_A1(Rb, 1)
    replay([Ra, Rb])
    cur = 0
    for s in range(NSEG):
        RB, RA2, RA1 = Rec(), Rec(), Rec()
        cur = thread_B(RB, s, cur)
        if s + 1 < NSEG:
            thread_A2(RA2, s + 1)
        if s + 2 < NSEG:
            thread_A1(RA1, s + 2)
        replay([RB, RA2, RA1])
    if stage == 3:
        P.barrier()
        for t in range(4):
            P.add("sp", lambda e, t=t: e.dma_start(out=yT.rearrange("(kc p) t -> p kc t", p=128)[:, :, T(t)], in_=X[:, :, T(t)]),
                  [kX[kc][t] for kc in range(KC)], [Tok()], dma=True, ch="out")
        return nc, P.emit()
    ar.reset(mC)
    P.barrier()

    mD = ar.mark()
    cand = [ar.alloc([128, 4, NT], BF16, "cand") for _ in range(2)]
    ysel = ar.alloc([128, 4, NT], BF16, "ysel")
    wo_bf = ar.alloc([128, KC, D], BF16, "wo")
    stgD_ = [ar.alloc([128, KC, 256], F32, "stgDD") for _ in range(2)]
    kcand = [Tok(), Tok()]
    kysel = Tok()
    kwo = [Tok() for _ in range(4)]
    kstgD_ = [Tok(), Tok()]
    wov = w_o.rearrange("(kc p) f -> p kc f", p=128)
    for q in range(4):
        b = q % 2
        P.add("sp", lambda e, q=q, b=b: e.dma_start(out=stgD_[b][:, :, :], in_=wov[:, :, q * 256:(q + 1) * 256]),
              [], [kstgD_[b]], dma=True, ch="wO%d" % b)
        P.add("act", lambda e, q=q, b=b: e.activation(out=wo_bf[:, :, q * 256:(q + 1) * 256], in_=stgD_[b][:, :, :], func=AF.Copy),
              [kstgD_[b]], [kwo[q]])
    kcandq = [[Tok() for _ in range(4)] for _ in range(2)]
    kyselt = [Tok() for _ in range(4)]
    for j in range(4):
        b = j % 2
        for q in range(4):
            P.add("sp", lambda e, j=j, b=b, q=q: e.dma_start(out=cand[b][:, :, q * 512:(q + 1) * 512],
                                                             in_=yb_out[4 * j + q].rearrange("h e t -> e h t")),
                  [kyb_out[4 * j + q]], [kcandq[b][q]], dma=True, ch="cand%d%d" % (b, q % 2))
        for q in range(4):
            if j == 0:
                P.add("dve", lambda e, b=b, q=q: e.tensor_scalar(out=ysel[:, :, T(q)], in0=cand[b][:, :, T(q)],
                                                                 scalar1=pvec[:, PV_SEL:PV_SEL + 1], scalar2=None, op0=ALU.mult),
                      [kcandq[b][q], kpv], [kyselt[q]])
            else:
                P.add("dve", lambda e, j=j, b=b, q=q: e.scalar_tensor_tensor(
                    out=ysel[:, :, T(q)], in0=cand[b][:, :, T(q)], scalar=pvec[:, PV_SEL + j:PV_SEL + j + 1], in1=ysel[:, :, T(q)],
                    op0=ALU.mult, op1=ALU.add), [kcandq[b][q], kyselt[q], kpv], [kyselt[q]])
    for t in range(4):
        for dc in range(KC):
            pb = (t * KC + dc) % 8
            for kc in range(KC):
                if kc < 4:
                    rhs, rt = ya[:, kc, T(t)], kYa[4 * t:4 * t + 4]
                else:
                    rhs, rt = ysel[:, kc - 4, T(t)], [kyselt[t]]
                P.add("pe", lambda e, dc=dc, kc=kc, pb=pb, rhs=rhs: e.matmul(
                    PS[pb][:, :], lhsT=wo_bf[:, kc, dc * 128:(dc + 1) * 128], rhs=rhs, start=(kc == 0), stop=(kc == KC - 1)),
                    [kwo[dc // 2]] + list(rt), [kPS[pb]])
            P.add("dve", lambda e, dc=dc, t=t, pb=pb: e.tensor_tensor(out=X[:, dc, T(t)], in0=PS[pb][:, :], in1=X[:, dc, T(t)],
                                                                      op=ALU.add), [kPS[pb], kX[dc][t]], [kX[dc][t]])
    ar.reset(mD)
    P.barrier()
    ot = [ar.alloc([128, 512], F32, "ot") for _ in range(3)]
    kot = [Tok(), Tok(), Tok()]
    otc = [0]

    def final_tile(t):
        stats_to_rs(t, [(X[:, kc, T(t)], kX[kc][t]) for kc in range(KC)], None, 1.0 / D)
        for kc in range(KC):
            b = otc[0] % 3
            otc[0] += 1
            P.add("dve", lambda e, kc=kc, t=t, b=b: e.scalar_tensor_tensor(
                out=ot[b][:, :], in0=X[:, kc, T(t)], scalar=pvec[:, PV_NF + kc:PV_NF + kc + 1], in1=rs[:, :],
                op0=ALU.mult, op1=ALU.mult), [kX[kc][t], krs, kpv], [kot[b]])
            P.add("sp", lambda e, kc=kc, t=t, b=b: e.dma_start(out=yT[kc * 128:(kc + 1) * 128, T(t)], in_=ot[b][:, :]),
                  [kot[b]], [Tok()], dma=True, ch="out%d" % b)

    ffn(w2g, w2u, w2d, "f2", PV_N2, tail=final_tile)
    return nc, P.emit()


def make_in_maps(x, ffn1_norm, ffn1_w_gate, ffn1_w_up, ffn1_w_down, mix_norm, w_in,
                 gm_ln_g, gm_ln_b, gm_w_s, gm_b_s, dn_conv_w, dn_a_log, dn_dt_bias, dn_norm,
                 w_out, ffn2_norm, ffn2_w_gate, ffn2_w_up, ffn2_w_down, final_norm):
    f = lambda a: np.ascontiguousarray(np.asarray(a, dtype=np.float32))
    x = f(x)
    w_in0 = f(w_in)[0]
    common = {
        "w1g": f(ffn1_w_gate)[0], "w1u": f(ffn1_w_up)[0], "w1d": f(ffn1_w_down)[0],
        "w2g": f(ffn2_w_gate)[0], "w2u": f(ffn2_w_up)[0], "w2d": f(ffn2_w_down)[0],
        "w_uv": np.ascontiguousarray(w_in0[:, 0:1024]),
        "w_o": f(w_out)[0],
        "lng": np.ascontiguousarray(np.broadcast_to(f(gm_ln_g)[0][None, :], (128, 512))),
        "lnb": np.ascontiguousarray(np.broadcast_to(f(gm_ln_b)[0][None, :], (128, 512))),
        "wsT": np.ascontiguousarray(np.transpose(f(gm_w_s)[0], (2, 0, 1))),
    }
    bs = f(gm_b_s)[0]
    bsT = np.empty((128, 4, 128), np.float32)
    for fc in range(4):
        bsT[0:64, fc, :] = bs[2 * fc][None, :]
        bsT[64:128, fc, :] = bs[2 * fc + 1][None, :]
    common["bsT"] = bsT
    cw = f(dn_conv_w)[0]
    in_maps = []
    for c in range(8):
        b, j = c // 4, c % 4
        hd = j
        m = dict(common)
        m["xT"] = np.ascontiguousarray(x[b, j * NT:(j + 1) * NT, :].T)
        cols = []
        for base in (1024, 1536, 2048, 2560):
            cols.append(w_in0[:, base + hd * 128: base + (hd + 1) * 128])
        cols.append(np.repeat(w_in0[:, 3072 + hd:3072 + hd + 1], 128, axis=1))
        cols.append(np.repeat(w_in0[:, 3076 + hd:3076 + hd + 1], 128, axis=1))
        m["w_dn"] = np.ascontiguousarray(np.concatenate(cols, axis=1))
        pv = np.zeros((128, NPV), np.float32)
        for k0, g in ((PV_N1, ffn1_norm), (PV_NM, mix_norm), (PV_N2, ffn2_norm)):
            pv[:, k0:k0 + 8] = f(g)[0].reshape(8, 128).T
        pv[:, PV_NF:PV_NF + 8] = f(final_norm).reshape(8, 128).T
        pv[:, PV_DN] = f(dn_norm)[0]
        for k0, base in ((PV_CQ, 0), (PV_CK, 512), (PV_CV, 1024)):
            pv[:, k0:k0 + 4] = cw[:, base + hd * 128: base + (hd + 1) * 128].T
        pv[:, PV_ALOG] = f(dn_a_log)[0][hd]
        pv[:, PV_DTB] = f(dn_dt_bias)[0][hd]
        pv[:, PV_SEL + j] = 1.0
        m["pvec"] = pv
        in_maps.append(m)
    return in_maps


_CACHE = {}


def kernel(**inputs):
    if "nc" not in _CACHE:
        _CACHE["nc"] = build()[0]
    nc = _CACHE["nc"]
    in_maps = make_in_maps(**inputs)
    res = run_bass_kernel_spmd(nc, in_maps, core_ids=list(range(8)))
    out = np.empty((2, TS, D), np.float32)
    for c in range(8):
        b, j = c // 4, c % 4
        out[b, j * NT:(j + 1) * NT, :] = np.asarray(res.results[c]["yT"]).T
    return out
```

```python
import numpy as np
import ml_dtypes
import concourse.bass as bass
import concourse.mybir as mybir
from concourse.bass_utils import run_bass_kernel_spmd

F32 = mybir.dt.float32
BF16 = mybir.dt.bfloat16
AF = mybir.ActivationFunctionType
ALU = mybir.AluOpType
AX = mybir.AxisListType

NT = 2048
TS = 8192
D = 1024
KC = 8
FF = 2816
NFB = 11
EPS = 1e-6
NEG = -30000.0


class Tok:
    __slots__ = ("name", "w", "r")

    def __init__(self, name=""):
        self.name = name
        self.w = None
        self.r = []


class Ins:
    __slots__ = ("eng", "fn", "dma", "deps", "sig", "ord", "ch", "chval", "n", "inc", "bar")

    def __init__(self, eng, fn, dma, ch):
        self.eng = eng
        self.fn = fn
        self.dma = dma
        self.deps = []
        self.sig = False
        self.ord = 0
        self.ch = ch
        self.chval = 0
        self.n = 0
        self.inc = 16
        self.bar = None


class Prog:
    ENGS = ("pe", "act", "dve", "pool", "sp")

    def __init__(self, nc):
        self.nc = nc
        self.ins = []
        self.ch_last = {}
        self.ch_cnt = {}
        self.last = {e: None for e in self.ENGS}

    def add(self, eng, fn, reads=(), writes=(), dma=False, ch=None, inc=16):
        i = Ins(eng, fn, dma, ch)
        i.inc = inc
        i.n = len(self.ins)
        deps = {}
        for t in reads:
            if t.w is not None:
                deps[t.w.n] = (t.w, "raw")
        for t in writes:
            if t.w is not None:
                deps[t.w.n] = (t.w, "waw")
            lastr = {}
            for r in t.r:
                if r.dma:
                    if r.n not in deps:
                        deps[r.n] = (r, "war")
                else:
                    lastr[r.eng] = r
            for r in lastr.values():
                if r.n not in deps:
                    deps[r.n] = (r, "war")
        if dma:
            prev = self.ch_last.get(ch)
            if prev is not None:
                deps[prev.n] = (prev, "raw")
            self.ch_last[ch] = i
            self.ch_cnt[ch] = self.ch_cnt.get(ch, 0) + inc
            i.chval = self.ch_cnt[ch]
        i.deps = list(deps.values())
        for t in reads:
            t.r.append(i)
        for t in writes:
            t.w = i
            t.r = []
        self.ins.append(i)
        if not dma:
            self.last[eng] = i
        return i

    def barrier(self):
        lastc = {e: self.last[e] for e in self.ENGS if self.last[e] is not None}
        chs = {c: v for c, v in self.ch_cnt.items() if not str(c).startswith("cc")}
        for e in self.ENGS:
            i = Ins(e, None, False, None)
            i.n = len(self.ins)
            i.bar = (lastc, chs)
            self.ins.append(i)

    def emit(self):
        nc = self.nc
        needed = {}
        for i in self.ins:
            lst = []
            if i.bar is not None:
                for e, d in i.bar[0].items():
                    d.sig = True
                needed[i.n] = lst
                continue
            for d, kind in i.deps:
                if d.dma:
                    lst.append(d)
                    continue
                if d.eng == i.eng and not i.dma:
                    if i.eng == "pe":
                        continue
                    if kind == "war":
                        continue
                d.sig = True
                lst.append(d)
            needed[i.n] = lst
        cnt = {e: 0 for e in self.ENGS}
        for i in self.ins:
            if i.bar is None and i.sig and not i.dma:
                cnt[i.eng] += 1
                i.ord = cnt[i.eng]
        esem = {e: nc.alloc_semaphore("sem_" + e) for e in self.ENGS}
        chsem = {c: nc.alloc_semaphore("ch_" + str(c)) for c in self.ch_cnt}
        streams = {e: [i for i in self.ins if i.eng == e] for e in self.ENGS}
        stats = {e: [len(streams[e]), cnt[e], 0] for e in self.ENGS}

        def run(eng_name, eng):
            waited = {}

            def do_wait(key, sem, val):
                if waited.get(key, 0) >= val:
                    return
                eng.wait_ge(sem, val)
                waited[key] = val
                stats[eng_name][2] += 1

            for i in streams[eng_name]:
                if i.bar is not None:
                    for e, d in i.bar[0].items():
                        do_wait(("e", e), esem[e], d.ord)
                    for c, v in i.bar[1].items():
                        do_wait(("c", c), chsem[c], v)
                    continue
                w = {}
                for d in needed[i.n]:
                    if d.dma:
                        key, val, sem = ("c", d.ch), d.chval, chsem[d.ch]
                    else:
                        key, val, sem = ("e", d.eng), d.ord, esem[d.eng]
                    if key not in w or w[key][1] < val:
                        w[key] = (sem, val)
                for key, (sem, val) in w.items():
                    do_wait(key, sem, val)
                bi = i.fn(eng)
                if i.dma:
                    bi.then_inc(chsem[i.ch], i.inc)
                elif i.sig:
                    bi.then_inc(esem[i.eng], 1)
            if eng_name == "sp":
                for c, n in self.ch_cnt.items():
                    do_wait(("c", c), chsem[c], n)
                for e in self.ENGS:
                    if cnt[e]:
                        do_wait(("e", e), esem[e], cnt[e])

        with nc.Block() as block:
            @block.tensor
            def _(e):
                run("pe", e)

            @block.scalar
            def _(e):
                run("act", e)

            @block.vector
            def _(e):
                run("dve", e)

            @block.gpsimd
            def _(e):
                run("pool", e)

            @block.sync
            def _(e):
                run("sp", e)
        return stats


class Arena:
    def __init__(self, nc, lo, hi):
        self.nc = nc
        self.lo = lo
        self.hi = hi
        self.cur = lo
        self.n = 0

    def alloc(self, shape, dtype, name="t"):
        sz = 4 if dtype == F32 else 2
        per = sz
        for s in shape[1:]:
            per *= s
        off = (self.cur + 63) // 64 * 64
        assert off + per <= self.hi, ("SBUF overflow", name, off + per - self.hi)
        self.cur = off + per
        self.n += 1
        return self.nc.alloc_sbuf_tensor_at("%s_%d" % (name, self.n), list(shape), dtype, offset=off)

    def mark(self):
        return self.cur

    def reset(self, m):
        self.cur = m


PV_N1, PV_NM, PV_N2, PV_NF = 0, 8, 16, 24
PV_DN = 32
PV_CQ, PV_CK, PV_CV = 33, 37, 41
PV_ALOG, PV_DTB = 45, 46
PV_SEL = 47
NPV = 52


def build(stage=99, dbg_cols=0):
    nc = bass.Bass("TRN2", target_bir_lowering=False)
    P = Prog(nc)

    def din(name, shape, dt=F32):
        return nc.dram_tensor(name, list(shape), dt, kind="ExternalInput").ap()

    xT = din("xT", [D, NT])
    w1g, w1u, w1d = din("w1g", [D, FF]), din("w1u", [D, FF]), din("w1d", [FF, D])
    w2g, w2u, w2d = din("w2g", [D, FF]), din("w2u", [D, FF]), din("w2d", [FF, D])
    w_uv = din("w_uv", [D, 1024])
    w_dn = din("w_dn", [D, 768])
    w_o = din("w_o", [D, D])
    pvec_d = din("pvec", [128, NPV])
    lng_d, lnb_d = din("lng", [128, 512]), din("lnb", [128, 512])
    wsT_d = din("wsT", [128, 8, 128])
    bsT_d = din("bsT", [128, 4, 128])
    yT = nc.dram_tensor("yT", [D, NT], F32, kind="ExternalOutput").ap()
    dbg = None
    if dbg_cols:
        dbg = nc.dram_tensor("dbg", [128, dbg_cols], F32, kind="ExternalOutput").ap()
    hg_in = nc.dram_tensor("hg_in", [8, KC, 128, 256], BF16).ap()
    hg_out = nc.dram_tensor("hg_out", [8, 4, KC, 128, 256], BF16).ap()
    yb_in = nc.dram_tensor("yb_in", [16, 128, 512], BF16).ap()
    yb_out = nc.dram_tensor("yb_out", [16, 4, 128, 512], BF16).ap()
    groups = [[0, 1, 2, 3], [4, 5, 6, 7]]

    ar = Arena(nc, 16512, 229344)
    X = ar.alloc([128, KC, NT], F32, "X")
    ya = ar.alloc([128, 4, NT], BF16, "ya")
    pvec = ar.alloc([128, NPV], F32, "pvec")
    ones_bf = ar.alloc([128, 128], BF16, "ones")
    identf = ar.alloc([128, 128], F32, "identf")
    identb = ar.alloc([128, 128], BF16, "identb")
    negA = ar.alloc([128, 1], F32, "negA")
    sq = [ar.alloc([128, 512], BF16, "sq") for _ in range(2)]
    rs = ar.alloc([128, 512], F32, "rs")
    mark_noH = ar.mark()
    hT = ar.alloc([128, KC, NT], BF16, "hT")
    PS = [nc.alloc_psum_tensor("ps%d" % i, [128, 512], F32) for i in range(8)]
    kPS = [Tok("ps%d" % i) for i in range(8)]
    kX = [[Tok() for _ in range(4)] for _ in range(KC)]
    kH = [[Tok() for _ in range(4)] for _ in range(KC)]
    kYa = [Tok() for _ in range(16)]
    kpv, kconst, knegA = Tok(), Tok(), Tok()
    ksq = [Tok(), Tok()]
    krs = Tok()
    phase_mark = ar.mark()

    def T(i):
        return slice(i * 512, (i + 1) * 512)

    P.add("sp", lambda e: e.dma_start(out=pvec[:, :], in_=pvec_d), [], [kpv], dma=True, ch="pv")
    for h in range(4):
        P.add("sp", lambda e, h=h: e.dma_start(out=X[:, 2 * h:2 * h + 2, :],
                                               in_=xT.rearrange("(kc p) t -> p kc t", p=128)[:, 2 * h:2 * h + 2, :]),
              [], [kX[2 * h][t] for t in range(4)] + [kX[2 * h + 1][t] for t in range(4)], dma=True, ch="x%d" % h)
    P.add("pool", lambda e: e.memset(ones_bf[:, :], 1.0), [], [kconst])
    P.add("pool", lambda e: e.memset(identf[:, :], 1.0), [kconst], [kconst])
    P.add("pool", lambda e: e.affine_select(out=identf[:, :], in_=identf[:, :], pattern=[[-1, 128]],
                                            compare_op=ALU.is_equal, fill=0.0, base=0, channel_multiplier=1),
          [kconst], [kconst])
    P.add("pool", lambda e: e.tensor_copy(out=identb[:, :], in_=identf[:, :]), [kconst], [kconst])
    P.add("act", lambda e: e.activation(out=negA[:, :], in_=pvec[:, PV_ALOG:PV_ALOG + 1], func=AF.Exp), [kpv], [knegA])
    P.add("dve", lambda e: e.tensor_scalar(out=negA[:, :], in0=negA[:, :], scalar1=-1.0, scalar2=None, op0=ALU.mult),
          [knegA], [knegA])

    def stats_to_rs(t, src_fn, src_toks, scale, nparts=128):
        n = len(src_fn)
        for j, (ap, tk) in enumerate(src_fn):
            b = j % 2
            P.add("act", lambda e, ap=ap, b=b: e.activation(out=sq[b][:, :], in_=ap, func=AF.Square), [tk], [ksq[b]])
            P.add("pe", lambda e, b=b, j=j: e.matmul(PS[7][:, :], lhsT=ones_bf[:, :], rhs=sq[b][:, :],
                                                     start=(j == 0), stop=(j == n - 1)),
                  [ksq[b], kconst], [kPS[7]])
        P.add("act", lambda e: e.activation(out=rs[:, :], in_=PS[7][:, :], func=AF.Ln, scale=scale, bias=EPS),
              [kPS[7]], [krs])
        P.add("act", lambda e: e.activation(out=rs[:, :], in_=rs[:, :], func=AF.Exp, scale=-0.5), [krs], [krs])

    def rmsnorm_to_h(gcol, tiles=range(4)):
        for t in tiles:
            stats_to_rs(t, [(X[:, kc, T(t)], kX[kc][t]) for kc in range(KC)], None, 1.0 / D)
            for kc in range(KC):
                P.add("dve", lambda e, kc=kc, t=t: e.scalar_tensor_tensor(
                    out=hT[:, kc, T(t)], in0=X[:, kc, T(t)], scalar=pvec[:, gcol + kc:gcol + kc + 1], in1=rs[:, :],
                    op0=ALU.mult, op1=ALU.mult), [kX[kc][t], krs, kpv], [kH[kc][t]])

    def dbg_dump(ap, toks, col0, ncols, parts=128):
        if dbg is None:
            return
        P.add("sp", lambda e: e.dma_start(out=dbg[0:parts, col0:col0 + ncols], in_=ap), toks, [Tok()], dma=True, ch="dbg")

    def ffn(wg_d, wu_d, wd_d, tag, gcol, tail=None):
        m = ar.mark()
        stg = [[ar.alloc([128, KC, 256], F32, "stgG") for _ in range(2)],
               [ar.alloc([128, KC, 256], F32, "stgU") for _ in range(2)],
               [ar.alloc([128, 2, 1024], F32, "stgD") for _ in range(2)]]
        wbf = [[ar.alloc([128, KC, 256], BF16, "wg") for _ in range(2)],
               [ar.alloc([128, KC, 256], BF16, "wu") for _ in range(2)],
               [ar.alloc([128, 2, 1024], BF16, "wd") for _ in range(2)]]
        sgt = [ar.alloc([128, 512], F32, "sg") for _ in range(2)]
        at = [ar.alloc([128, 2, 512], BF16, "a") for _ in range(2)]
        kstg = [[Tok(), Tok()] for _ in range(3)]
        kw = [[Tok(), Tok()] for _ in range(3)]
        ksg = [Tok(), Tok()]
        ka = [[Tok(), Tok()] for _ in range(2)]
        wgv = wg_d.rearrange("(kc p) f -> p kc f", p=128)
        wuv = wu_d.rearrange("(kc p) f -> p kc f", p=128)
        wdv = wd_d.rearrange("(fc p) d -> p fc d", p=128)

        def load(F):
            b = F % 2
            fs = slice(F * 256, (F + 1) * 256)
            P.add("sp", lambda e: e.dma_start(out=stg[0][b][:, :, :], in_=wgv[:, :, fs]), [], [kstg[0][b]], dma=True, ch="wG%d" % b)
            P.add("sp", lambda e: e.dma_start(out=stg[1][b][:, :, :], in_=wuv[:, :, fs]), [], [kstg[1][b]], dma=True, ch="wU%d" % b)
            P.add("sp", lambda e: e.dma_start(out=stg[2][b][:, :, :], in_=wdv[:, 2 * F:2 * F + 2, :]), [], [kstg[2][b]], dma=True, ch="wD%d" % b)

        def cast(F):
            b = F % 2
            for k in range(3):
                P.add("act", lambda e, k=k: e.activation(out=wbf[k][b][:, :, :], in_=stg[k][b][:, :, :], func=AF.Copy),
                      [kstg[k][b]], [kw[k][b]])

        def gu(F, t, idx):
            b = F % 2
            for fc in range(2):
                pg, pu = 2 * fc, 2 * fc + 1
                for kc in range(KC):
                    P.add("pe", lambda e, kc=kc, fc=fc, pg=pg: e.matmul(
                        PS[pg][:, :], lhsT=wbf[0][b][:, kc, fc * 128:(fc + 1) * 128], rhs=hT[:, kc, T(t)],
                        start=(kc == 0), stop=(kc == KC - 1)), [kw[0][b], kH[kc][t]], [kPS[pg]])
                for kc in range(KC):
                    P.add("pe", lambda e, kc=kc, fc=fc, pu=pu: e.matmul(
                        PS[pu][:, :], lhsT=wbf[1][b][:, kc, fc * 128:(fc + 1) * 128], rhs=hT[:, kc, T(t)],
                        start=(kc == 0), stop=(kc == KC - 1)), [kw[1][b], kH[kc][t]], [kPS[pu]])
                P.add("act", lambda e, fc=fc, pg=pg: e.activation(out=sgt[fc][:, :], in_=PS[pg][:, :], func=AF.Silu),
                      [kPS[pg]], [ksg[fc]])
                P.add("dve", lambda e, fc=fc, pu=pu: e.tensor_tensor(out=at[idx % 2][:, fc, :], in0=sgt[fc][:, :],
                                                                     in1=PS[pu][:, :], op=ALU.mult),
                      [ksg[fc], kPS[pu]], [ka[idx % 2][fc]])

        def down(F, t, idx):
            b = F % 2
            for dc in range(KC):
                pb = 4 + dc % 4
                for fc in range(2):
                    P.add("pe", lambda e, dc=dc, fc=fc, pb=pb: e.matmul(
                        PS[pb][:, :], lhsT=wbf[2][b][:, fc, dc * 128:(dc + 1) * 128], rhs=at[idx % 2][:, fc, :],
                        start=(fc == 0), stop=(fc == 1)), [kw[2][b], ka[idx % 2][fc]], [kPS[pb]])
                P.add("dve", lambda e, dc=dc, pb=pb: e.scalar_tensor_tensor(
                    out=X[:, dc, T(t)], in0=PS[pb][:, :], scalar=0.5, in1=X[:, dc, T(t)], op0=ALU.mult, op1=ALU.add),
                    [kPS[pb], kX[dc][t]], [kX[dc][t]])

        load(0)
        load(1)
        items = [(F, t) for F in range(NFB) for t in range(4)]
        rmsnorm_to_h(gcol, [0])
        for idx, (F, t) in enumerate(items):
            if t == 0:
                cast(F)
            if F == 0 and t + 1 < 4:
                rmsnorm_to_h(gcol, [t + 1])
            gu(F, t, idx)
            if idx > 0:
                pF, pt = items[idx - 1]
                down(pF, pt, idx - 1)
                if pt == 3 and pF + 2 < NFB:
                    load(pF + 2)
                if tail is not None and pF == NFB - 1:
                    tail(pt)
        pF, pt = items[-1]
        down(pF, pt, len(items) - 1)
        if tail is not None:
            tail(pt)
        ar.reset(m)

    ffn(w1g, w1u, w1d, "f1", PV_N1)
    if stage == 1:
        P.barrier()
        for t in range(4):
            P.add("sp", lambda e, t=t: e.dma_start(out=yT.rearrange("(kc p) t -> p kc t", p=128)[:, :, T(t)], in_=X[:, :, T(t)]),
                  [kX[kc][t] for kc in range(KC)], [Tok()], dma=True, ch="out")
        return nc, P.emit()
    P.barrier()

    mB = ar.mark()
    stgB = [ar.alloc([128, KC, 256], F32, "stgB") for _ in range(2)]
    wuv_bf = ar.alloc([128, KC, 1024], BF16, "wuv")
    gu_t = ar.alloc([128, 4, NT], BF16, "gu")
    lng = ar.alloc([128, 512], F32, "lng")
    lnb = ar.alloc([128, 512], F32, "lnb")
    wsTf = ar.alloc([128, 8, 128], F32, "wsTf")
    wsTb = ar.alloc([128, 8, 128], BF16, "wsTb")
    bsT = ar.alloc([128, 4, 128], F32, "bsT")
    NTH = 4
    vg = [ar.alloc([128, 512], F32, "vg") for _ in range(NTH)]
    vsq_ = [ar.alloc([128, 512], F32, "vsq") for _ in range(NTH)]
    vn = [ar.alloc([128, 512], BF16, "vn") for _ in range(NTH)]
    st = [ar.alloc([128, 8], F32, "st") for _ in range(NTH)]
    kstgB = [Tok(), Tok()]
    kwuv = [Tok() for _ in range(4)]
    kgu = [[Tok() for _ in range(4)] for _ in range(4)]
    kln, kws, kbs = Tok(), Tok(), Tok()
    kvg = [Tok() for _ in range(NTH)]
    kvsq_ = [Tok() for _ in range(NTH)]
    kvn = [Tok() for _ in range(NTH)]
    kst = [Tok() for _ in range(NTH)]
    P.add("sp", lambda e: e.dma_start(out=lng[:, :], in_=lng_d), [], [kln], dma=True, ch="c1")
    P.add("sp", lambda e: e.dma_start(out=lnb[:, :], in_=lnb_d), [], [kln], dma=True, ch="c1")
    P.add("sp", lambda e: e.dma_start(out=wsTf[:, :, :], in_=wsT_d), [], [kws], dma=True, ch="c2")
    P.add("sp", lambda e: e.dma_start(out=bsT[:, :, :], in_=bsT_d), [], [kbs], dma=True, ch="c3")
    P.add("pool", lambda e: e.memset(wsTf[64:128, :, 0:64], 0.0), [kws], [kws])
    P.add("pool", lambda e: e.tensor_copy(out=wsTb[:, :, :], in_=wsTf[:, :, :]), [kws], [kws])
    wuvv = w_uv.rearrange("(kc p) f -> p kc f", p=128)
    for q in range(4):
        b = q % 2
        P.add("sp", lambda e, q=q, b=b: e.dma_start(out=stgB[b][:, :, :], in_=wuvv[:, :, q * 256:(q + 1) * 256]),
              [], [kstgB[b]], dma=True, ch="wB%d" % b)
        P.add("act", lambda e, q=q, b=b: e.activation(out=wuv_bf[:, :, q * 256:(q + 1) * 256], in_=stgB[b][:, :, :], func=AF.Copy),
              [kstgB[b]], [kwuv[q]])
    rmsnorm_to_h(PV_NM)
    khg_in = [Tok() for _ in range(KC)]
    khg_out = [Tok() for _ in range(KC)]
    for pc in range(8):
        P.add("sp", lambda e, pc=pc: e.dma_start(out=hg_in[pc].rearrange("kc p t -> p kc t"), in_=hT[:, :, pc * 256:(pc + 1) * 256]),
              [kH[kc][pc // 2] for kc in range(KC)], [khg_in[pc]], dma=True, ch="hgx%d" % (pc % 2))
        P.add("pool", lambda e, pc=pc: e.collective_compute("AllGather", ALU.bypass, replica_groups=groups,
                                                            ins=[hg_in[pc].opt()], outs=[hg_out[pc].opt()]),
              [khg_in[pc]], [khg_out[pc]], dma=True, ch="cch%d" % pc, inc=1)

    for fc in range(4):
        for t in range(4):
            pb = (fc * 4 + t) % 4
            for kc in range(KC):
                P.add("pe", lambda e, kc=kc, fc=fc, t=t, pb=pb: e.matmul(
                    PS[pb][:, :], lhsT=wuv_bf[:, kc, fc * 128:(fc + 1) * 128], rhs=hT[:, kc, T(t)],
                    start=(kc == 0), stop=(kc == KC - 1)), [kwuv[fc // 2], kH[kc][t]], [kPS[pb]])
            P.add("act", lambda e, fc=fc, t=t, pb=pb: e.activation(out=gu_t[:, fc, T(t)], in_=PS[pb][:, :],
                                                                   func=AF.Gelu_apprx_tanh), [kPS[pb]], [kgu[fc][t]])
    class RecB:
        def __init__(self):
            self.calls = []

        def add(self, *a, **k):
            self.calls.append((a, k))

    def gm_block(P, blk):
        b = blk % NTH
        t = blk // 4
        bs = slice(blk * 128, (blk + 1) * 128)
        pv_, pm_ = b, 4 + b
        vsq, vt = vsq_[b], vg[b]
        mt = vsq_[b][:, :].rearrange("p (f i) -> p f i", f=4)
        kvsq, kvt, kmt = kvsq_[b], kvg[b], kvsq_[b]
        for kc in range(KC):
            P.add("pe", lambda e, kc=kc, bs=bs, pv_=pv_: e.matmul(
                PS[pv_][:, :], lhsT=hT[:, kc, bs], rhs=wuv_bf[:, kc, 512:1024],
                start=(kc == 0), stop=(kc == KC - 1)), [kwuv[2], kwuv[3], kH[kc][t]], [kPS[pv_]])
        P.add("act", lambda e, b=b, pv_=pv_: e.activation(out=vg[b][:, :], in_=PS[pv_][:, :], func=AF.Gelu_apprx_tanh),
              [kPS[pv_]], [kvg[b]])
        s = st[b]
        P.add("dve", lambda e, b=b, s=s: e.reduce_sum(out=s[:, 0:1], in_=vg[b][:, :], axis=AX.X), [kvg[b]], [kst[b]])
        P.add("act", lambda e, b=b, vsq=vsq: e.activation(out=vsq[:, :], in_=vg[b][:, :], func=AF.Square), [kvg[b]], [kvsq])
        P.add("dve", lambda e, s=s, vsq=vsq: e.reduce_sum(out=s[:, 1:2], in_=vsq[:, :], axis=AX.X), [kvsq], [kst[b]])
        P.add("dve", lambda e, s=s: e.tensor_scalar(out=s[:, 2:3], in0=s[:, 0:1], scalar1=1.0 / 512, scalar2=None, op0=ALU.mult),
              [kst[b]], [kst[b]])
        P.add("dve", lambda e, s=s: e.tensor_tensor(out=s[:, 3:4], in0=s[:, 2:3], in1=s[:, 2:3], op=ALU.mult), [kst[b]], [kst[b]])
        P.add("dve", lambda e, s=s: e.scalar_tensor_tensor(out=s[:, 4:5], in0=s[:, 1:2], scalar=1.0 / 512, in1=s[:, 3:4],
                                                           op0=ALU.mult, op1=ALU.subtract), [kst[b]], [kst[b]])
        P.add("act", lambda e, s=s: e.activation(out=s[:, 5:6], in_=s[:, 4:5], func=AF.Sqrt, bias=EPS), [kst[b]], [kst[b]])
        P.add("dve", lambda e, s=s: e.reciprocal(out=s[:, 5:6], in_=s[:, 5:6]), [kst[b]], [kst[b]])
        P.add("dve", lambda e, s=s: e.scalar_tensor_tensor(out=s[:, 6:7], in0=s[:, 2:3], scalar=-1.0, in1=s[:, 5:6],
                                                           op0=ALU.mult, op1=ALU.mult), [kst[b]], [kst[b]])
        P.add("act", lambda e, b=b, s=s: e.activation(out=vg[b][:, :], in_=vg[b][:, :], func=AF.Identity,
                                                      scale=s[:, 5:6], bias=s[:, 6:7]), [kvg[b], kst[b], kvsq], [kvg[b]])
        P.add("pool", lambda e, b=b: e.tensor_tensor(out=vg[b][:, :], in0=vg[b][:, :], in1=lng[:, :], op=ALU.mult), [kvg[b], kln], [kvg[b]])
        P.add("pool", lambda e, b=b: e.tensor_tensor(out=vn[b][:, :], in0=vg[b][:, :], in1=lnb[:, :], op=ALU.add), [kvg[b], kln], [kvn[b]])
        for fc in range(4):
            for hh in range(2):
                g = 2 * fc + hh
                P.add("pe", lambda e, fc=fc, hh=hh, g=g, b=b, pm_=pm_: e.matmul(
                    PS[pm_][hh * 64:(hh + 1) * 64, fc * 128:(fc + 1) * 128], lhsT=vn[b][:, g * 64:(g + 1) * 64],
                    rhs=wsTb[:, g, :], start=True, stop=True), [kvn[b], kws], [kPS[pm_]])
        P.add("dve", lambda e, pm_=pm_, mt=mt: e.tensor_tensor(out=mt, in0=PS[pm_][:, :].rearrange("p (f i) -> p f i", f=4),
                                                               in1=bsT[:, :, :], op=ALU.add), [kPS[pm_], kbs], [kmt])
        P.add("dve", lambda e, bs=bs, mt=mt: e.tensor_tensor(out=ya[:, :, bs], in0=mt, in1=gu_t[:, :, bs], op=ALU.mult),
              [kmt] + [kgu[fc][t] for fc in range(4)], [kYa[blk]])

    rr_ = [RecB() for _ in range(NTH)]
    for k in range(NTH):
        for m in range(16 // NTH):
            gm_block(rr_[k], NTH * m + k)
    nblk = len(rr_[0].calls) // (16 // NTH)
    posb = [0] * NTH
    nb = [len(r.calls) for r in rr_]
    while any(posb[k] < nb[k] for k in range(NTH)):
        best, bf = None, None
        for k in range(NTH):
            if posb[k] < nb[k]:
                f = posb[k] + k * nblk // NTH
                if bf is None or f < bf:
                    best, bf = k, f
        c0 = rr_[best].calls[posb[best]]
        P.add(*c0[0], **c0[1])
        posb[best] += 1
    if stage == 2:
        P.barrier()
        tmpf = ar.alloc([128, 4, 512], F32, "tmpf")
        ktmp = Tok()
        for t in range(4):
            P.add("pool", lambda e, t=t: e.tensor_copy(out=tmpf[:, :, :], in_=ya[:, :, T(t)]), kYa[4 * t:4 * t + 4], [ktmp])
            P.add("sp", lambda e, t=t: e.dma_start(out=yT.rearrange("(kc p) t -> p kc t", p=128)[:, 0:4, T(t)], in_=tmpf[:, :, :]),
                  [ktmp], [Tok()], dma=True, ch="out")
        return nc, P.emit()
    ar.reset(mB)
    P.barrier()

    mC = ar.mark()
    ar.reset(mark_noH)
    wdn_bf = ar.alloc([128, KC, 768], BF16, "wdn")
    mC1 = ar.mark()
    stgC = [ar.alloc([128, KC, 256], F32, "stgC") for _ in range(2)]
    kwdn = [Tok() for _ in range(3)]
    kstgC = [Tok(), Tok()]
    wdnv = w_dn.rearrange("(kc p) f -> p kc f", p=128)
    for q in range(3):
        b = q % 2
        P.add("sp", lambda e, q=q, b=b: e.dma_start(out=stgC[b][:, :, :], in_=wdnv[:, :, q * 256:(q + 1) * 256]),
              [], [kstgC[b]], dma=True, ch="wC%d" % b)
        P.add("act", lambda e, q=q, b=b: e.activation(out=wdn_bf[:, :, q * 256:(q + 1) * 256], in_=stgC[b][:, :, :], func=AF.Copy),
              [kstgC[b]], [kwdn[q]])
    P.barrier()
    ar.reset(mC1)
    h2 = [ar.alloc([128, KC, 512], BF16, "h2") for _ in range(2)]
    pre = [ar.alloc([128, 515], F32, "pre") for _ in range(3)]
    cacc = [ar.alloc([128, 512], F32, "cacc") for _ in range(3)]
    q_raw = ar.alloc([128, 512], F32, "q_raw")
    k_raw = ar.alloc([128, 512], F32, "k_raw")
    B_bc = ar.alloc([128, 512], F32, "B_bc")
    g_bc = ar.alloc([128, 512], F32, "g_bc")
    G_bc = ar.alloc([128, 512], F32, "G_bc")
    pk = g_bc
    mscan = ar.alloc([128, 512], F32, "mscan")
    negm = ar.alloc([64, 64], F32, "negm")
    q_bf = ar.alloc([128, 512], BF16, "q_bf")
    DT = ar.alloc([64, 8, 64], F32, "DT")
    expGt = ar.alloc([64, 8], F32, "expGt")
    k_bf = [ar.alloc([128, 512], BF16, "k_bf") for _ in range(2)]
    v_bf = [ar.alloc([128, 512], BF16, "v_bf") for _ in range(2)]
    QZ0 = [ar.alloc([64, 8, 128], BF16, "QZ0") for _ in range(2)]
    P0 = [ar.alloc([64, 8, 64], BF16, "P0") for _ in range(2)]
    tokS = [ar.alloc([64, 8, 2], F32, "tokS") for _ in range(2)]
    deckt = [ar.alloc([64, 8], F32, "deckt") for _ in range(2)]
    bexpt = [ar.alloc([64, 8], F32, "bexpt") for _ in range(2)]
    QZ = [ar.alloc([64, 8, 128], BF16, "QZ") for _ in range(2)]
    Pm = [ar.alloc([64, 8, 64], BF16, "Pm") for _ in range(2)]
    kbg = ar.alloc([64, 8, 128], BF16, "kbg")
    vb = ar.alloc([64, 8, 128], BF16, "vb")
    qd = [ar.alloc([128, 512], BF16, "qd") for _ in range(3)]
    expG = ar.alloc([128, 512], F32, "expG")
    kexpG = Tok()
    zg = [ar.alloc([128, 512], F32, "zg") for _ in range(3)]
    attnT = [ar.alloc([64, 8, 64], BF16, "attnT") for _ in range(3)]
    eGl = [ar.alloc([128, 8], F32, "eGl") for _ in range(3)]
    kd = [ar.alloc([64, 8, 128], BF16, "kd") for _ in range(2)]
    uc = [ar.alloc([64, 8, 128], F32, "uc") for _ in range(2)]
    wcT = [ar.alloc([128, 512], BF16, "wcT") for _ in range(2)]
    vnew = ar.alloc([64, 8, 128], BF16, "vnew")
    S = [ar.alloc([128, 128], F32, "S") for _ in range(2)]
    S_bf = [ar.alloc([128, 128], BF16, "S_bf") for _ in range(2)]
    kS_bf = [Tok(), Tok()]
    osb = ar.alloc([128, 512], F32, "osb")
    ybf0 = ar.alloc([128, 512], BF16, "ybf")
    ybf = [ybf0, ybf0]

    kh2 = [Tok(), Tok()]
    kpre = [Tok() for _ in range(3)]
    kcacc = [Tok() for _ in range(3)]
    kq_raw, kk_raw, kB, kg, kG = (Tok() for _ in range(5))
    kpk = kg
    kmask, kq_bf, kDT, kexpGt = (Tok() for _ in range(4))
    kk_bf, kv_bf, kP0, ktokS, kdeckt, kbexpt = ([Tok(), Tok()] for _ in range(6))
    kQZ0 = [[Tok(), Tok()], [Tok(), Tok()]]
    kP0h = [[Tok(), Tok()], [Tok(), Tok()]]
    kQZ = [[Tok(), Tok()], [Tok(), Tok()]]
    kPmh = [[Tok(), Tok()], [Tok(), Tok()]]
    kkbg, kvb = Tok(), Tok()
    kqd, kzg, kattn, keGl = ([Tok(), Tok(), Tok()] for _ in range(4))
    kkd, kuc, kwcT = ([Tok(), Tok()] for _ in range(3))
    kvnew = [Tok() for _ in range(8)]
    kS = [Tok(), Tok()]
    kosb = Tok()
    kybf0 = Tok()
    kybf = [kybf0, kybf0]
    kyb_in = [Tok() for _ in range(16)]
    kyb_out = [Tok() for _ in range(16)]
    kPS5h = [kPS[5], kPS[5]]

    P.add("pool", lambda e: e.memset(mscan[:, :], 1.0), [], [kmask])
    P.add("pool", lambda e: e.memset(mscan[:, :].rearrange("p (c k) -> p c k", k=64)[:, :, 0:1], 0.0), [kmask], [kmask])
    P.add("pool", lambda e: e.memset(negm[:, :], 0.0), [kmask], [kmask])
    P.add("pool", lambda e: e.affine_select(out=negm[:, :], in_=negm[:, :], pattern=[[1, 64]], compare_op=ALU.is_ge,
                                            fill=NEG, base=0, channel_multiplier=-1), [kmask], [kmask])
    P.add("pool", lambda e: e.memset(S[0][:, :], 0.0), [], [kS[0]])
    P.add("pool", lambda e: e.memset(S_bf[0][:, :], 0.0), [], [kS_bf[0]])
    for i in range(3):
        P.add("pool", lambda e, i=i: e.memset(pre[i][:, 0:3], 0.0), [], [kpre[i]])
    for p2 in range(2):
        P.add("pool", lambda e, p2=p2: e.tensor_copy(out=QZ0[p2][:, :, 64:128],
                                                     in_=identb[0:64, 0:64].unsqueeze(1).to_broadcast([64, 8, 64])),
              [kconst], [kQZ0[p2][0], kQZ0[p2][1]])

    class Rec:
        def __init__(self):
            self.calls = []

        def add(self, *a, **k):
            self.calls.append((a, k))

    def replay(recs):
        recs = [r for r in recs if r.calls]
        units = []
        for r in recs:
            u, curu = [], []
            for call in r.calls:
                is_pe = (call[0][0] == "pe")
                if curu and (is_pe != curu_pe or len(curu) >= 8):
                    u.append(curu)
                    curu = []
                if not is_pe and curu:
                    u.append(curu)
                    curu = []
                curu.append(call)
                curu_pe = is_pe
            if curu:
                u.append(curu)
            units.append(u)
        n = [sum(len(x) for x in u) for u in units]
        done = [0] * len(recs)
        pos = [0] * len(recs)
        while True:
            best, bf = None, None
            for i in range(len(recs)):
                if pos[i] < len(units[i]):
                    f = done[i] / n[i]
                    if bf is None or f < bf:
                        best, bf = i, f
            if best is None:
                break
            for a, k in units[best][pos[best]]:
                P.add(*a, **k)
            done[best] += len(units[best][pos[best]])
            pos[best] += 1

    NSEG = TS // 512
    if stage == 3:
        NSEG = 4
    conv_cols = [PV_CQ, PV_CK, PV_CV]

    def load_h2(R, s):
        b = s % 2
        r = s // 4
        for half in range(2):
            pc = 2 * (s % 4) + half
            src = hg_out[pc, r].rearrange("kc p t -> p kc t")
            R.add("sp", lambda e, half=half, src=src: e.dma_start(out=h2[b][:, :, half * 256:(half + 1) * 256], in_=src),
                  [khg_out[pc]], [kh2[b]], dma=True, ch="h2%d" % b)

    def rsqrt_act(R, dst, kdst, pb, scale):
        R.add("act", lambda e: e.activation(out=dst[:, :], in_=PS[pb][:, :], func=AF.Ln, scale=scale, bias=EPS), [kPS[pb]], [kdst])
        R.add("act", lambda e: e.activation(out=dst[:, :], in_=dst[:, :], func=AF.Exp, scale=-0.5), [kdst], [kdst])

    def thread_A1(R, s):
        hb = s % 2
        p2 = s % 2
        p3 = s % 3
        if s + 1 < NSEG:
            load_h2(R, s + 1)

        def proj(c6, pb):
            for kc in range(KC):
                R.add("pe", lambda e, kc=kc: e.matmul(
                    PS[pb][:, :], lhsT=wdn_bf[:, kc, c6 * 128:(c6 + 1) * 128], rhs=h2[hb][:, kc, :],
                    start=(kc == 0), stop=(kc == KC - 1)), [kwdn[c6 // 2], kh2[hb]], [kPS[pb]])
        proj(0, 0)
        proj(1, 1)
        proj(2, 2)
        for i in range(3):
            R.add("act", lambda e, i=i: e.activation(out=pre[i][:, 3:515], in_=PS[i][:, :], func=AF.Copy), [kPS[i]], [kpre[i]])
        proj(3, 0)
        proj(4, 1)
        proj(5, 2)
        R.add("act", lambda e: e.activation(out=zg[p3][:, :], in_=PS[0][:, :], func=AF.Copy), [kPS[0]], [kzg[p3]])
        R.add("act", lambda e: e.activation(out=B_bc[:, :], in_=PS[1][:, :], func=AF.Sigmoid), [kPS[1]], [kB])
        R.add("act", lambda e: e.activation(out=g_bc[:, :], in_=PS[2][:, :], func=AF.Exp, bias=pvec[:, PV_DTB:PV_DTB + 1]),
              [kPS[2], kpv], [kg])
        R.add("act", lambda e: e.activation(out=g_bc[:, :], in_=g_bc[:, :], func=AF.Ln, bias=1.0), [kg], [kg])
        R.add("dve", lambda e: e.tensor_scalar(out=g_bc[:, :], in0=g_bc[:, :], scalar1=negA[:, 0:1], scalar2=None, op0=ALU.mult),
              [kg, knegA], [kg])
        R.add("dve", lambda e: e.tensor_tensor_scan(out=G_bc[:, :], data0=mscan[:, :], data1=g_bc[:, :], initial=0.0,
                                                    op0=ALU.mult, op1=ALU.add), [kg, kmask], [kG])
        R.add("act", lambda e: e.activation(out=expG[:, :], in_=G_bc[:, :], func=AF.Exp), [kG], [kexpG])
        R.add("act", lambda e: e.activation(out=eGl[p3][:, :], in_=G_bc[:, :].rearrange("p (c k) -> p c k", k=64)[:, :, 63],
                                            func=AF.Exp), [kG], [keGl[p3]])
        for i in range(3):
            cc = conv_cols[i]
            R.add("dve", lambda e, i=i, cc=cc: e.tensor_scalar(out=cacc[i][:, :], in0=pre[i][:, 0:512], scalar1=pvec[:, cc:cc + 1],
                                                               scalar2=None, op0=ALU.mult), [kpre[i], kpv], [kcacc[i]])
        for tap in range(1, 4):
            for i in range(3):
                cc = conv_cols[i]
                R.add("dve", lambda e, i=i, cc=cc, tap=tap: e.scalar_tensor_tensor(
                    out=cacc[i][:, :], in0=pre[i][:, tap:tap + 512], scalar=pvec[:, cc + tap:cc + tap + 1], in1=cacc[i][:, :],
                    op0=ALU.mult, op1=ALU.add), [kpre[i], kcacc[i], kpv], [kcacc[i]])
        for i in range(3):
            R.add("dve", lambda e, i=i: e.tensor_copy(out=pre[i][:, 0:3], in_=pre[i][:, 512:515]), [kpre[i]], [kpre[i]])
        R.add("act", lambda e: e.activation(out=q_raw[:, :], in_=cacc[0][:, :], func=AF.Silu), [kcacc[0]], [kq_raw])
        R.add("act", lambda e: e.activation(out=k_raw[:, :], in_=cacc[1][:, :], func=AF.Silu), [kcacc[1]], [kk_raw])
        R.add("act", lambda e: e.activation(out=v_bf[p2][:, :], in_=cacc[2][:, :], func=AF.Silu), [kcacc[2]], [kv_bf[p2]])
        R.add("act", lambda e: e.activation(out=zg[p3][:, :], in_=zg[p3][:, :], func=AF.Silu), [kzg[p3]], [kzg[p3]])
        R.add("act", lambda e: e.activation(out=pk[0:64, :], in_=G_bc[0:64, :], func=AF.Copy), [kG, kg], [kpk])
        R.add("act", lambda e: e.activation(out=pk[64:128, :], in_=B_bc[64:128, :], func=AF.Copy), [kB, kg], [kpk])
        for c in range(8):
            pb = c // 4
            R.add("pe", lambda e, c=c, pb=pb: e.transpose(out=PS[pb][0:64, (c % 4) * 128:(c % 4 + 1) * 128],
                                                          in_=pk[:, c * 64:(c + 1) * 64], identity=identf[:, :]),
                  [kpk, kconst], [kPS[pb]])
        tS = tokS[p2]
        for hh in range(2):
            R.add("act", lambda e, hh=hh: e.activation(
                out=tS[:, 4 * hh:4 * hh + 4, :],
                in_=PS[hh][0:64, :].rearrange("p (c two k) -> p c two k", c=4, two=2)[:, :, :, 0],
                func=AF.Copy), [kPS[hh]], [ktokS[p2]])
        Gt = tS[:, :, 0]
        Bt = tS[:, :, 1]
        Glast64 = G_bc[0:64, :].rearrange("p (c k) -> p c k", k=64)[:, :, 63]
        R.add("act", lambda e: e.activation(out=expGt[:, :], in_=Gt, func=AF.Exp), [ktokS[p2]], [kexpGt])
        R.add("dve", lambda e: e.tensor_tensor(out=deckt[p2][:, :], in0=Glast64, in1=Gt, op=ALU.subtract), [kG, ktokS[p2]], [kdeckt[p2]])
        R.add("act", lambda e: e.activation(out=deckt[p2][:, :], in_=deckt[p2][:, :], func=AF.Exp), [kdeckt[p2]], [kdeckt[p2]])
        R.add("dve", lambda e: e.tensor_tensor(out=bexpt[p2][:, :], in0=Bt, in1=expGt[:, :], op=ALU.mult), [ktokS[p2], kexpGt], [kbexpt[p2]])
        G3 = G_bc[0:64, :].rearrange("p (c k) -> p c k", k=64)
        R.add("dve", lambda e: e.tensor_tensor(out=DT[:, :, :], in0=G3, in1=tS[:, :, 0:1].to_broadcast([64, 8, 64]),
                                               op=ALU.subtract), [kG, ktokS[p2]], [kDT])
        R.add("dve", lambda e: e.tensor_tensor(out=DT[:, :, :], in0=DT[:, :, :],
                                               in1=negm[:, :].unsqueeze(1).to_broadcast([64, 8, 64]), op=ALU.add),
              [kDT, kmask], [kDT])
        R.add("act", lambda e: e.activation(out=DT[:, :, :], in_=DT[:, :, :], func=AF.Exp), [kDT], [kDT])
        R.add("act", lambda e: e.activation(out=sq[0][:, :], in_=q_raw[:, :], func=AF.Square), [kq_raw], [ksq[0]])
        R.add("pe", lambda e: e.matmul(PS[0][:, :], lhsT=ones_bf[:, :], rhs=sq[0][:, :], start=True, stop=True),
              [ksq[0], kconst], [kPS[0]])
        R.add("act", lambda e: e.activation(out=sq[0][:, :], in_=k_raw[:, :], func=AF.Square), [kk_raw], [ksq[0]])
        R.add("pe", lambda e: e.matmul(PS[1][:, :], lhsT=ones_bf[:, :], rhs=sq[0][:, :], start=True, stop=True),
              [ksq[0], kconst], [kPS[1]])
        rsqrt_act(R, cacc[0], kcacc[0], 0, 1.0)
        rsqrt_act(R, cacc[1], kcacc[1], 1, 1.0)
        R.add("dve", lambda e: e.scalar_tensor_tensor(out=q_raw[:, :], in0=q_raw[:, :], scalar=float(128 ** -0.5), in1=cacc[0][:, :],
                                                      op0=ALU.mult, op1=ALU.mult), [kq_raw, kcacc[0]], [kq_raw])
        R.add("dve", lambda e: e.tensor_tensor(out=k_bf[p2][:, :], in0=k_raw[:, :], in1=cacc[1][:, :], op=ALU.mult),
              [kk_raw, kcacc[1]], [kk_bf[p2]])
        R.add("act", lambda e: e.activation(out=q_bf[:, :], in_=q_raw[:, :], func=AF.Copy), [kq_raw], [kq_bf])
        R.add("dve", lambda e: e.tensor_tensor(out=qd[p3][:, :], in0=expG[:, :], in1=q_raw[:, :], op=ALU.mult),
              [kq_raw, kexpG], [kqd[p3]])
        for c in range(8):
            cs = slice(c * 64, (c + 1) * 64)
            R.add("pe", lambda e, cs=cs: e.matmul(PS[2][0:64, cs], lhsT=k_bf[p2][:, cs], rhs=k_bf[p2][:, cs], start=True, stop=True),
                  [kk_bf[p2]], [kPS[2]])
        for c in range(8):
            cs = slice(c * 64, (c + 1) * 64)
            R.add("pe", lambda e, cs=cs: e.matmul(PS[0][0:64, cs], lhsT=k_bf[p2][:, cs], rhs=q_bf[:, cs], start=True, stop=True),
                  [kk_bf[p2], kq_bf], [kPS[0]])
        R.add("dve", lambda e: e.tensor_tensor(out=attnT[p3][:, :, :], in0=PS[0][0:64, :].rearrange("p (c k) -> p c k", k=64),
                                               in1=DT[:, :, :], op=ALU.mult), [kPS[0], kDT], [kattn[p3]])
        R.add("dve", lambda e: e.tensor_tensor(out=DT[:, :, :], in0=DT[:, :, :],
                                               in1=identf[0:64, 0:64].unsqueeze(1).to_broadcast([64, 8, 64]), op=ALU.subtract),
              [kDT, kconst], [kDT])
        R.add("dve", lambda e: e.tensor_tensor(out=DT[:, :, :], in0=DT[:, :, :],
                                               in1=B_bc[0:64, :].rearrange("p (c k) -> p c k", k=64), op=ALU.mult),
              [kDT, kB], [kDT])
        R.add("dve", lambda e: e.scalar_tensor_tensor(out=QZ0[p2][:, :, 0:64], in0=PS[2][0:64, :].rearrange("p (c k) -> p c k", k=64),
                                                      scalar=-1.0, in1=DT[:, :, :], op0=ALU.mult, op1=ALU.mult),
              [kPS[2], kDT], [kQZ0[p2][0], kQZ0[p2][1]])
        psT = PS[1][0:64, 0:256].bitcast(BF16)
        for c in range(8):
            R.add("pe", lambda e, c=c: e.transpose(out=psT[:, c * 64:(c + 1) * 64], in_=QZ0[p2][:, c, 0:64],
                                                   identity=identb[0:64, 0:64]), [kQZ0[p2][c // 4], kconst], [kPS[1]])
        R.add("act", lambda e: e.activation(out=P0[p2][:, :, :], in_=psT.rearrange("p (c k) -> p c k", k=64), func=AF.Copy),
              [kPS[1]], [kP0h[p2][0], kP0h[p2][1]])

    def thread_A2(R, s):
        p2 = s % 2
        psK = PS[3][0:64, :].bitcast(BF16)
        psV = PS[4][0:64, :].bitcast(BF16)
        for c in range(8):
            cs = slice(c * 64, (c + 1) * 64)
            R.add("pe", lambda e, c=c, cs=cs: e.transpose(out=psK[:, c * 128:(c + 1) * 128], in_=k_bf[p2][:, cs], identity=identb[:, :]),
                  [kk_bf[p2], kconst], [kPS[3]])
        for c in range(8):
            cs = slice(c * 64, (c + 1) * 64)
            R.add("pe", lambda e, c=c, cs=cs: e.transpose(out=psV[:, c * 128:(c + 1) * 128], in_=v_bf[p2][:, cs], identity=identb[:, :]),
                  [kv_bf[p2], kconst], [kPS[4]])
        psK3 = psK.rearrange("p (c k) -> p c k", k=128)
        psV3 = psV.rearrange("p (c k) -> p c k", k=128)
        R.add("dve", lambda e: e.tensor_tensor(out=kbg[:, :, :], in0=psK3, in1=bexpt[p2][:, :].unsqueeze(2).to_broadcast([64, 8, 128]),
                                               op=ALU.mult), [kPS[3], kbexpt[p2]], [kkbg])
        R.add("dve", lambda e: e.tensor_tensor(out=kd[p2][:, :, :], in0=psK3, in1=deckt[p2][:, :].unsqueeze(2).to_broadcast([64, 8, 128]),
                                               op=ALU.mult), [kPS[3], kdeckt[p2]], [kkd[p2]])
        R.add("dve", lambda e: e.tensor_tensor(out=vb[:, :, :], in0=psV3, in1=tokS[p2][:, :, 1:2].to_broadcast([64, 8, 128]),
                                               op=ALU.mult), [kPS[4], ktokS[p2]], [kvb])
        for lv in range(6):
            last = (lv == 5)
            if lv == 0:
                srcQZ, ksQZ, srcP, ksP = QZ0[p2], kQZ0[p2], P0[p2], kP0h[p2]
            else:
                srcQZ, ksQZ, srcP, ksP = QZ[(lv - 1) % 2], kQZ[(lv - 1) % 2], Pm[(lv - 1) % 2], kPmh[(lv - 1) % 2]
            dstQZ, kdQZ, dstP, kdP = QZ[lv % 2], kQZ[lv % 2], Pm[lv % 2], kPmh[lv % 2]
            for hh in range(2):
                for c in range(4 * hh, 4 * hh + 4):
                    R.add("pe", lambda e, c=c, hh=hh, srcP=srcP, srcQZ=srcQZ: e.matmul(
                        PS[3 + hh][0:64, (c % 4) * 128:(c % 4 + 1) * 128], lhsT=srcP[:, c, :], rhs=srcQZ[:, c, :],
                        start=True, stop=True), [ksP[hh], ksQZ[hh]], [kPS[3 + hh]])
                if not last:
                    for c in range(4 * hh, 4 * hh + 4):
                        R.add("pe", lambda e, c=c, srcP=srcP, srcQZ=srcQZ: e.matmul(
                            PS[5][0:64, c * 64:(c + 1) * 64], lhsT=srcQZ[:, c, 0:64], rhs=srcP[:, c, :],
                            start=True, stop=True), [ksP[hh], ksQZ[hh]], [kPS5h[hh]])
            for hh in range(2):
                psv = PS[3 + hh][0:64, :].rearrange("p (c k) -> p c k", k=128)
                hs = slice(4 * hh, 4 * hh + 4)
                if not last:
                    R.add("act", lambda e, psv=psv, hs=hs, dstQZ=dstQZ: e.activation(
                        out=dstQZ[:, hs, 0:64], in_=psv[:, :, 0:64], func=AF.Copy), [kPS[3 + hh]], [kdQZ[hh]])
                R.add("dve", lambda e, psv=psv, hs=hs, dstQZ=dstQZ, srcQZ=srcQZ: e.tensor_tensor(
                    out=dstQZ[:, hs, 64:128], in0=psv[:, :, 64:128], in1=srcQZ[:, hs, 64:128],
                    op=ALU.add), [kPS[3 + hh], ksQZ[hh]], [kdQZ[hh]])
                if not last:
                    R.add("act", lambda e, hh=hh, hs=hs, dstP=dstP: e.activation(
                        out=dstP[:, hs, :], in_=PS[5][0:64, hh * 256:(hh + 1) * 256].rearrange("p (c k) -> p c k", k=64),
                        func=AF.Copy), [kPS5h[hh]], [kdP[hh]])
        Zf = QZ[1]
        kZf = kQZ[1]
        for c in range(8):
            pb = 3 + c // 4
            R.add("pe", lambda e, c=c, pb=pb: e.matmul(PS[pb][0:64, (c % 4) * 128:(c % 4 + 1) * 128], lhsT=Zf[:, c, 64:128],
                                                       rhs=vb[:, c, :], start=True, stop=True), [kZf[c // 4], kvb], [kPS[pb]])
        for c in range(8):
            R.add("pe", lambda e, c=c: e.matmul(PS[5][:, c * 64:(c + 1) * 64], lhsT=kbg[:, c, :], rhs=Zf[:, c, 64:128],
                                                start=True, stop=True), [kZf[c // 4], kkbg], [kPS5h[0], kPS5h[1]])
        for hh in range(2):
            R.add("act", lambda e, hh=hh: e.activation(out=uc[p2][:, 4 * hh:4 * hh + 4, :],
                                                       in_=PS[3 + hh][0:64, :].rearrange("p (c k) -> p c k", k=128), func=AF.Copy),
                  [kPS[3 + hh]], [kuc[p2]])
        R.add("act", lambda e: e.activation(out=wcT[p2][:, :], in_=PS[5][:, :], func=AF.Copy), [kPS5h[0], kPS5h[1]], [kwcT[p2]])

    def thread_B(R, s, cur):
        p2 = s % 2
        p3 = s % 3
        for c in range(8):
            cs = slice(c * 64, (c + 1) * 64)
            nxt = 1 - cur
            R.add("pe", lambda e, cs=cs, cur=cur: e.matmul(PS[6][0:64, 0:128], lhsT=wcT[p2][:, cs], rhs=S_bf[cur][:, :],
                                                           start=True, stop=True), [kwcT[p2], kS_bf[cur]], [kPS[6]])
            R.add("dve", lambda e, c=c: e.tensor_tensor(out=vnew[:, c, :], in0=uc[p2][:, c, :], in1=PS[6][0:64, 0:128],
                                                        op=ALU.subtract), [kuc[p2], kPS[6]], [kvnew[c]])
            R.add("pe", lambda e, cs=cs, cur=cur: e.matmul(PS[7][:, cs], lhsT=S_bf[cur][:, :], rhs=qd[p3][:, cs], start=True, stop=False),
                  [kS_bf[cur], kqd[p3]], [kPS[7]])
            R.add("pe", lambda e, c=c, cs=cs: e.matmul(PS[7][:, cs], lhsT=vnew[:, c, :], rhs=attnT[p3][:, c, :], start=False, stop=True),
                  [kvnew[c], kattn[p3]], [kPS[7]])
            R.add("pe", lambda e, c=c: e.matmul(PS[6][:, 128:256], lhsT=kd[p2][:, c, :], rhs=vnew[:, c, :], start=True, stop=True),
                  [kkd[p2], kvnew[c]], [kPS[6]])
            R.add("dve", lambda e, c=c, cur=cur, nxt=nxt: e.scalar_tensor_tensor(
                out=S[nxt][:, :], in0=S[cur][:, :], scalar=eGl[p3][:, c:c + 1], in1=PS[6][:, 128:256], op0=ALU.mult, op1=ALU.add),
                [kS[cur], keGl[p3], kPS[6]], [kS[nxt]])
            R.add("act", lambda e, nxt=nxt: e.activation(out=S_bf[nxt][:, :], in_=S[nxt][:, :], func=AF.Copy), [kS[nxt]], [kS_bf[nxt]])
            cur = nxt
        R.add("act", lambda e: e.activation(out=osb[:, :], in_=PS[7][:, :], func=AF.Copy), [kPS[7]], [kosb])
        R.add("act", lambda e: e.activation(out=sq[1][:, :], in_=osb[:, :], func=AF.Square), [kosb], [ksq[1]])
        R.add("pe", lambda e: e.matmul(PS[6][:, :], lhsT=ones_bf[:, :], rhs=sq[1][:, :], start=True, stop=True),
              [ksq[1], kconst], [kPS[6]])
        rsqrt_act(R, rs, krs, 6, 1.0 / 128)
        R.add("dve", lambda e: e.scalar_tensor_tensor(out=osb[:, :], in0=osb[:, :], scalar=pvec[:, PV_DN:PV_DN + 1], in1=rs[:, :],
                                                      op0=ALU.mult, op1=ALU.mult), [kosb, krs, kpv], [kosb])
        yb_ = s % 2
        R.add("dve", lambda e: e.tensor_tensor(out=ybf[yb_][:, :], in0=osb[:, :], in1=zg[p3][:, :], op=ALU.mult),
              [kosb, kzg[p3]], [kybf[yb_]])
        R.add("sp", lambda e: e.dma_start(out=yb_in[s], in_=ybf[yb_][:, :]),
              [kybf[yb_]], [kyb_in[s]], dma=True, ch="yb%d" % yb_)
        R.add("pool", lambda e: e.collective_compute("AllGather", ALU.bypass, replica_groups=groups,
                                                     ins=[yb_in[s].opt()], outs=[yb_out[s].opt()]),
              [kyb_in[s]], [kyb_out[s]], dma=True, ch="ccy%d" % (s % 4), inc=1)
        return cur

    R0 = Rec()
    load_h2(R0, 0)
    thread_A1(R0, 0)
    replay([R0])
    Ra, Rb = Rec(), Rec()
    thread_A2(Ra, 0)
    if NSEG > 1:
        thread_A1(Rb, 1)
    replay([Ra, Rb])
    cur = 0
    for s in range(NSEG):
        RB, RA2, RA1 = Rec(), Rec(), Rec()
        cur = thread_B(RB, s, cur)
        if s + 1 < NSEG:
            thread_A2(RA2, s + 1)
        if s + 2 < NSEG:
            thread_A1(RA1, s + 2)
        replay([RB, RA2, RA1])
    if stage == 3:
        P.barrier()
        for t in range(4):
            P.add("sp", lambda e, t=t: e.dma_start(out=yT.rearrange("(kc p) t -> p kc t", p=128)[:, :, T(t)], in_=X[:, :, T(t)]),
                  [kX[kc][t] for kc in range(KC)], [Tok()], dma=True, ch="out")
        return nc, P.emit()
    ar.reset(mC)
    P.barrier()

    mD = ar.mark()
    cand = [ar.alloc([128, 4, NT], BF16, "cand") for _ in range(2)]
    ysel = ar.alloc([128, 4, NT], BF16, "ysel")
    wo_bf = ar.alloc([128, KC, D], BF16, "wo")
    stgD_ = [ar.alloc([128, KC, 256], F32, "stgDD") for _ in range(2)]
    kcand = [Tok(), Tok()]
    kysel = Tok()
    kwo = [Tok() for _ in range(4)]
    kstgD_ = [Tok(), Tok()]
    wov = w_o.rearrange("(kc p) f -> p kc f", p=128)
    for q in range(4):
        b = q % 2
        P.add("sp", lambda e, q=q, b=b: e.dma_start(out=stgD_[b][:, :, :], in_=wov[:, :, q * 256:(q + 1) * 256]),
              [], [kstgD_[b]], dma=True, ch="wO%d" % b)
        P.add("act", lambda e, q=q, b=b: e.activation(out=wo_bf[:, :, q * 256:(q + 1) * 256], in_=stgD_[b][:, :, :], func=AF.Copy),
              [kstgD_[b]], [kwo[q]])
    kcandq = [[Tok() for _ in range(4)] for _ in range(2)]
    kyselt = [Tok() for _ in range(4)]
    for j in range(4):
        b = j % 2
        for q in range(4):
            P.add("sp", lambda e, j=j, b=b, q=q: e.dma_start(out=cand[b][:, :, q * 512:(q + 1) * 512],
                                                             in_=yb_out[4 * j + q].rearrange("h e t -> e h t")),
                  [kyb_out[4 * j + q]], [kcandq[b][q]], dma=True, ch="cand%d%d" % (b, q % 2))
        for q in range(4):
            if j == 0:
                P.add("dve", lambda e, b=b, q=q: e.tensor_scalar(out=ysel[:, :, T(q)], in0=cand[b][:, :, T(q)],
                                                                 scalar1=pvec[:, PV_SEL:PV_SEL + 1], scalar2=None, op0=ALU.mult),
                      [kcandq[b][q], kpv], [kyselt[q]])
            else:
                P.add("dve", lambda e, j=j, b=b, q=q: e.scalar_tensor_tensor(
                    out=ysel[:, :, T(q)], in0=cand[b][:, :, T(q)], scalar=pvec[:, PV_SEL + j:PV_SEL + j + 1], in1=ysel[:, :, T(q)],
                    op0=ALU.mult, op1=ALU.add), [kcandq[b][q], kyselt[q], kpv], [kyselt[q]])
    for t in range(4):
        for dc in range(KC):
            pb = (t * KC + dc) % 8
            for kc in range(KC):
                if kc < 4:
                    rhs, rt = ya[:, kc, T(t)], kYa[4 * t:4 * t + 4]
                else:
                    rhs, rt = ysel[:, kc - 4, T(t)], [kyselt[t]]
                P.add("pe", lambda e, dc=dc, kc=kc, pb=pb, rhs=rhs: e.matmul(
                    PS[pb][:, :], lhsT=wo_bf[:, kc, dc * 128:(dc + 1) * 128], rhs=rhs, start=(kc == 0), stop=(kc == KC - 1)),
                    [kwo[dc // 2]] + list(rt), [kPS[pb]])
            P.add("dve", lambda e, dc=dc, t=t, pb=pb: e.tensor_tensor(out=X[:, dc, T(t)], in0=PS[pb][:, :], in1=X[:, dc, T(t)],
                                                                      op=ALU.add), [kPS[pb], kX[dc][t]], [kX[dc][t]])
    ar.reset(mD)
    P.barrier()
    ot = [ar.alloc([128, 512], F32, "ot") for _ in range(3)]
    kot = [Tok(), Tok(), Tok()]
    otc = [0]

    def final_tile(t):
        stats_to_rs(t, [(X[:, kc, T(t)], kX[kc][t]) for kc in range(KC)], None, 1.0 / D)
        for kc in range(KC):
            b = otc[0] % 3
            otc[0] += 1
            P.add("dve", lambda e, kc=kc, t=t, b=b: e.scalar_tensor_tensor(
                out=ot[b][:, :], in0=X[:, kc, T(t)], scalar=pvec[:, PV_NF + kc:PV_NF + kc + 1], in1=rs[:, :],
                op0=ALU.mult, op1=ALU.mult), [kX[kc][t], krs, kpv], [kot[b]])
            P.add("sp", lambda e, kc=kc, t=t, b=b: e.dma_start(out=yT[kc * 128:(kc + 1) * 128, T(t)], in_=ot[b][:, :]),
                  [kot[b]], [Tok()], dma=True, ch="out%d" % b)

    ffn(w2g, w2u, w2d, "f2", PV_N2, tail=final_tile)
    return nc, P.emit()


def make_in_maps(x, ffn1_norm, ffn1_w_gate, ffn1_w_up, ffn1_w_down, mix_norm, w_in,
                 gm_ln_g, gm_ln_b, gm_w_s, gm_b_s, dn_conv_w, dn_a_log, dn_dt_bias, dn_norm,
                 w_out, ffn2_norm, ffn2_w_gate, ffn2_w_up, ffn2_w_down, final_norm):
    f = lambda a: np.ascontiguousarray(np.asarray(a, dtype=np.float32))
    x = f(x)
    w_in0 = f(w_in)[0]
    common = {
        "w1g": f(ffn1_w_gate)[0], "w1u": f(ffn1_w_up)[0], "w1d": f(ffn1_w_down)[0],
        "w2g": f(ffn2_w_gate)[0], "w2u": f(ffn2_w_up)[0], "w2d": f(ffn2_w_down)[0],
        "w_uv": np.ascontiguousarray(w_in0[:, 0:1024]),
        "w_o": f(w_out)[0],
        "lng": np.ascontiguousarray(np.broadcast_to(f(gm_ln_g)[0][None, :], (128, 512))),
        "lnb": np.ascontiguousarray(np.broadcast_to(f(gm_ln_b)[0][None, :], (128, 512))),
        "wsT": np.ascontiguousarray(np.transpose(f(gm_w_s)[0], (2, 0, 1))),
    }
    bs = f(gm_b_s)[0]
    bsT = np.empty((128, 4, 128), np.float32)
    for fc in range(4):
        bsT[0:64, fc, :] = bs[2 * fc][None, :]
        bsT[64:128, fc, :] = bs[2 * fc + 1][None, :]
    common["bsT"] = bsT
    cw = f(dn_conv_w)[0]
    in_maps = []
    for c in range(8):
        b, j = c // 4, c % 4
        hd = j
        m = dict(common)
        m["xT"] = np.ascontiguousarray(x[b, j * NT:(j + 1) * NT, :].T)
        cols = []
        for base in (1024, 1536, 2048, 2560):
            cols.append(w_in0[:, base + hd * 128: base + (hd + 1) * 128])
        cols.append(np.repeat(w_in0[:, 3072 + hd:3072 + hd + 1], 128, axis=1))
        cols.append(np.repeat(w_in0[:, 3076 + hd:3076 + hd + 1], 128, axis=1))
        m["w_dn"] = np.ascontiguousarray(np.concatenate(cols, axis=1))
        pv = np.zeros((128, NPV), np.float32)
        for k0, g in ((PV_N1, ffn1_norm), (PV_NM, mix_norm), (PV_N2, ffn2_norm)):
            pv[:, k0:k0 + 8] = f(g)[0].reshape(8, 128).T
        pv[:, PV_NF:PV_NF + 8] = f(final_norm).reshape(8, 128).T
        pv[:, PV_DN] = f(dn_norm)[0]
        for k0, base in ((PV_CQ, 0), (PV_CK, 512), (PV_CV, 1024)):
            pv[:, k0:k0 + 4] = cw[:, base + hd * 128: base + (hd + 1) * 128].T
        pv[:, PV_ALOG] = f(dn_a_log)[0][hd]
        pv[:, PV_DTB] = f(dn_dt_bias)[0][hd]
        pv[:, PV_SEL + j] = 1.0
        m["pvec"] = pv
        in_maps.append(m)
    return in_maps


_CACHE = {}


def kernel(**inputs):
    if "nc" not in _CACHE:
        _CACHE["nc"] = build()[0]
    nc = _CACHE["nc"]
    in_maps = make_in_maps(**inputs)
    res = run_bass_kernel_spmd(nc, in_maps, core_ids=list(range(8)))
    out = np.empty((2, TS, D), np.float32)
    for c in range(8):
        b, j = c // 4, c % 4
        out[b, j * NT:(j + 1) * NT, :] = np.asarray(res.results[c]["yT"]).T
    return out
```

```python
import numpy as np
import ml_dtypes
import concourse.bass as bass
import concourse.mybir as mybir
from concourse.bass_utils import run_bass_kernel_spmd

F32 = mybir.dt.float32
BF16 = mybir.dt.bfloat16
AF = mybir.ActivationFunctionType
ALU = mybir.AluOpType
AX = mybir.AxisListType

NT = 2048
TS = 8192
D = 1024
KC = 8
FF = 2816
NFB = 11
EPS = 1e-6
NEG = -30000.0


class Tok:
    __slots__ = ("name", "w", "r")

    def __init__(self, name=""):
        self.name = name
        self.w = None
        self.r = []


class Ins:
    __slots__ = ("eng", "fn", "dma", "deps", "sig", "ord", "ch", "chval", "n", "inc", "bar")

    def __init__(self, eng, fn, dma, ch):
        self.eng = eng
        self.fn = fn
        self.dma = dma
        self.deps = []
        self.sig = False
        self.ord = 0
        self.ch = ch
        self.chval = 0
        self.n = 0
        self.inc = 16
        self.bar = None


class Prog:
    ENGS = ("pe", "act", "dve", "pool", "sp")

    def __init__(self, nc):
        self.nc = nc
        self.ins = []
        self.ch_last = {}
        self.ch_cnt = {}
        self.last = {e: None for e in self.ENGS}

    def add(self, eng, fn, reads=(), writes=(), dma=False, ch=None, inc=16):
        i = Ins(eng, fn, dma, ch)
        i.inc = inc
        i.n = len(self.ins)
        deps = {}
        for t in reads:
            if t.w is not None:
                deps[t.w.n] = (t.w, "raw")
        for t in writes:
            if t.w is not None:
                deps[t.w.n] = (t.w, "waw")
            lastr = {}
            for r in t.r:
                if r.dma:
                    if r.n not in deps:
                        deps[r.n] = (r, "war")
                else:
                    lastr[r.eng] = r
            for r in lastr.values():
                if r.n not in deps:
                    deps[r.n] = (r, "war")
        if dma:
            prev = self.ch_last.get(ch)
            if prev is not None:
                deps[prev.n] = (prev, "raw")
            self.ch_last[ch] = i
            self.ch_cnt[ch] = self.ch_cnt.get(ch, 0) + inc
            i.chval = self.ch_cnt[ch]
        i.deps = list(deps.values())
        for t in reads:
            t.r.append(i)
        for t in writes:
            t.w = i
            t.r = []
        self.ins.append(i)
        if not dma:
            self.last[eng] = i
        return i

    def barrier(self):
        lastc = {e: self.last[e] for e in self.ENGS if self.last[e] is not None}
        chs = {c: v for c, v in self.ch_cnt.items() if not str(c).startswith("cc")}
        for e in self.ENGS:
            i = Ins(e, None, False, None)
            i.n = len(self.ins)
            i.bar = (lastc, chs)
            self.ins.append(i)

    def emit(self):
        nc = self.nc
        needed = {}
        for i in self.ins:
            lst = []
            if i.bar is not None:
                for e, d in i.bar[0].items():
                    d.sig = True
                needed[i.n] = lst
                continue
            for d, kind in i.deps:
                if d.dma:
                    lst.append(d)
                    continue
                if d.eng == i.eng and not i.dma:
                    if i.eng == "pe":
                        continue
                    if kind == "war":
                        continue
                d.sig = True
                lst.append(d)
            needed[i.n] = lst
        cnt = {e: 0 for e in self.ENGS}
        for i in self.ins:
            if i.bar is None and i.sig and not i.dma:
                cnt[i.eng] += 1
                i.ord = cnt[i.eng]
        esem = {e: nc.alloc_semaphore("sem_" + e) for e in self.ENGS}
        chsem = {c: nc.alloc_semaphore("ch_" + str(c)) for c in self.ch_cnt}
        streams = {e: [i for i in self.ins if i.eng == e] for e in self.ENGS}
        stats = {e: [len(streams[e]), cnt[e], 0] for e in self.ENGS}

        def run(eng_name, eng):
            waited = {}

            def do_wait(key, sem, val):
                if waited.get(key, 0) >= val:
                    return
                eng.wait_ge(sem, val)
                waited[key] = val
                stats[eng_name][2] += 1

            for i in streams[eng_name]:
                if i.bar is not None:
                    for e, d in i.bar[0].items():
                        do_wait(("e", e), esem[e], d.ord)
                    for c, v in i.bar[1].items():
                        do_wait(("c", c), chsem[c], v)
                    continue
                w = {}
                for d in needed[i.n]:
                    if d.dma:
                        key, val, sem = ("c", d.ch), d.chval, chsem[d.ch]
                    else:
                        key, val, sem = ("e", d.eng), d.ord, esem[d.eng]
                    if key not in w or w[key][1] < val:
                        w[key] = (sem, val)
                for key, (sem, val) in w.items():
                    do_wait(key, sem, val)
                bi = i.fn(eng)
                if i.dma:
                    bi.then_inc(chsem[i.ch], i.inc)
                elif i.sig:
                    bi.then_inc(esem[i.eng], 1)
            if eng_name == "sp":
                for c, n in self.ch_cnt.items():
                    do_wait(("c", c), chsem[c], n)
                for e in self.ENGS:
                    if cnt[e]:
                        do_wait(("e", e), esem[e], cnt[e])

        with nc.Block() as block:
            @block.tensor
            def _(e):
                run("pe", e)

            @block.scalar
            def _(e):
                run("act", e)

            @block.vector
            def _(e):
                run("dve", e)

            @block.gpsimd
            def _(e):
                run("pool", e)

            @block.sync
            def _(e):
                run("sp", e)
        return stats


class Arena:
    def __init__(self, nc, lo, hi):
        self.nc = nc
        self.lo = lo
        self.hi = hi
        self.cur = lo
        self.n = 0

    def alloc(self, shape, dtype, name="t"):
        sz = 4 if dtype == F32 else 2
        per = sz
        for s in shape[1:]:
            per *= s
        off = (self.cur + 63) // 64 * 64
        assert off + per <= self.hi, ("SBUF overflow", name, off + per - self.hi)
        self.cur = off + per
        self.n += 1
        return self.nc.alloc_sbuf_tensor_at("%s_%d" % (name, self.n), list(shape), dtype, offset=off)

    def mark(self):
        return self.cur

    def reset(self, m):
        self.cur = m


PV_N1, PV_NM, PV_N2, PV_NF = 0, 8, 16, 24
PV_DN = 32
PV_CQ, PV_CK, PV_CV = 33, 37, 41
PV_ALOG, PV_DTB = 45, 46
PV_SEL = 47
NPV = 52


def build(stage=99, dbg_cols=0):
    nc = bass.Bass("TRN2", target_bir_lowering=False)
    P = Prog(nc)

    def din(name, shape, dt=F32):
        return nc.dram_tensor(name, list(shape), dt, kind="ExternalInput").ap()

    xT = din("xT", [D, NT])
    w1g, w1u, w1d = din("w1g", [D, FF]), din("w1u", [D, FF]), din("w1d", [FF, D])
    w2g, w2u, w2d = din("w2g", [D, FF]), din("w2u", [D, FF]), din("w2d", [FF, D])
    w_uv = din("w_uv", [D, 1024])
    w_dn = din("w_dn", [D, 768])
    w_o = din("w_o", [D, D])
    pvec_d = din("pvec", [128, NPV])
    lng_d, lnb_d = din("lng", [128, 512]), din("lnb", [128, 512])
    wsT_d = din("wsT", [128, 8, 128])
    bsT_d = din("bsT", [128, 4, 128])
    yT = nc.dram_tensor("yT", [D, NT], F32, kind="ExternalOutput").ap()
    dbg = None
    if dbg_cols:
        dbg = nc.dram_tensor("dbg", [128, dbg_cols], F32, kind="ExternalOutput").ap()
    hg_in = nc.dram_tensor("hg_in", [8, KC, 128, 256], BF16).ap()
    hg_out = nc.dram_tensor("hg_out", [8, 4, KC, 128, 256], BF16).ap()
    yb_in = nc.dram_tensor("yb_in", [16, 128, 512], BF16).ap()
    yb_out = nc.dram_tensor("yb_out", [16, 4, 128, 512], BF16).ap()
    groups = [[0, 1, 2, 3], [4, 5, 6, 7]]

    ar = Arena(nc, 16512, 229344)
    X = ar.alloc([128, KC, NT], F32, "X")
    ya = ar.alloc([128, 4, NT], BF16, "ya")
    pvec = ar.alloc([128, NPV], F32, "pvec")
    ones_bf = ar.alloc([128, 128], BF16, "ones")
    identf = ar.alloc([128, 128], F32, "identf")
    identb = ar.alloc([128, 128], BF16, "identb")
    negA = ar.alloc([128, 1], F32, "negA")
    sq = [ar.alloc([128, 512], BF16, "sq") for _ in range(2)]
    rs = ar.alloc([128, 512], F32, "rs")
    mark_noH = ar.mark()
    hT = ar.alloc([128, KC, NT], BF16, "hT")
    PS = [nc.alloc_psum_tensor("ps%d" % i, [128, 512], F32) for i in range(8)]
    kPS = [Tok("ps%d" % i) for i in range(8)]
    kX = [[Tok() for _ in range(4)] for _ in range(KC)]
    kH = [[Tok() for _ in range(4)] for _ in range(KC)]
    kYa = [Tok() for _ in range(16)]
    kpv, kconst, knegA = Tok(), Tok(), Tok()
    ksq = [Tok(), Tok()]
    krs = Tok()
    phase_mark = ar.mark()

    def T(i):
        return slice(i * 512, (i + 1) * 512)

    P.add("sp", lambda e: e.dma_start(out=pvec[:, :], in_=pvec_d), [], [kpv], dma=True, ch="pv")
    xTv = xT.rearrange("(kc p) t -> p kc t", p=128)

    def load_x(t):
        P.add("sp", lambda e: e.dma_start(out=X[:, :, T(t)], in_=xTv[:, :, T(t)]),
              [], [kX[kc][t] for kc in range(KC)], dma=True, ch="x%d" % t)
    load_x(0)
    P.add("pool", lambda e: e.memset(ones_bf[:, :], 1.0), [], [kconst])
    P.add("pool", lambda e: e.memset(identf[:, :], 1.0), [kconst], [kconst])
    P.add("pool", lambda e: e.affine_select(out=identf[:, :], in_=identf[:, :], pattern=[[-1, 128]],
                                            compare_op=ALU.is_equal, fill=0.0, base=0, channel_multiplier=1),
          [kconst], [kconst])
    P.add("pool", lambda e: e.tensor_copy(out=identb[:, :], in_=identf[:, :]), [kconst], [kconst])
    P.add("act", lambda e: e.activation(out=negA[:, :], in_=pvec[:, PV_ALOG:PV_ALOG + 1], func=AF.Exp), [kpv], [knegA])
    P.add("dve", lambda e: e.tensor_scalar(out=negA[:, :], in0=negA[:, :], scalar1=-1.0, scalar2=None, op0=ALU.mult),
          [knegA], [knegA])

    def stats_to_rs(t, src_fn, src_toks, scale, nparts=128):
        n = len(src_fn)
        for j, (ap, tk) in enumerate(src_fn):
            b = j % 2
            P.add("act", lambda e, ap=ap, b=b: e.activation(out=sq[b][:, :], in_=ap, func=AF.Square), [tk], [ksq[b]])
            P.add("pe", lambda e, b=b, j=j: e.matmul(PS[7][:, :], lhsT=ones_bf[:, :], rhs=sq[b][:, :],
                                                     start=(j == 0), stop=(j == n - 1)),
                  [ksq[b], kconst], [kPS[7]])
        P.add("act", lambda e: e.activation(out=rs[:, :], in_=PS[7][:, :], func=AF.Ln, scale=scale, bias=EPS),
              [kPS[7]], [krs])
        P.add("act", lambda e: e.activation(out=rs[:, :], in_=rs[:, :], func=AF.Exp, scale=-0.5), [krs], [krs])

    def rmsnorm_to_h(gcol, tiles=range(4)):
        for t in tiles:
            stats_to_rs(t, [(X[:, kc, T(t)], kX[kc][t]) for kc in range(KC)], None, 1.0 / D)
            for kc in range(KC):
                P.add("dve", lambda e, kc=kc, t=t: e.scalar_tensor_tensor(
                    out=hT[:, kc, T(t)], in0=X[:, kc, T(t)], scalar=pvec[:, gcol + kc:gcol + kc + 1], in1=rs[:, :],
                    op0=ALU.mult, op1=ALU.mult), [kX[kc][t], krs, kpv], [kH[kc][t]])

    def dbg_dump(ap, toks, col0, ncols, parts=128):
        if dbg is None:
            return
        P.add("sp", lambda e: e.dma_start(out=dbg[0:parts, col0:col0 + ncols], in_=ap), toks, [Tok()], dma=True, ch="dbg")

    def ffn(wg_d, wu_d, wd_d, tag, gcol, tail=None, after_load0=None):
        m = ar.mark()
        stg = [[ar.alloc([128, KC, 256], F32, "stgG") for _ in range(2)],
               [ar.alloc([128, KC, 256], F32, "stgU") for _ in range(2)],
               [ar.alloc([128, 2, 1024], F32, "stgD") for _ in range(2)]]
        wbf = [[ar.alloc([128, KC, 256], BF16, "wg") for _ in range(2)],
               [ar.alloc([128, KC, 256], BF16, "wu") for _ in range(2)],
               [ar.alloc([128, 2, 1024], BF16, "wd") for _ in range(2)]]
        sgt = [ar.alloc([128, 512], F32, "sg") for _ in range(2)]
        at = [ar.alloc([128, 2, 512], BF16, "a") for _ in range(2)]
        kstg = [[Tok(), Tok()] for _ in range(3)]
        kw = [[Tok(), Tok()] for _ in range(3)]
        ksg = [Tok(), Tok()]
        ka = [[Tok(), Tok()] for _ in range(2)]
        wgv = wg_d.rearrange("(kc p) f -> p kc f", p=128)
        wuv = wu_d.rearrange("(kc p) f -> p kc f", p=128)
        wdv = wd_d.rearrange("(fc p) d -> p fc d", p=128)

        def load(F):
            b = F % 2
            fs = slice(F * 256, (F + 1) * 256)
            P.add("sp", lambda e: e.dma_start(out=stg[0][b][:, :, :], in_=wgv[:, :, fs]), [], [kstg[0][b]], dma=True, ch="wG%d" % b)
            P.add("sp", lambda e: e.dma_start(out=stg[1][b][:, :, :], in_=wuv[:, :, fs]), [], [kstg[1][b]], dma=True, ch="wU%d" % b)
            P.add("sp", lambda e: e.dma_start(out=stg[2][b][:, :, :], in_=wdv[:, 2 * F:2 * F + 2, :]), [], [kstg[2][b]], dma=True, ch="wD%d" % b)

        def cast(F):
            b = F % 2
            for k in range(3):
                P.add("act", lambda e, k=k: e.activation(out=wbf[k][b][:, :, :], in_=stg[k][b][:, :, :], func=AF.Copy),
                      [kstg[k][b]], [kw[k][b]])

        def gu(F, t, idx):
            b = F % 2
            for fc in range(2):
                pg, pu = 2 * fc, 2 * fc + 1
                for kc in range(KC):
                    P.add("pe", lambda e, kc=kc, fc=fc, pg=pg: e.matmul(
                        PS[pg][:, :], lhsT=wbf[0][b][:, kc, fc * 128:(fc + 1) * 128], rhs=hT[:, kc, T(t)],
                        start=(kc == 0), stop=(kc == KC - 1)), [kw[0][b], kH[kc][t]], [kPS[pg]])
                for kc in range(KC):
                    P.add("pe", lambda e, kc=kc, fc=fc, pu=pu: e.matmul(
                        PS[pu][:, :], lhsT=wbf[1][b][:, kc, fc * 128:(fc + 1) * 128], rhs=hT[:, kc, T(t)],
                        start=(kc == 0), stop=(kc == KC - 1)), [kw[1][b], kH[kc][t]], [kPS[pu]])
                P.add("act", lambda e, fc=fc, pg=pg: e.activation(out=sgt[fc][:, :], in_=PS[pg][:, :], func=AF.Silu),
                      [kPS[pg]], [ksg[fc]])
                P.add("dve", lambda e, fc=fc, pu=pu: e.tensor_tensor(out=at[idx % 2][:, fc, :], in0=sgt[fc][:, :],
                                                                     in1=PS[pu][:, :], op=ALU.mult),
                      [ksg[fc], kPS[pu]], [ka[idx % 2][fc]])

        def down(F, t, idx):
            b = F % 2
            for dc in range(KC):
                pb = 4 + dc % 4
                for fc in range(2):
                    P.add("pe", lambda e, dc=dc, fc=fc, pb=pb: e.matmul(
                        PS[pb][:, :], lhsT=wbf[2][b][:, fc, dc * 128:(dc + 1) * 128], rhs=at[idx % 2][:, fc, :],
                        start=(fc == 0), stop=(fc == 1)), [kw[2][b], ka[idx % 2][fc]], [kPS[pb]])
                P.add("dve", lambda e, dc=dc, pb=pb: e.scalar_tensor_tensor(
                    out=X[:, dc, T(t)], in0=PS[pb][:, :], scalar=0.5, in1=X[:, dc, T(t)], op0=ALU.mult, op1=ALU.add),
                    [kPS[pb], kX[dc][t]], [kX[dc][t]])

        load(0)
        if after_load0 is not None:
            after_load0()
        load(1)
        items = [(F, t) for F in range(NFB) for t in range(4)]
        rmsnorm_to_h(gcol, [0])
        for idx, (F, t) in enumerate(items):
            if t == 0:
                cast(F)
            if F == 0 and t + 1 < 4:
                rmsnorm_to_h(gcol, [t + 1])
            gu(F, t, idx)
            if idx > 0:
                pF, pt = items[idx - 1]
                down(pF, pt, idx - 1)
                if pt == 3 and pF + 2 < NFB:
                    load(pF + 2)
                if tail is not None and pF == NFB - 1:
                    tail(pt)
        pF, pt = items[-1]
        down(pF, pt, len(items) - 1)
        if tail is not None:
            tail(pt)
        ar.reset(m)

    ffn(w1g, w1u, w1d, "f1", PV_N1, after_load0=lambda: [load_x(t) for t in (1, 2, 3)])
    if stage == 1:
        P.barrier()
        for t in range(4):
            P.add("sp", lambda e, t=t: e.dma_start(out=yT.rearrange("(kc p) t -> p kc t", p=128)[:, :, T(t)], in_=X[:, :, T(t)]),
                  [kX[kc][t] for kc in range(KC)], [Tok()], dma=True, ch="out")
        return nc, P.emit()
    P.barrier()

    mB = ar.mark()
    stgB = [ar.alloc([128, KC, 256], F32, "stgB") for _ in range(2)]
    wuv_bf = ar.alloc([128, KC, 1024], BF16, "wuv")
    gu_t = ar.alloc([128, 4, NT], BF16, "gu")
    lng = ar.alloc([128, 512], F32, "lng")
    lnb = ar.alloc([128, 512], F32, "lnb")
    wsTf = ar.alloc([128, 8, 128], F32, "wsTf")
    wsTb = ar.alloc([128, 8, 128], BF16, "wsTb")
    bsT = ar.alloc([128, 4, 128], F32, "bsT")
    NTH = 4
    vg = [ar.alloc([128, 512], F32, "vg") for _ in range(NTH)]
    vsq_ = [ar.alloc([128, 512], F32, "vsq") for _ in range(NTH)]
    vn = [ar.alloc([128, 512], BF16, "vn") for _ in range(NTH)]
    st = [ar.alloc([128, 8], F32, "st") for _ in range(NTH)]
    kstgB = [Tok(), Tok()]
    kwuv = [Tok() for _ in range(4)]
    kgu = [[Tok() for _ in range(4)] for _ in range(4)]
    kln, kws, kbs = Tok(), Tok(), Tok()
    kvg = [Tok() for _ in range(NTH)]
    kvsq_ = [Tok() for _ in range(NTH)]
    kvn = [Tok() for _ in range(NTH)]
    kst = [Tok() for _ in range(NTH)]
    P.add("sp", lambda e: e.dma_start(out=lng[:, :], in_=lng_d), [], [kln], dma=True, ch="c1")
    P.add("sp", lambda e: e.dma_start(out=lnb[:, :], in_=lnb_d), [], [kln], dma=True, ch="c1")
    P.add("sp", lambda e: e.dma_start(out=wsTf[:, :, :], in_=wsT_d), [], [kws], dma=True, ch="c2")
    P.add("sp", lambda e: e.dma_start(out=bsT[:, :, :], in_=bsT_d), [], [kbs], dma=True, ch="c3")
    P.add("pool", lambda e: e.memset(wsTf[64:128, :, 0:64], 0.0), [kws], [kws])
    P.add("pool", lambda e: e.tensor_copy(out=wsTb[:, :, :], in_=wsTf[:, :, :]), [kws], [kws])
    wuvv = w_uv.rearrange("(kc p) f -> p kc f", p=128)
    for q in range(4):
        b = q % 2
        P.add("sp", lambda e, q=q, b=b: e.dma_start(out=stgB[b][:, :, :], in_=wuvv[:, :, q * 256:(q + 1) * 256]),
              [], [kstgB[b]], dma=True, ch="wB%d" % b)
        P.add("act", lambda e, q=q, b=b: e.activation(out=wuv_bf[:, :, q * 256:(q + 1) * 256], in_=stgB[b][:, :, :], func=AF.Copy),
              [kstgB[b]], [kwuv[q]])
    rmsnorm_to_h(PV_NM)
    khg_in = [Tok() for _ in range(KC)]
    khg_out = [Tok() for _ in range(KC)]
    for pc in range(8):
        P.add("sp", lambda e, pc=pc: e.dma_start(out=hg_in[pc].rearrange("kc p t -> p kc t"), in_=hT[:, :, pc * 256:(pc + 1) * 256]),
              [kH[kc][pc // 2] for kc in range(KC)], [khg_in[pc]], dma=True, ch="hgx%d" % (pc % 2))
        P.add("pool", lambda e, pc=pc: e.collective_compute("AllGather", ALU.bypass, replica_groups=groups,
                                                            ins=[hg_in[pc].opt()], outs=[hg_out[pc].opt()]),
              [khg_in[pc]], [khg_out[pc]], dma=True, ch="cch%d" % pc, inc=1)

    for fc in range(4):
        for t in range(4):
            pb = (fc * 4 + t) % 4
            for kc in range(KC):
                P.add("pe", lambda e, kc=kc, fc=fc, t=t, pb=pb: e.matmul(
                    PS[pb][:, :], lhsT=wuv_bf[:, kc, fc * 128:(fc + 1) * 128], rhs=hT[:, kc, T(t)],
                    start=(kc == 0), stop=(kc == KC - 1)), [kwuv[fc // 2], kH[kc][t]], [kPS[pb]])
            P.add("act", lambda e, fc=fc, t=t, pb=pb: e.activation(out=gu_t[:, fc, T(t)], in_=PS[pb][:, :],
                                                                   func=AF.Gelu_apprx_tanh), [kPS[pb]], [kgu[fc][t]])
    class RecB:
        def __init__(self):
            self.calls = []

        def add(self, *a, **k):
            self.calls.append((a, k))

    def gm_block(P, blk):
        b = blk % NTH
        t = blk // 4
        bs = slice(blk * 128, (blk + 1) * 128)
        pv_, pm_ = b, 4 + b
        vsq, vt = vsq_[b], vg[b]
        mt = vsq_[b][:, :].rearrange("p (f i) -> p f i", f=4)
        kvsq, kvt, kmt = kvsq_[b], kvg[b], kvsq_[b]
        for kc in range(KC):
            P.add("pe", lambda e, kc=kc, bs=bs, pv_=pv_: e.matmul(
                PS[pv_][:, :], lhsT=hT[:, kc, bs], rhs=wuv_bf[:, kc, 512:1024],
                start=(kc == 0), stop=(kc == KC - 1)), [kwuv[2], kwuv[3], kH[kc][t]], [kPS[pv_]])
        P.add("act", lambda e, b=b, pv_=pv_: e.activation(out=vg[b][:, :], in_=PS[pv_][:, :], func=AF.Gelu_apprx_tanh),
              [kPS[pv_]], [kvg[b]])
        s = st[b]
        P.add("dve", lambda e, b=b, s=s: e.reduce_sum(out=s[:, 0:1], in_=vg[b][:, :], axis=AX.X), [kvg[b]], [kst[b]])
        P.add("act", lambda e, b=b, vsq=vsq: e.activation(out=vsq[:, :], in_=vg[b][:, :], func=AF.Square), [kvg[b]], [kvsq])
        P.add("dve", lambda e, s=s, vsq=vsq: e.reduce_sum(out=s[:, 1:2], in_=vsq[:, :], axis=AX.X), [kvsq], [kst[b]])
        P.add("dve", lambda e, s=s: e.tensor_scalar(out=s[:, 2:3], in0=s[:, 0:1], scalar1=1.0 / 512, scalar2=None, op0=ALU.mult),
              [kst[b]], [kst[b]])
        P.add("dve", lambda e, s=s: e.tensor_tensor(out=s[:, 3:4], in0=s[:, 2:3], in1=s[:, 2:3], op=ALU.mult), [kst[b]], [kst[b]])
        P.add("dve", lambda e, s=s: e.scalar_tensor_tensor(out=s[:, 4:5], in0=s[:, 1:2], scalar=1.0 / 512, in1=s[:, 3:4],
                                                           op0=ALU.mult, op1=ALU.subtract), [kst[b]], [kst[b]])
        P.add("act", lambda e, s=s: e.activation(out=s[:, 5:6], in_=s[:, 4:5], func=AF.Sqrt, bias=EPS), [kst[b]], [kst[b]])
        P.add("dve", lambda e, s=s: e.reciprocal(out=s[:, 5:6], in_=s[:, 5:6]), [kst[b]], [kst[b]])
        P.add("dve", lambda e, s=s: e.scalar_tensor_tensor(out=s[:, 6:7], in0=s[:, 2:3], scalar=-1.0, in1=s[:, 5:6],
                                                           op0=ALU.mult, op1=ALU.mult), [kst[b]], [kst[b]])
        P.add("act", lambda e, b=b, s=s: e.activation(out=vg[b][:, :], in_=vg[b][:, :], func=AF.Identity,
                                                      scale=s[:, 5:6], bias=s[:, 6:7]), [kvg[b], kst[b], kvsq], [kvg[b]])
        P.add("pool", lambda e, b=b: e.tensor_tensor(out=vg[b][:, :], in0=vg[b][:, :], in1=lng[:, :], op=ALU.mult), [kvg[b], kln], [kvg[b]])
        P.add("pool", lambda e, b=b: e.tensor_tensor(out=vn[b][:, :], in0=vg[b][:, :], in1=lnb[:, :], op=ALU.add), [kvg[b], kln], [kvn[b]])
        for fc in range(4):
            for hh in range(2):
                g = 2 * fc + hh
                P.add("pe", lambda e, fc=fc, hh=hh, g=g, b=b, pm_=pm_: e.matmul(
                    PS[pm_][hh * 64:(hh + 1) * 64, fc * 128:(fc + 1) * 128], lhsT=vn[b][:, g * 64:(g + 1) * 64],
                    rhs=wsTb[:, g, :], start=True, stop=True), [kvn[b], kws], [kPS[pm_]])
        P.add("dve", lambda e, pm_=pm_, mt=mt: e.tensor_tensor(out=mt, in0=PS[pm_][:, :].rearrange("p (f i) -> p f i", f=4),
                                                               in1=bsT[:, :, :], op=ALU.add), [kPS[pm_], kbs], [kmt])
        P.add("dve", lambda e, bs=bs, mt=mt: e.tensor_tensor(out=ya[:, :, bs], in0=mt, in1=gu_t[:, :, bs], op=ALU.mult),
              [kmt] + [kgu[fc][t] for fc in range(4)], [kYa[blk]])

    rr_ = [RecB() for _ in range(NTH)]
    for k in range(NTH):
        for m in range(16 // NTH):
            gm_block(rr_[k], NTH * m + k)
    nblk = len(rr_[0].calls) // (16 // NTH)
    posb = [0] * NTH
    nb = [len(r.calls) for r in rr_]
    while any(posb[k] < nb[k] for k in range(NTH)):
        best, bf = None, None
        for k in range(NTH):
            if posb[k] < nb[k]:
                f = posb[k] + k * nblk // NTH
                if bf is None or f < bf:
                    best, bf = k, f
        c0 = rr_[best].calls[posb[best]]
        P.add(*c0[0], **c0[1])
        posb[best] += 1
    if stage == 2:
        P.barrier()
        tmpf = ar.alloc([128, 4, 512], F32, "tmpf")
        ktmp = Tok()
        for t in range(4):
            P.add("pool", lambda e, t=t: e.tensor_copy(out=tmpf[:, :, :], in_=ya[:, :, T(t)]), kYa[4 * t:4 * t + 4], [ktmp])
            P.add("sp", lambda e, t=t: e.dma_start(out=yT.rearrange("(kc p) t -> p kc t", p=128)[:, 0:4, T(t)], in_=tmpf[:, :, :]),
                  [ktmp], [Tok()], dma=True, ch="out")
        return nc, P.emit()
    ar.reset(mB)
    P.barrier()

    mC = ar.mark()
    ar.reset(mark_noH)
    wdn_bf = ar.alloc([128, KC, 768], BF16, "wdn")
    mC1 = ar.mark()
    stgC = [ar.alloc([128, KC, 256], F32, "stgC") for _ in range(2)]
    kwdn = [Tok() for _ in range(3)]
    kstgC = [Tok(), Tok()]
    wdnv = w_dn.rearrange("(kc p) f -> p kc f", p=128)
    for q in range(3):
        b = q % 2
        P.add("sp", lambda e, q=q, b=b: e.dma_start(out=stgC[b][:, :, :], in_=wdnv[:, :, q * 256:(q + 1) * 256]),
              [], [kstgC[b]], dma=True, ch="wC%d" % b)
        P.add("act", lambda e, q=q, b=b: e.activation(out=wdn_bf[:, :, q * 256:(q + 1) * 256], in_=stgC[b][:, :, :], func=AF.Copy),
              [kstgC[b]], [kwdn[q]])
    P.barrier()
    ar.reset(mC1)
    h2 = [ar.alloc([128, KC, 512], BF16, "h2") for _ in range(2)]
    pre = [ar.alloc([128, 515], F32, "pre") for _ in range(3)]
    cacc = [ar.alloc([128, 512], F32, "cacc") for _ in range(3)]
    q_raw = ar.alloc([128, 512], F32, "q_raw")
    k_raw = ar.alloc([128, 512], F32, "k_raw")
    B_bc = ar.alloc([128, 512], F32, "B_bc")
    g_bc = ar.alloc([128, 512], F32, "g_bc")
    G_bc = ar.alloc([128, 512], F32, "G_bc")
    pk = g_bc
    mscan = ar.alloc([128, 512], F32, "mscan")
    negm = ar.alloc([64, 64], F32, "negm")
    q_bf = ar.alloc([128, 512], BF16, "q_bf")
    DT = ar.alloc([64, 8, 64], F32, "DT")
    expGt = ar.alloc([64, 8], F32, "expGt")
    k_bf = [ar.alloc([128, 512], BF16, "k_bf") for _ in range(2)]
    v_bf = [ar.alloc([128, 512], BF16, "v_bf") for _ in range(2)]
    QZ0 = [ar.alloc([64, 8, 128], BF16, "QZ0") for _ in range(2)]
    P0 = [ar.alloc([64, 8, 64], BF16, "P0") for _ in range(2)]
    tokS = [ar.alloc([64, 8, 2], F32, "tokS") for _ in range(2)]
    deckt = [ar.alloc([64, 8], F32, "deckt") for _ in range(2)]
    bexpt = [ar.alloc([64, 8], F32, "bexpt") for _ in range(2)]
    QZ = [ar.alloc([64, 8, 128], BF16, "QZ") for _ in range(2)]
    Pm = [ar.alloc([64, 8, 64], BF16, "Pm") for _ in range(2)]
    kbg = ar.alloc([64, 8, 128], BF16, "kbg")
    vb = ar.alloc([64, 8, 128], BF16, "vb")
    qd = [ar.alloc([128, 512], BF16, "qd") for _ in range(3)]
    expG = ar.alloc([128, 512], F32, "expG")
    kexpG = Tok()
    zg = [ar.alloc([128, 512], F32, "zg") for _ in range(3)]
    attnT = [ar.alloc([64, 8, 64], BF16, "attnT") for _ in range(3)]
    eGl = [ar.alloc([128, 8], F32, "eGl") for _ in range(3)]
    kd = [ar.alloc([64, 8, 128], BF16, "kd") for _ in range(2)]
    uc = [ar.alloc([64, 8, 128], F32, "uc") for _ in range(2)]
    wcT = [ar.alloc([128, 512], BF16, "wcT") for _ in range(2)]
    vnew = ar.alloc([64, 8, 128], BF16, "vnew")
    S = [ar.alloc([128, 128], F32, "S") for _ in range(2)]
    S_bf = [ar.alloc([128, 128], BF16, "S_bf") for _ in range(2)]
    kS_bf = [Tok(), Tok()]
    osb = ar.alloc([128, 512], F32, "osb")
    ybf0 = ar.alloc([128, 512], BF16, "ybf")
    ybf = [ybf0, ybf0]

    kh2 = [Tok(), Tok()]
    kpre = [Tok() for _ in range(3)]
    kcacc = [Tok() for _ in range(3)]
    kq_raw, kk_raw, kB, kg, kG = (Tok() for _ in range(5))
    kpk = kg
    kmask, kq_bf, kDT, kexpGt = (Tok() for _ in range(4))
    kk_bf, kv_bf, kP0, ktokS, kdeckt, kbexpt = ([Tok(), Tok()] for _ in range(6))
    kQZ0 = [[Tok(), Tok()], [Tok(), Tok()]]
    kP0h = [[Tok(), Tok()], [Tok(), Tok()]]
    kQZ = [[Tok(), Tok()], [Tok(), Tok()]]
    kPmh = [[Tok(), Tok()], [Tok(), Tok()]]
    kkbg, kvb = Tok(), Tok()
    kqd, kzg, kattn, keGl = ([Tok(), Tok(), Tok()] for _ in range(4))
    kkd, kuc, kwcT = ([Tok(), Tok()] for _ in range(3))
    kvnew = [Tok() for _ in range(8)]
    kS = [Tok(), Tok()]
    kosb = Tok()
    kybf0 = Tok()
    kybf = [kybf0, kybf0]
    kyb_in = [Tok() for _ in range(16)]
    kyb_out = [Tok() for _ in range(16)]
    kPS5h = [kPS[5], kPS[5]]

    P.add("pool", lambda e: e.memset(mscan[:, :], 1.0), [], [kmask])
    P.add("pool", lambda e: e.memset(mscan[:, :].rearrange("p (c k) -> p c k", k=64)[:, :, 0:1], 0.0), [kmask], [kmask])
    P.add("pool", lambda e: e.memset(negm[:, :], 0.0), [kmask], [kmask])
    P.add("pool", lambda e: e.affine_select(out=negm[:, :], in_=negm[:, :], pattern=[[1, 64]], compare_op=ALU.is_ge,
                                            fill=NEG, base=0, channel_multiplier=-1), [kmask], [kmask])
    P.add("pool", lambda e: e.memset(S[0][:, :], 0.0), [], [kS[0]])
    P.add("pool", lambda e: e.memset(S_bf[0][:, :], 0.0), [], [kS_bf[0]])
    for i in range(3):
        P.add("pool", lambda e, i=i: e.memset(pre[i][:, 0:3], 0.0), [], [kpre[i]])
    for p2 in range(2):
        P.add("pool", lambda e, p2=p2: e.tensor_copy(out=QZ0[p2][:, :, 64:128],
                                                     in_=identb[0:64, 0:64].unsqueeze(1).to_broadcast([64, 8, 64])),
              [kconst], [kQZ0[p2][0], kQZ0[p2][1]])

    class Rec:
        def __init__(self):
            self.calls = []

        def add(self, *a, **k):
            self.calls.append((a, k))

    def replay(recs):
        recs = [r for r in recs if r.calls]
        units = []
        for r in recs:
            u, curu = [], []
            for call in r.calls:
                is_pe = (call[0][0] == "pe")
                if curu and (is_pe != curu_pe or len(curu) >= 8):
                    u.append(curu)
                    curu = []
                if not is_pe and curu:
                    u.append(curu)
                    curu = []
                curu.append(call)
                curu_pe = is_pe
            if curu:
                u.append(curu)
            units.append(u)
        n = [sum(len(x) for x in u) for u in units]
        done = [0] * len(recs)
        pos = [0] * len(recs)
        while True:
            best, bf = None, None
            for i in range(len(recs)):
                if pos[i] < len(units[i]):
                    f = done[i] / n[i]
                    if bf is None or f < bf:
                        best, bf = i, f
            if best is None:
                break
            for a, k in units[best][pos[best]]:
                P.add(*a, **k)
            done[best] += len(units[best][pos[best]])
            pos[best] += 1

    NSEG = TS // 512
    if stage == 3:
        NSEG = 4
    conv_cols = [PV_CQ, PV_CK, PV_CV]

    def load_h2(R, s):
        b = s % 2
        r = s // 4
        for half in range(2):
            pc = 2 * (s % 4) + half
            src = hg_out[pc, r].rearrange("kc p t -> p kc t")
            R.add("sp", lambda e, half=half, src=src: e.dma_start(out=h2[b][:, :, half * 256:(half + 1) * 256], in_=src),
                  [khg_out[pc]], [kh2[b]], dma=True, ch="h2%d" % b)

    def rsqrt_act(R, dst, kdst, pb, scale):
        R.add("act", lambda e: e.activation(out=dst[:, :], in_=PS[pb][:, :], func=AF.Ln, scale=scale, bias=EPS), [kPS[pb]], [kdst])
        R.add("act", lambda e: e.activation(out=dst[:, :], in_=dst[:, :], func=AF.Exp, scale=-0.5), [kdst], [kdst])

    def thread_A1(R, s):
        hb = s % 2
        p2 = s % 2
        p3 = s % 3
        if s + 1 < NSEG:
            load_h2(R, s + 1)

        def proj(c6, pb):
            for kc in range(KC):
                R.add("pe", lambda e, kc=kc: e.matmul(
                    PS[pb][:, :], lhsT=wdn_bf[:, kc, c6 * 128:(c6 + 1) * 128], rhs=h2[hb][:, kc, :],
                    start=(kc == 0), stop=(kc == KC - 1)), [kwdn[c6 // 2], kh2[hb]], [kPS[pb]])
        proj(0, 0)
        proj(1, 1)
        proj(2, 2)
        for i in range(3):
            R.add("act", lambda e, i=i: e.activation(out=pre[i][:, 3:515], in_=PS[i][:, :], func=AF.Copy), [kPS[i]], [kpre[i]])
        proj(3, 0)
        proj(4, 1)
        proj(5, 2)
        R.add("act", lambda e: e.activation(out=zg[p3][:, :], in_=PS[0][:, :], func=AF.Copy), [kPS[0]], [kzg[p3]])
        R.add("act", lambda e: e.activation(out=B_bc[:, :], in_=PS[1][:, :], func=AF.Sigmoid), [kPS[1]], [kB])
        R.add("act", lambda e: e.activation(out=g_bc[:, :], in_=PS[2][:, :], func=AF.Exp, bias=pvec[:, PV_DTB:PV_DTB + 1]),
              [kPS[2], kpv], [kg])
        R.add("act", lambda e: e.activation(out=g_bc[:, :], in_=g_bc[:, :], func=AF.Ln, bias=1.0), [kg], [kg])
        R.add("dve", lambda e: e.tensor_scalar(out=g_bc[:, :], in0=g_bc[:, :], scalar1=negA[:, 0:1], scalar2=None, op0=ALU.mult),
              [kg, knegA], [kg])
        R.add("dve", lambda e: e.tensor_tensor_scan(out=G_bc[:, :], data0=mscan[:, :], data1=g_bc[:, :], initial=0.0,
                                                    op0=ALU.mult, op1=ALU.add), [kg, kmask], [kG])
        R.add("act", lambda e: e.activation(out=expG[:, :], in_=G_bc[:, :], func=AF.Exp), [kG], [kexpG])
        R.add("act", lambda e: e.activation(out=eGl[p3][:, :], in_=G_bc[:, :].rearrange("p (c k) -> p c k", k=64)[:, :, 63],
                                            func=AF.Exp), [kG], [keGl[p3]])
        for i in range(3):
            cc = conv_cols[i]
            R.add("dve", lambda e, i=i, cc=cc: e.tensor_scalar(out=cacc[i][:, :], in0=pre[i][:, 0:512], scalar1=pvec[:, cc:cc + 1],
                                                               scalar2=None, op0=ALU.mult), [kpre[i], kpv], [kcacc[i]])
            for tap in range(1, 4):
                R.add("dve", lambda e, i=i, cc=cc, tap=tap: e.scalar_tensor_tensor(
                    out=cacc[i][:, :], in0=pre[i][:, tap:tap + 512], scalar=pvec[:, cc + tap:cc + tap + 1], in1=cacc[i][:, :],
                    op0=ALU.mult, op1=ALU.add), [kpre[i], kcacc[i], kpv], [kcacc[i]])
            R.add("dve", lambda e, i=i: e.tensor_copy(out=pre[i][:, 0:3], in_=pre[i][:, 512:515]), [kpre[i]], [kpre[i]])
        R.add("act", lambda e: e.activation(out=q_raw[:, :], in_=cacc[0][:, :], func=AF.Silu), [kcacc[0]], [kq_raw])
        R.add("act", lambda e: e.activation(out=k_raw[:, :], in_=cacc[1][:, :], func=AF.Silu), [kcacc[1]], [kk_raw])
        R.add("act", lambda e: e.activation(out=v_bf[p2][:, :], in_=cacc[2][:, :], func=AF.Silu), [kcacc[2]], [kv_bf[p2]])
        R.add("act", lambda e: e.activation(out=zg[p3][:, :], in_=zg[p3][:, :], func=AF.Silu), [kzg[p3]], [kzg[p3]])
        R.add("act", lambda e: e.activation(out=pk[0:64, :], in_=G_bc[0:64, :], func=AF.Copy), [kG, kg], [kpk])
        R.add("act", lambda e: e.activation(out=pk[64:128, :], in_=B_bc[64:128, :], func=AF.Copy), [kB, kg], [kpk])
        for c in range(8):
            pb = c // 4
            R.add("pe", lambda e, c=c, pb=pb: e.transpose(out=PS[pb][0:64, (c % 4) * 128:(c % 4 + 1) * 128],
                                                          in_=pk[:, c * 64:(c + 1) * 64], identity=identf[:, :]),
                  [kpk, kconst], [kPS[pb]])
        tS = tokS[p2]
        for hh in range(2):
            R.add("act", lambda e, hh=hh: e.activation(
                out=tS[:, 4 * hh:4 * hh + 4, :],
                in_=PS[hh][0:64, :].rearrange("p (c two k) -> p c two k", c=4, two=2)[:, :, :, 0],
                func=AF.Copy), [kPS[hh]], [ktokS[p2]])
        Gt = tS[:, :, 0]
        Bt = tS[:, :, 1]
        Glast64 = G_bc[0:64, :].rearrange("p (c k) -> p c k", k=64)[:, :, 63]
        R.add("act", lambda e: e.activation(out=expGt[:, :], in_=Gt, func=AF.Exp), [ktokS[p2]], [kexpGt])
        R.add("dve", lambda e: e.tensor_tensor(out=deckt[p2][:, :], in0=Glast64, in1=Gt, op=ALU.subtract), [kG, ktokS[p2]], [kdeckt[p2]])
        R.add("act", lambda e: e.activation(out=deckt[p2][:, :], in_=deckt[p2][:, :], func=AF.Exp), [kdeckt[p2]], [kdeckt[p2]])
        R.add("dve", lambda e: e.tensor_tensor(out=bexpt[p2][:, :], in0=Bt, in1=expGt[:, :], op=ALU.mult), [ktokS[p2], kexpGt], [kbexpt[p2]])
        G3 = G_bc[0:64, :].rearrange("p (c k) -> p c k", k=64)
        R.add("dve", lambda e: e.tensor_tensor(out=DT[:, :, :], in0=G3, in1=tS[:, :, 0:1].to_broadcast([64, 8, 64]),
                                               op=ALU.subtract), [kG, ktokS[p2]], [kDT])
        R.add("dve", lambda e: e.tensor_tensor(out=DT[:, :, :], in0=DT[:, :, :],
                                               in1=negm[:, :].unsqueeze(1).to_broadcast([64, 8, 64]), op=ALU.add),
              [kDT, kmask], [kDT])
        R.add("act", lambda e: e.activation(out=DT[:, :, :], in_=DT[:, :, :], func=AF.Exp), [kDT], [kDT])
        R.add("act", lambda e: e.activation(out=sq[0][:, :], in_=q_raw[:, :], func=AF.Square), [kq_raw], [ksq[0]])
        R.add("pe", lambda e: e.matmul(PS[0][:, :], lhsT=ones_bf[:, :], rhs=sq[0][:, :], start=True, stop=True),
              [ksq[0], kconst], [kPS[0]])
        R.add("act", lambda e: e.activation(out=sq[0][:, :], in_=k_raw[:, :], func=AF.Square), [kk_raw], [ksq[0]])
        R.add("pe", lambda e: e.matmul(PS[1][:, :], lhsT=ones_bf[:, :], rhs=sq[0][:, :], start=True, stop=True),
              [ksq[0], kconst], [kPS[1]])
        rsqrt_act(R, cacc[0], kcacc[0], 0, 1.0)
        rsqrt_act(R, cacc[1], kcacc[1], 1, 1.0)
        R.add("dve", lambda e: e.scalar_tensor_tensor(out=q_raw[:, :], in0=q_raw[:, :], scalar=float(128 ** -0.5), in1=cacc[0][:, :],
                                                      op0=ALU.mult, op1=ALU.mult), [kq_raw, kcacc[0]], [kq_raw])
        R.add("dve", lambda e: e.tensor_tensor(out=k_bf[p2][:, :], in0=k_raw[:, :], in1=cacc[1][:, :], op=ALU.mult),
              [kk_raw, kcacc[1]], [kk_bf[p2]])
        R.add("act", lambda e: e.activation(out=q_bf[:, :], in_=q_raw[:, :], func=AF.Copy), [kq_raw], [kq_bf])
        R.add("dve", lambda e: e.tensor_tensor(out=qd[p3][:, :], in0=expG[:, :], in1=q_raw[:, :], op=ALU.mult),
              [kq_raw, kexpG], [kqd[p3]])
        for c in range(8):
            cs = slice(c * 64, (c + 1) * 64)
            R.add("pe", lambda e, cs=cs: e.matmul(PS[2][0:64, cs], lhsT=k_bf[p2][:, cs], rhs=k_bf[p2][:, cs], start=True, stop=True),
                  [kk_bf[p2]], [kPS[2]])
        for c in range(8):
            cs = slice(c * 64, (c + 1) * 64)
            R.add("pe", lambda e, cs=cs: e.matmul(PS[0][0:64, cs], lhsT=k_bf[p2][:, cs], rhs=q_bf[:, cs], start=True, stop=True),
                  [kk_bf[p2], kq_bf], [kPS[0]])
        R.add("dve", lambda e: e.tensor_tensor(out=attnT[p3][:, :, :], in0=PS[0][0:64, :].rearrange("p (c k) -> p c k", k=64),
                                               in1=DT[:, :, :], op=ALU.mult), [kPS[0], kDT], [kattn[p3]])
        R.add("dve", lambda e: e.tensor_tensor(out=DT[:, :, :], in0=DT[:, :, :],
                                               in1=identf[0:64, 0:64].unsqueeze(1).to_broadcast([64, 8, 64]), op=ALU.subtract),
              [kDT, kconst], [kDT])
        R.add("dve", lambda e: e.tensor_tensor(out=DT[:, :, :], in0=DT[:, :, :],
                                               in1=B_bc[0:64, :].rearrange("p (c k) -> p c k", k=64), op=ALU.mult),
              [kDT, kB], [kDT])
        R.add("dve", lambda e: e.scalar_tensor_tensor(out=QZ0[p2][:, :, 0:64], in0=PS[2][0:64, :].rearrange("p (c k) -> p c k", k=64),
                                                      scalar=-1.0, in1=DT[:, :, :], op0=ALU.mult, op1=ALU.mult),
              [kPS[2], kDT], [kQZ0[p2][0], kQZ0[p2][1]])
        psT = PS[1][0:64, 0:256].bitcast(BF16)
        for c in range(8):
            R.add("pe", lambda e, c=c: e.transpose(out=psT[:, c * 64:(c + 1) * 64], in_=QZ0[p2][:, c, 0:64],
                                                   identity=identb[0:64, 0:64]), [kQZ0[p2][c // 4], kconst], [kPS[1]])
        R.add("act", lambda e: e.activation(out=P0[p2][:, :, :], in_=psT.rearrange("p (c k) -> p c k", k=64), func=AF.Copy),
              [kPS[1]], [kP0h[p2][0], kP0h[p2][1]])

    def thread_A2(R, s):
        p2 = s % 2
        psK = PS[3][0:64, :].bitcast(BF16)
        psV = PS[4][0:64, :].bitcast(BF16)
        for c in range(8):
            cs = slice(c * 64, (c + 1) * 64)
            R.add("pe", lambda e, c=c, cs=cs: e.transpose(out=psK[:, c * 128:(c + 1) * 128], in_=k_bf[p2][:, cs], identity=identb[:, :]),
                  [kk_bf[p2], kconst], [kPS[3]])
        for c in range(8):
            cs = slice(c * 64, (c + 1) * 64)
            R.add("pe", lambda e, c=c, cs=cs: e.transpose(out=psV[:, c * 128:(c + 1) * 128], in_=v_bf[p2][:, cs], identity=identb[:, :]),
                  [kv_bf[p2], kconst], [kPS[4]])
        psK3 = psK.rearrange("p (c k) -> p c k", k=128)
        psV3 = psV.rearrange("p (c k) -> p c k", k=128)
        R.add("dve", lambda e: e.tensor_tensor(out=kbg[:, :, :], in0=psK3, in1=bexpt[p2][:, :].unsqueeze(2).to_broadcast([64, 8, 128]),
                                               op=ALU.mult), [kPS[3], kbexpt[p2]], [kkbg])
        R.add("dve", lambda e: e.tensor_tensor(out=kd[p2][:, :, :], in0=psK3, in1=deckt[p2][:, :].unsqueeze(2).to_broadcast([64, 8, 128]),
                                               op=ALU.mult), [kPS[3], kdeckt[p2]], [kkd[p2]])
        R.add("dve", lambda e: e.tensor_tensor(out=vb[:, :, :], in0=psV3, in1=tokS[p2][:, :, 1:2].to_broadcast([64, 8, 128]),
                                               op=ALU.mult), [kPS[4], ktokS[p2]], [kvb])
        for lv in range(6):
            last = (lv == 5)
            if lv == 0:
                srcQZ, ksQZ, srcP, ksP = QZ0[p2], kQZ0[p2], P0[p2], kP0h[p2]
            else:
                srcQZ, ksQZ, srcP, ksP = QZ[(lv - 1) % 2], kQZ[(lv - 1) % 2], Pm[(lv - 1) % 2], kPmh[(lv - 1) % 2]
            dstQZ, kdQZ, dstP, kdP = QZ[lv % 2], kQZ[lv % 2], Pm[lv % 2], kPmh[lv % 2]
            for hh in range(2):
                for c in range(4 * hh, 4 * hh + 4):
                    R.add("pe", lambda e, c=c, hh=hh, srcP=srcP, srcQZ=srcQZ: e.matmul(
                        PS[3 + hh][0:64, (c % 4) * 128:(c % 4 + 1) * 128], lhsT=srcP[:, c, :], rhs=srcQZ[:, c, :],
                        start=True, stop=True), [ksP[hh], ksQZ[hh]], [kPS[3 + hh]])
                if not last:
                    for c in range(4 * hh, 4 * hh + 4):
                        R.add("pe", lambda e, c=c, srcP=srcP, srcQZ=srcQZ: e.matmul(
                            PS[5][0:64, c * 64:(c + 1) * 64], lhsT=srcQZ[:, c, 0:64], rhs=srcP[:, c, :],
                            start=True, stop=True), [ksP[hh], ksQZ[hh]], [kPS5h[hh]])
            for hh in range(2):
                psv = PS[3 + hh][0:64, :].rearrange("p (c k) -> p c k", k=128)
                hs = slice(4 * hh, 4 * hh + 4)
                if not last:
                    R.add("act", lambda e, psv=psv, hs=hs, dstQZ=dstQZ: e.activation(
                        out=dstQZ[:, hs, 0:64], in_=psv[:, :, 0:64], func=AF.Copy), [kPS[3 + hh]], [kdQZ[hh]])
                R.add("dve", lambda e, psv=psv, hs=hs, dstQZ=dstQZ, srcQZ=srcQZ: e.tensor_tensor(
                    out=dstQZ[:, hs, 64:128], in0=psv[:, :, 64:128], in1=srcQZ[:, hs, 64:128],
                    op=ALU.add), [kPS[3 + hh], ksQZ[hh]], [kdQZ[hh]])
                if not last:
                    R.add("act", lambda e, hh=hh, hs=hs, dstP=dstP: e.activation(
                        out=dstP[:, hs, :], in_=PS[5][0:64, hh * 256:(hh + 1) * 256].rearrange("p (c k) -> p c k", k=64),
                        func=AF.Copy), [kPS5h[hh]], [kdP[hh]])
        Zf = QZ[1]
        kZf = kQZ[1]
        for c in range(8):
            pb = 3 + c // 4
            R.add("pe", lambda e, c=c, pb=pb: e.matmul(PS[pb][0:64, (c % 4) * 128:(c % 4 + 1) * 128], lhsT=Zf[:, c, 64:128],
                                                       rhs=vb[:, c, :], start=True, stop=True), [kZf[c // 4], kvb], [kPS[pb]])
        for c in range(8):
            R.add("pe", lambda e, c=c: e.matmul(PS[5][:, c * 64:(c + 1) * 64], lhsT=kbg[:, c, :], rhs=Zf[:, c, 64:128],
                                                start=True, stop=True), [kZf[c // 4], kkbg], [kPS5h[0], kPS5h[1]])
        for hh in range(2):
            R.add("act", lambda e, hh=hh: e.activation(out=uc[p2][:, 4 * hh:4 * hh + 4, :],
                                                       in_=PS[3 + hh][0:64, :].rearrange("p (c k) -> p c k", k=128), func=AF.Copy),
                  [kPS[3 + hh]], [kuc[p2]])
        R.add("act", lambda e: e.activation(out=wcT[p2][:, :], in_=PS[5][:, :], func=AF.Copy), [kPS5h[0], kPS5h[1]], [kwcT[p2]])

    def thread_B(R, s, cur):
        p2 = s % 2
        p3 = s % 3
        for c in range(8):
            cs = slice(c * 64, (c + 1) * 64)
            nxt = 1 - cur
            R.add("pe", lambda e, cs=cs, cur=cur: e.matmul(PS[6][0:64, 0:128], lhsT=wcT[p2][:, cs], rhs=S_bf[cur][:, :],
                                                           start=True, stop=True), [kwcT[p2], kS_bf[cur]], [kPS[6]])
            R.add("dve", lambda e, c=c: e.tensor_tensor(out=vnew[:, c, :], in0=uc[p2][:, c, :], in1=PS[6][0:64, 0:128],
                                                        op=ALU.subtract), [kuc[p2], kPS[6]], [kvnew[c]])
            R.add("pe", lambda e, cs=cs, cur=cur: e.matmul(PS[7][:, cs], lhsT=S_bf[cur][:, :], rhs=qd[p3][:, cs], start=True, stop=False),
                  [kS_bf[cur], kqd[p3]], [kPS[7]])
            R.add("pe", lambda e, c=c, cs=cs: e.matmul(PS[7][:, cs], lhsT=vnew[:, c, :], rhs=attnT[p3][:, c, :], start=False, stop=True),
                  [kvnew[c], kattn[p3]], [kPS[7]])
            R.add("pe", lambda e, c=c: e.matmul(PS[6][:, 128:256], lhsT=kd[p2][:, c, :], rhs=vnew[:, c, :], start=True, stop=True),
                  [kkd[p2], kvnew[c]], [kPS[6]])
            R.add("dve", lambda e, c=c, cur=cur, nxt=nxt: e.scalar_tensor_tensor(
                out=S[nxt][:, :], in0=S[cur][:, :], scalar=eGl[p3][:, c:c + 1], in1=PS[6][:, 128:256], op0=ALU.mult, op1=ALU.add),
                [kS[cur], keGl[p3], kPS[6]], [kS[nxt]])
            R.add("act", lambda e, nxt=nxt: e.activation(out=S_bf[nxt][:, :], in_=S[nxt][:, :], func=AF.Copy), [kS[nxt]], [kS_bf[nxt]])
            cur = nxt
        R.add("act", lambda e: e.activation(out=osb[:, :], in_=PS[7][:, :], func=AF.Copy), [kPS[7]], [kosb])
        R.add("act", lambda e: e.activation(out=sq[1][:, :], in_=osb[:, :], func=AF.Square), [kosb], [ksq[1]])
        R.add("pe", lambda e: e.matmul(PS[6][:, :], lhsT=ones_bf[:, :], rhs=sq[1][:, :], start=True, stop=True),
              [ksq[1], kconst], [kPS[6]])
        rsqrt_act(R, rs, krs, 6, 1.0 / 128)
        R.add("dve", lambda e: e.scalar_tensor_tensor(out=osb[:, :], in0=osb[:, :], scalar=pvec[:, PV_DN:PV_DN + 1], in1=rs[:, :],
                                                      op0=ALU.mult, op1=ALU.mult), [kosb, krs, kpv], [kosb])
        yb_ = s % 2
        R.add("dve", lambda e: e.tensor_tensor(out=ybf[yb_][:, :], in0=osb[:, :], in1=zg[p3][:, :], op=ALU.mult),
              [kosb, kzg[p3]], [kybf[yb_]])
        R.add("sp", lambda e: e.dma_start(out=yb_in[s], in_=ybf[yb_][:, :]),
              [kybf[yb_]], [kyb_in[s]], dma=True, ch="yb%d" % yb_)
        R.add("pool", lambda e: e.collective_compute("AllGather", ALU.bypass, replica_groups=groups,
                                                     ins=[yb_in[s].opt()], outs=[yb_out[s].opt()]),
              [kyb_in[s]], [kyb_out[s]], dma=True, ch="ccy%d" % (s % 4), inc=1)
        return cur

    R0 = Rec()
    load_h2(R0, 0)
    thread_A1(R0, 0)
    replay([R0])
    Ra, Rb = Rec(), Rec()
    thread_A2(Ra, 0)
    if NSEG > 1:
        thread_A1(Rb, 1)
    replay([Ra, Rb])
    cur = 0
    for s in range(NSEG):
        RB, RA2, RA1 = Rec(), Rec(), Rec()
        cur = thread_B(RB, s, cur)
        if s + 1 < NSEG:
            thread_A2(RA2, s + 1)
        if s + 2 < NSEG:
            thread_A1(RA1, s + 2)
        replay([RB, RA2, RA1])
    if stage == 3:
        P.barrier()
        for t in range(4):
            P.add("sp", lambda e, t=t: e.dma_start(out=yT.rearrange("(kc p) t -> p kc t", p=128)[:, :, T(t)], in_=X[:, :, T(t)]),
                  [kX[kc][t] for kc in range(KC)], [Tok()], dma=True, ch="out")
        return nc, P.emit()
    ar.reset(mC)
    P.barrier()

    mD = ar.mark()
    cand = [ar.alloc([128, 4, NT], BF16, "cand") for _ in range(2)]
    ysel = ar.alloc([128, 4, NT], BF16, "ysel")
    wo_bf = ar.alloc([128, KC, D], BF16, "wo")
    stgD_ = [ar.alloc([128, KC, 256], F32, "stgDD") for _ in range(2)]
    kcand = [Tok(), Tok()]
    kysel = Tok()
    kwo = [Tok() for _ in range(4)]
    kstgD_ = [Tok(), Tok()]
    wov = w_o.rearrange("(kc p) f -> p kc f", p=128)
    for q in range(4):
        b = q % 2
        P.add("sp", lambda e, q=q, b=b: e.dma_start(out=stgD_[b][:, :, :], in_=wov[:, :, q * 256:(q + 1) * 256]),
              [], [kstgD_[b]], dma=True, ch="wO%d" % b)
        P.add("act", lambda e, q=q, b=b: e.activation(out=wo_bf[:, :, q * 256:(q + 1) * 256], in_=stgD_[b][:, :, :], func=AF.Copy),
              [kstgD_[b]], [kwo[q]])
    kcandq = [[Tok() for _ in range(4)] for _ in range(2)]
    kyselt = [Tok() for _ in range(4)]
    for j in range(4):
        b = j % 2
        for q in range(4):
            P.add("sp", lambda e, j=j, b=b, q=q: e.dma_start(out=cand[b][:, :, q * 512:(q + 1) * 512],
                                                             in_=yb_out[4 * j + q].rearrange("h e t -> e h t")),
                  [kyb_out[4 * j + q]], [kcandq[b][q]], dma=True, ch="cand%d%d" % (b, q % 2))
        for q in range(4):
            if j == 0:
                P.add("dve", lambda e, b=b, q=q: e.tensor_scalar(out=ysel[:, :, T(q)], in0=cand[b][:, :, T(q)],
                                                                 scalar1=pvec[:, PV_SEL:PV_SEL + 1], scalar2=None, op0=ALU.mult),
                      [kcandq[b][q], kpv], [kyselt[q]])
            else:
                P.add("dve", lambda e, j=j, b=b, q=q: e.scalar_tensor_tensor(
                    out=ysel[:, :, T(q)], in0=cand[b][:, :, T(q)], scalar=pvec[:, PV_SEL + j:PV_SEL + j + 1], in1=ysel[:, :, T(q)],
                    op0=ALU.mult, op1=ALU.add), [kcandq[b][q], kyselt[q], kpv], [kyselt[q]])
    cnt = 0
    for half in range(2):
        for t in range(4):
            for dc in range(KC):
                pb = cnt % 8
                cnt += 1
                for k4 in range(4):
                    kc = 4 * half + k4
                    if half == 0:
                        rhs, rt = ya[:, k4, T(t)], kYa[4 * t:4 * t + 4]
                    else:
                        rhs, rt = ysel[:, k4, T(t)], [kyselt[t]]
                    P.add("pe", lambda e, dc=dc, kc=kc, k4=k4, pb=pb, rhs=rhs: e.matmul(
                        PS[pb][:, :], lhsT=wo_bf[:, kc, dc * 128:(dc + 1) * 128], rhs=rhs, start=(k4 == 0), stop=(k4 == 3)),
                        [kwo[dc // 2]] + list(rt), [kPS[pb]])
                P.add("dve", lambda e, dc=dc, t=t, pb=pb: e.tensor_tensor(out=X[:, dc, T(t)], in0=PS[pb][:, :], in1=X[:, dc, T(t)],
                                                                          op=ALU.add), [kPS[pb], kX[dc][t]], [kX[dc][t]])
    ar.reset(mD)
    P.barrier()
    ot = [ar.alloc([128, 512], F32, "ot") for _ in range(3)]
    kot = [Tok(), Tok(), Tok()]
    otc = [0]

    def final_tile(t):
        stats_to_rs(t, [(X[:, kc, T(t)], kX[kc][t]) for kc in range(KC)], None, 1.0 / D)
        for kc in range(KC):
            b = otc[0] % 3
            otc[0] += 1
            P.add("dve", lambda e, kc=kc, t=t, b=b: e.scalar_tensor_tensor(
                out=ot[b][:, :], in0=X[:, kc, T(t)], scalar=pvec[:, PV_NF + kc:PV_NF + kc + 1], in1=rs[:, :],
                op0=ALU.mult, op1=ALU.mult), [kX[kc][t], krs, kpv], [kot[b]])
            P.add("sp", lambda e, kc=kc, t=t, b=b: e.dma_start(out=yT[kc * 128:(kc + 1) * 128, T(t)], in_=ot[b][:, :]),
                  [kot[b]], [Tok()], dma=True, ch="out%d" % b)

    ffn(w2g, w2u, w2d, "f2", PV_N2, tail=final_tile)
    return nc, P.emit()


def make_in_maps(x, ffn1_norm, ffn1_w_gate, ffn1_w_up, ffn1_w_down, mix_norm, w_in,
                 gm_ln_g, gm_ln_b, gm_w_s, gm_b_s, dn_conv_w, dn_a_log, dn_dt_bias, dn_norm,
                 w_out, ffn2_norm, ffn2_w_gate, ffn2_w_up, ffn2_w_down, final_norm):
    f = lambda a: np.ascontiguousarray(np.asarray(a, dtype=np.float32))
    x = f(x)
    w_in0 = f(w_in)[0]
    common = {
        "w1g": f(ffn1_w_gate)[0], "w1u": f(ffn1_w_up)[0], "w1d": f(ffn1_w_down)[0],
        "w2g": f(ffn2_w_gate)[0], "w2u": f(ffn2_w_up)[0], "w2d": f(ffn2_w_down)[0],
        "w_uv": np.ascontiguousarray(w_in0[:, 0:1024]),
        "w_o": f(w_out)[0],
        "lng": np.ascontiguousarray(np.broadcast_to(f(gm_ln_g)[0][None, :], (128, 512))),
        "lnb": np.ascontiguousarray(np.broadcast_to(f(gm_ln_b)[0][None, :], (128, 512))),
        "wsT": np.ascontiguousarray(np.transpose(f(gm_w_s)[0], (2, 0, 1))),
    }
    bs = f(gm_b_s)[0]
    bsT = np.empty((128, 4, 128), np.float32)
    for fc in range(4):
        bsT[0:64, fc, :] = bs[2 * fc][None, :]
        bsT[64:128, fc, :] = bs[2 * fc + 1][None, :]
    common["bsT"] = bsT
    cw = f(dn_conv_w)[0]
    in_maps = []
    for c in range(8):
        b, j = c // 4, c % 4
        hd = j
        m = dict(common)
        m["xT"] = np.ascontiguousarray(x[b, j * NT:(j + 1) * NT, :].T)
        cols = []
        for base in (1024, 1536, 2048, 2560):
            cols.append(w_in0[:, base + hd * 128: base + (hd + 1) * 128])
        cols.append(np.repeat(w_in0[:, 3072 + hd:3072 + hd + 1], 128, axis=1))
        cols.append(np.repeat(w_in0[:, 3076 + hd:3076 + hd + 1], 128, axis=1))
        m["w_dn"] = np.ascontiguousarray(np.concatenate(cols, axis=1))
        pv = np.zeros((128, NPV), np.float32)
        for k0, g in ((PV_N1, ffn1_norm), (PV_NM, mix_norm), (PV_N2, ffn2_norm)):
            pv[:, k0:k0 + 8] = f(g)[0].reshape(8, 128).T
        pv[:, PV_NF:PV_NF + 8] = f(final_norm).reshape(8, 128).T
        pv[:, PV_DN] = f(dn_norm)[0]
        for k0, base in ((PV_CQ, 0), (PV_CK, 512), (PV_CV, 1024)):
            pv[:, k0:k0 + 4] = cw[:, base + hd * 128: base + (hd + 1) * 128].T
        pv[:, PV_ALOG] = f(dn_a_log)[0][hd]
        pv[:, PV_DTB] = f(dn_dt_bias)[0][hd]
        pv[:, PV_SEL + j] = 1.0
        m["pvec"] = pv
        in_maps.append(m)
    return in_maps


_CACHE = {}


def kernel(**inputs):
    if "nc" not in _CACHE:
        _CACHE["nc"] = build()[0]
    nc = _CACHE["nc"]
    in_maps = make_in_maps(**inputs)
    res = run_bass_kernel_spmd(nc, in_maps, core_ids=list(range(8)))
    out = np.empty((2, TS, D), np.float32)
    for c in range(8):
        b, j = c // 4, c % 4
        out[b, j * NT:(j + 1) * NT, :] = np.asarray(res.results[c]["yT"]).T
    return out
```
